# Optimizing a Trainium2 kernel written in Bass

```python
import jax, jax.numpy as jnp
from jax import lax
import numpy as np

D_MODEL = 1024
BATCH = 2
SEQ = 8192
DEPTH = 1

N_MEM = 256
HEAD_DIM = 128
HEADS_PER_GROUP = 4
DIL_GROUPS = ((128, 1), (512, 4), (2048, 16))
N_GROUPS = len(DIL_GROUPS)
ATTN_HEADS = N_GROUPS * HEADS_PER_GROUP
ATTN_WIDTH = ATTN_HEADS * HEAD_DIM
ATTN_OUT = HEADS_PER_GROUP * HEAD_DIM
ROT_DIM = HEAD_DIM // 4
ROPE_THETA = 500000.0
CONV_CH = 3 * D_MODEL // 4
CONV_K = 31
N_BRANCH = 2
IN_SPLITS = (ATTN_WIDTH, ATTN_WIDTH, ATTN_WIDTH, CONV_CH, CONV_CH, D_MODEL, D_MODEL)
IN_WIDTH = sum(IN_SPLITS)
CROSS_HEADS = 4
CROSS_HEAD_DIM = D_MODEL // CROSS_HEADS
D_FF = 4 * D_MODEL
EPS = 1e-6

kernel_name = 'hybrid_dilated_attn_conformer_conv_gated'


def rmsnorm(x, g):
    xf = x.astype(jnp.float32)
    y = xf * lax.rsqrt(jnp.mean(xf * xf, axis=-1, keepdims=True) + EPS) * g.astype(jnp.float32)
    return y.astype(x.dtype)


def layernorm(x, g, b):
    xf = x.astype(jnp.float32)
    mu = jnp.mean(xf, axis=-1, keepdims=True)
    var = jnp.mean(jnp.square(xf - mu), axis=-1, keepdims=True)
    y = (xf - mu) * lax.rsqrt(var + EPS) * g.astype(jnp.float32) + b.astype(jnp.float32)
    return y.astype(x.dtype)


def rope_partial(t, pos):
    half = ROT_DIM // 2
    inv_freq = ROPE_THETA ** (-jnp.arange(0, ROT_DIM, 2, dtype=jnp.float32) / ROT_DIM)
    ang = pos[:, None] * inv_freq[None, :]
    cos = jnp.cos(ang)[None, :, None, :]
    sin = jnp.sin(ang)[None, :, None, :]
    tf = t.astype(jnp.float32)
    x1, x2, rest = tf[..., :half], tf[..., half:ROT_DIM], tf[..., ROT_DIM:]
    out = jnp.concatenate([x1 * cos - x2 * sin, x2 * cos + x1 * sin, rest], axis=-1)
    return out.astype(t.dtype)


def dilated_window_attention(q, k, v, window, dilation):
    B, S, H, Dh = q.shape
    L = window // dilation
    span = dilation * L
    Sp = -(-S // span) * span
    M = Sp // dilation
    nb = M // L
    pad = ((0, 0), (0, Sp - S), (0, 0), (0, 0))

    def to_blocks(t):
        t = jnp.pad(t, pad).reshape(B, M, dilation, H, Dh)
        t = t.transpose(0, 3, 2, 1, 4)
        return t.reshape(B, H, dilation, nb, L, Dh)

    def with_prev(t):
        prev = jnp.pad(t, ((0, 0), (0, 0), (0, 0), (1, 0), (0, 0), (0, 0)))[:, :, :, :-1]
        return jnp.concatenate([prev, t], axis=-2)

    qb = to_blocks(q)
    kw = with_prev(to_blocks(k))
    vw = with_prev(to_blocks(v))
    s = jnp.einsum('bhrnqe,bhrnke->bhrnqk', qb, kw).astype(jnp.float32) * (Dh ** -0.5)
    qi = jnp.arange(L)[:, None]
    kj = jnp.arange(2 * L)[None, :]
    band = (kj >= qi) & (kj <= qi + L)
    first = (jnp.arange(nb)[:, None, None] == 0) & (kj[None] < L)
    mask = band[None] & jnp.logical_not(first)
    s = jnp.where(mask, s, -jnp.inf)
    lse = jax.nn.logsumexp(s, axis=-1)
    p = jnp.exp(s - lse[..., None])
    o = jnp.einsum('bhrnqk,bhrnke->bhrnqe', p.astype(v.dtype), vw)
    o = o.reshape(B, H, dilation, M, Dh).transpose(0, 3, 2, 1, 4).reshape(B, Sp, H, Dh)[:, :S]
    lse = lse.reshape(B, H, dilation, M).transpose(0, 3, 2, 1).reshape(B, Sp, H)[:, :S]
    return o, lse


def setup_inputs(seed: int = 0) -> dict:
    key = jax.random.key(seed)
    ks = jax.random.split(key, 24)
    f32 = jnp.float32
    nrm = lambda k, shape, fan: jax.random.normal(k, shape, f32) * (fan ** -0.5)
    gain = lambda k, shape: 1.0 + 0.01 * jax.random.normal(k, shape, f32)
    small = lambda k, shape: 0.01 * jax.random.normal(k, shape, f32)
    return {
        'x': jax.random.normal(ks[0], (BATCH, SEQ, D_MODEL), f32),
        'mem': jax.random.normal(ks[1], (BATCH, N_MEM, D_MODEL), f32),
        'g_mix': gain(ks[2], (DEPTH, D_MODEL)),
        'w_in': nrm(ks[3], (DEPTH, D_MODEL, IN_WIDTH), D_MODEL),
        'b_gate': small(ks[4], (DEPTH, N_BRANCH * D_MODEL)),
        'conv_w': nrm(ks[5], (DEPTH, CONV_K, CONV_CH), CONV_K),
        'conv_b': small(ks[6], (DEPTH, CONV_CH)),
        'conv_ln_g': gain(ks[7], (DEPTH, CONV_CH)),
        'conv_ln_b': small(ks[8], (DEPTH, CONV_CH)),
        'w_attn_proj': nrm(ks[9], (DEPTH, ATTN_OUT, D_MODEL), ATTN_OUT),
        'w_conv_proj': nrm(ks[10], (DEPTH, CONV_CH, D_MODEL), CONV_CH),
        'w_out': nrm(ks[11], (DEPTH, D_MODEL, D_MODEL), D_MODEL),
        'g_cross': gain(ks[12], (DEPTH, D_MODEL)),
        'g_mem': gain(ks[13], (DEPTH, D_MODEL)),
        'w_cq': nrm(ks[14], (DEPTH, D_MODEL, D_MODEL), D_MODEL),
        'w_ckv': nrm(ks[15], (DEPTH, D_MODEL, 2 * D_MODEL), D_MODEL),
        'w_co': nrm(ks[16], (DEPTH, D_MODEL, D_MODEL), D_MODEL),
        'g_mlp': gain(ks[17], (DEPTH, D_MODEL)),
        'w_up': nrm(ks[18], (DEPTH, D_MODEL, D_FF), D_MODEL),
        'w_down': nrm(ks[19], (DEPTH, D_FF, D_MODEL), D_FF),
        'g_final': gain(ks[20], (D_MODEL,)),
    }


def reference(x, mem, g_mix, w_in, b_gate, conv_w, conv_b, conv_ln_g, conv_ln_b, w_attn_proj,
              w_conv_proj, w_out, g_cross, g_mem, w_cq, w_ckv, w_co, g_mlp, w_up, w_down, g_final):
    B, S, _ = x.shape
    pos = jnp.arange(S, dtype=jnp.float32)
    split_pts = [int(v) for v in np.cumsum(IN_SPLITS)[:-1]]
    for l in range(DEPTH):
        u = rmsnorm(x, g_mix[l])
        z = u @ w_in[l]
        q, k, v, glu_a, glu_b, gate_a, gate_b = jnp.split(z, split_pts, axis=-1)
        q = rope_partial(q.reshape(B, S, ATTN_HEADS, HEAD_DIM), pos)
        k = rope_partial(k.reshape(B, S, ATTN_HEADS, HEAD_DIM), pos)
        v = v.reshape(B, S, ATTN_HEADS, HEAD_DIM)
        q = q.reshape(B, S, N_GROUPS, HEADS_PER_GROUP, HEAD_DIM)
        k = k.reshape(B, S, N_GROUPS, HEADS_PER_GROUP, HEAD_DIM)
        v = v.reshape(B, S, N_GROUPS, HEADS_PER_GROUP, HEAD_DIM)
        outs, lses = [], []
        for g, (win, dil) in enumerate(DIL_GROUPS):
            o_g, lse_g = dilated_window_attention(q[:, :, g], k[:, :, g], v[:, :, g], win, dil)
            outs.append(o_g)
            lses.append(lse_g)
        wts = jax.nn.softmax(jnp.stack(lses, axis=0), axis=0)
        attn = jnp.sum(wts[..., None] * jnp.stack(outs, axis=0).astype(jnp.float32), axis=0)
        y_attn = attn.astype(x.dtype).reshape(B, S, ATTN_OUT) @ w_attn_proj[l]

        c = glu_a * jax.nn.sigmoid(glu_b)
        c = lax.conv_general_dilated(c, conv_w[l].astype(c.dtype)[:, None, :], window_strides=(1,),
                                     padding=[(CONV_K - 1, 0)], dimension_numbers=('NWC', 'WIO', 'NWC'),
                                     feature_group_count=CONV_CH) + conv_b[l]
        c = jax.nn.silu(layernorm(c, conv_ln_g[l], conv_ln_b[l]))
        y_conv = c @ w_conv_proj[l]

        bg_a, bg_b = jnp.split(b_gate[l], 2)
        merged = jax.nn.sigmoid(gate_a + bg_a) * y_attn + jax.nn.sigmoid(gate_b + bg_b) * y_conv
        x = x + merged @ w_out[l]

        uq = rmsnorm(x, g_cross[l])
        m = rmsnorm(mem, g_mem[l])
        cq = (uq @ w_cq[l]).reshape(B, S, CROSS_HEADS, CROSS_HEAD_DIM)
        ck, cv = jnp.split(m @ w_ckv[l], 2, axis=-1)
        ck = ck.reshape(B, N_MEM, CROSS_HEADS, CROSS_HEAD_DIM)
        cv = cv.reshape(B, N_MEM, CROSS_HEADS, CROSS_HEAD_DIM)
        sc = jnp.einsum('bshe,bmhe->bhsm', cq, ck).astype(jnp.float32) * (CROSS_HEAD_DIM ** -0.5)
        pc = jax.nn.softmax(sc, axis=-1).astype(cv.dtype)
        co = jnp.einsum('bhsm,bmhe->bshe', pc, cv).reshape(B, S, D_MODEL)
        x = x + co @ w_co[l]

        h = jnp.square(jax.nn.relu(rmsnorm(x, g_mlp[l]) @ w_up[l]))
        x = x + h @ w_down[l]
    return rmsnorm(x, g_final)
```

```python
import numpy as np
import ml_dtypes
import concourse.bass as bass
import concourse.mybir as mybir
from concourse.bass_utils import run_bass_kernel_spmd

F32 = mybir.dt.float32
BF16 = mybir.dt.bfloat16
AF = mybir.ActivationFunctionType
ALU = mybir.AluOpType

NCORES = 8
T = 2048
HALO = 2048
D = 1024
EPS = 1e-6
WBLK = 4096
PERM = np.array(list(range(0, 16)) + list(range(32, 48)) + list(range(16, 32)) + list(range(48, 128)))
DIL = (1, 4, 16)

C_ID = 0
C_MASK = 128
C_MASK0 = 384
C_EPS = 640
C_GMIX = 641
C_GCROSS = 649
C_GMEM = 657
C_GMLP = 665
C_GFIN = 673
C_BGA = 681
C_BGB = 689
C_CONVB = 697
C_LNG = 703
C_LNB = 709
C_CONVW = 715
CW = 715 + 186

WB_A = 0
WB_C = 12
WB_E = 15
WB_M = 23
WB_OUT = 27
WB_CQ = 29
WB_CO = 31
WB_UP = 33
WB_DN = 41
NWB = 49


class Trk:
    def __init__(self, nc):
        self.nc = nc
        self.eng = {"pe": nc.tensor, "act": nc.scalar, "dve": nc.vector, "pool": nc.gpsimd, "sp": nc.sync}
        self.sems = {}
        self.cnt = {}
        for e in ("pe", "act", "dve", "pool"):
            self.cnt[e] = 0
        self.seen = {e: {} for e in self.eng}
        self.last_w = {}
        self.readers = {}
        self.dma_cnt = {}
        self.label = ""
        self.log = {e: [] for e in self.eng}

    def _sem(self, name):
        if self.sems.get(name) is None:
            cm = self.nc.semaphore(f"s_{name}")
            self.sems[name] = cm.__enter__()
        return self.sems[name]

    def _deps(self, eng, r, w):
        deps = {}

        def add(tok):
            if tok is None:
                return
            s, v = tok
            if deps.get(s, 0) < v:
                deps[s] = v

        for k in r:
            add(self.last_w.get(k))
        for k in w:
            add(self.last_w.get(k))
            for tok in self.readers.get(k, ()):
                add(tok)
        need = []
        for s, v in deps.items():
            if s == "pe" and eng == "pe":
                continue
            if self.seen[eng].get(s, 0) < v:
                need.append((s, v))
        return need

    def _emit(self, eng, fn, need):
        e = self.eng[eng]
        for s, v in need[:-1]:
            e.wait_ge(self._sem(s), v)
        ins = fn()
        if need:
            s, v = need[-1]
            ins._wait_ge(self._sem(s), v)
        for s, v in need:
            self.seen[eng][s] = v
        return ins

    def _record(self, tok, r, w):
        for k in w:
            self.last_w[k] = tok
            self.readers[k] = []
        for k in r:
            self.readers.setdefault(k, []).append(tok)

    def op(self, eng, fn, r=(), w=()):
        need = self._deps(eng, r, w)
        ins = self._emit(eng, fn, need)
        self.cnt[eng] += 1
        self.log[eng].append(self.label)
        ins.then_inc(self._sem(eng), 1)
        self._record((eng, self.cnt[eng]), r, w)

    def dma(self, q, fn, stream, r=(), w=()):
        need = self._deps(q, r, w)
        ins = self._emit(q, fn, need)
        s = "dma_" + stream
        self.dma_cnt[s] = self.dma_cnt.get(s, 0) + 16
        ins.then_inc(self._sem(s), 16)
        self._record((s, self.dma_cnt[s]), r, w)

    def barrier(self, waiters=("pe", "act", "dve", "sp"), skip_prefix="dma_w"):
        toks = [(e, self.cnt[e]) for e in ("pe", "act", "dve", "pool") if self.cnt[e] > 0]
        toks += [(s, v) for s, v in self.dma_cnt.items() if not s.startswith(skip_prefix)]
        for wtr in waiters:
            for s, v in toks:
                if self.seen[wtr].get(s, 0) < v:
                    self.eng[wtr].wait_ge(self._sem(s), v)
                    self.seen[wtr][s] = v


def _selfsync(t, engines=("act", "dve", "pool")):
    for e in engines:
        v = t.cnt[e]
        if v > 0 and t.seen[e].get(e, 0) < v:
            t.eng[e].wait_ge(t._sem(e), v)
            t.seen[e][e] = v


class Kern:
    def __init__(self, stop_after=None, dbg=False):
        self.stop_after = stop_after
        nc = self.nc = bass.Bass("TRN2", target_bir_lowering=False)
        self.xh = nc.dram_tensor("xh", [HALO + T, D], F32, kind="ExternalInput").ap()
        self.memd = nc.dram_tensor("memb", [256, D], F32, kind="ExternalInput").ap()
        self.constd = nc.dram_tensor("consts", [128, CW], F32, kind="ExternalInput").ap()
        self.ropeCd = nc.dram_tensor("ropeC", [64, 4096], F32, kind="ExternalInput").ap()
        self.ropeSd = nc.dram_tensor("ropeS", [64, 4096], F32, kind="ExternalInput").ap()
        self.wpack = nc.dram_tensor("wpack", [NWB, 128, WBLK], F32, kind="ExternalInput").ap()
        self.growd = nc.dram_tensor("grow", [128, D], F32, kind="ExternalInput").ap()
        self.outd = nc.dram_tensor("out", [T, D], F32, kind="ExternalOutput").ap()
        self.dbg = dbg
        if dbg:
            self.dbgd = nc.dram_tensor("dbg", [128, 8 * 4096], F32, kind="ExternalOutput").ap()
        self.t = Trk(nc)
        self.consts = nc.alloc_sbuf_tensor("consts_sb", [128, CW], F32)
        self.ones_bf = nc.alloc_sbuf_tensor("ones_bf", [128, 128], BF16)
        self.ident_bf = nc.alloc_sbuf_tensor("ident_bf", [128, 128], BF16)
        self.mask_bf = nc.alloc_sbuf_tensor("mask_bf", [128, 256], BF16)
        self.mask0_bf = nc.alloc_sbuf_tensor("mask0_bf", [128, 256], BF16)
        self.rcols = nc.alloc_sbuf_tensor("rcols", [128, 8], F32)
        base = nc.SBUF_PARTITION_SIZE_BYTES - nc.sbuf_bytes_remaining
        base = (base + 63) // 64 * 64
        self.arena_total = 24576 + 65536 + 16384 + 24576 + 73728 + 2048
        nc.alloc_sbuf_tensor("arena", [128, self.arena_total + 64], mybir.dt.uint8)
        self.oW = base
        self.oU = self.oW + 24576
        self.oB = self.oU + 65536
        self.oC = self.oB + 16384
        self.oD = self.oC + 24576
        self._n = 0
        self.ps = [nc.alloc_psum_tensor(f"psb{i}", [128, 512], F32) for i in range(8)]
        self._bank = 0
        self.bank_pool = list(range(8))
        self.wslots = [self.at(self.oW + i * 8192, [128, WBLK], BF16) for i in range(3)]
        self.wq = []
        self.wq_issued = 0
        self.wq_pos = 0

    def at(self, off, shape, dt):
        self._n += 1
        assert off % 32 == 0, off
        return self.nc.alloc_sbuf_tensor_at(f"m{self._n}", shape, dt, offset=off)

    def bank(self):
        pool = self.bank_pool
        b = pool[self._bank % len(pool)]
        self._bank += 1
        return b

    def cc(self, col, n=1):
        return self.consts[:, col:col + n]

    def w_plan(self, blocks):
        self.wq.extend(blocks)

    def _w_issue(self):
        i = self.wq_issued
        blk = self.wq[i]
        slot = i % 3
        dst = self.wslots[slot]
        self.t.dma("pool", lambda: self.nc.gpsimd.dma_start(out=dst[:, :], in_=self.wpack[blk]),
                   stream=f"w{slot}", w=[("w", slot)])
        self.wq_issued += 1

    def w_take(self, n):
        first = self.wq_pos
        while self.wq_issued < min(len(self.wq), first + 3):
            self._w_issue()
        out = []
        for i in range(n):
            slot = (first + i) % 3
            out.append((self.wslots[slot], ("w", slot)))
        self.wq_pos += n
        return out

    def w_next(self):
        return self.w_take(1)[0]

    def mm_group(self, b, ncols, pairs, r_keys, col0=0):
        n = len(pairs)
        out = self.ps[b][:, col0:col0 + ncols]
        for i, (lh, rh) in enumerate(pairs):
            self.t.op("pe", lambda lh=lh, rh=rh, i=i: self.nc.tensor.matmul(
                out, lhsT=lh, rhs=rh, start=(i == 0), stop=(i == n - 1)),
                r=r_keys if i == 0 else (), w=[("ps", b)])
        self.t._record(("pe", self.t.cnt["pe"]), r_keys, ())

    def norm(self, xall, xk, xkeys, gcol, out_fn, okeys, ntok, sq, lnv, rstd, tag, split=False, pool_sq=None):
        nc, t = self.nc, self.t

        def part_sq():
            if pool_sq is None:
                t.op("act", lambda: nc.scalar.activation(out=sq[:, :, 0:ntok], in_=xall, func=AF.Square),
                     r=xkeys, w=[("sq", tag)])
            else:
                xlo, xhi = pool_sq
                t.op("act", lambda: nc.scalar.activation(out=sq[:, 0:4, 0:ntok], in_=xlo, func=AF.Square),
                     r=xkeys, w=[("sq", tag)])
                t.op("pool", lambda: nc.gpsimd.tensor_tensor(out=sq[:, 4:8, 0:ntok], in0=xhi, in1=xhi,
                                                             op=ALU.mult),
                     r=xkeys, w=[("sq", tag, 1)])

        def part_rest():
            self._norm_rest(xk, xkeys, gcol, out_fn, okeys, ntok, sq, lnv, rstd, tag)
        if split:
            return part_sq, part_rest
        part_sq()
        part_rest()

    def _norm_rest(self, xk, xkeys, gcol, out_fn, okeys, ntok, sq, lnv, rstd, tag):
        nc, t = self.nc, self.t
        b = self.bank()
        self.mm_group(b, ntok, [(self.ones_bf[:, :], sq[:, k, 0:ntok]) for k in range(8)],
                      [("sq", tag), ("sq", tag, 1)])
        t.op("act", lambda: nc.scalar.activation(out=lnv[:, 0:ntok], in_=self.ps[b][:, 0:ntok], func=AF.Ln,
                                                 bias=self.cc(C_EPS), scale=1.0 / D),
             r=[("ps", b)], w=[("lnv", tag)])
        t.op("act", lambda: nc.scalar.activation(out=rstd[:, 0:ntok], in_=lnv[:, 0:ntok], func=AF.Exp, scale=-0.5),
             r=[("lnv", tag)], w=[("rstd", tag)])
        for k in range(8):
            t.op("dve", lambda k=k: nc.vector.scalar_tensor_tensor(
                out=out_fn(k), in0=xk(k), scalar=self.cc(gcol + k), in1=rstd[:, 0:ntok],
                op0=ALU.mult, op1=ALU.mult),
                r=xkeys + [("rstd", tag)], w=okeys)

    def norm_split(self, xall, xk, xkeys, gcol, ug_fn, ugkeys, ntok, sq, lnv, rstd_out, rkey, tag):
        nc, t = self.nc, self.t
        for k in range(8):
            if k < 4:
                t.op("act", lambda k=k: nc.scalar.activation(out=ug_fn(k), in_=xk(k), func=AF.Copy,
                                                             scale=self.cc(gcol + k)),
                     r=xkeys + ["consts"], w=ugkeys)
            else:
                t.op("dve", lambda k=k: nc.vector.tensor_scalar(out=ug_fn(k), in0=xk(k), scalar1=self.cc(gcol + k),
                                                                scalar2=None, op0=ALU.mult),
                     r=xkeys + ["consts"], w=ugkeys)
        def part_sq():
            t.op("act", lambda: nc.scalar.activation(out=sq[:, :, 0:ntok], in_=xall, func=AF.Square),
                 r=xkeys, w=[("sq", tag)])

        def part_b():
            b = self.bank()
            self.mm_group(b, ntok, [(self.ones_bf[:, :], sq[:, k, 0:ntok]) for k in range(8)], [("sq", tag)])
            t.op("act", lambda: nc.scalar.activation(out=lnv[:, 0:ntok], in_=self.ps[b][:, 0:ntok], func=AF.Ln,
                                                     bias=self.cc(C_EPS), scale=1.0 / D),
                 r=[("ps", b)], w=[("lnv", tag)])
            t.op("act", lambda: nc.scalar.activation(out=rstd_out[:, 0:ntok], in_=lnv[:, 0:ntok], func=AF.Exp,
                                                     scale=-0.5),
                 r=[("lnv", tag)], w=[rkey])
        return part_sq, part_b

    def io_alloc(self, nslots, exclude=()):
        while True:
            slot = self._xs_i % nslots
            self._xs_i += 1
            if slot not in exclude:
                return slot

    def lt_issue(self, rows, xs_slots, stream, exclude=()):
        nc, t = self.nc, self.t
        slot = self.io_alloc(len(xs_slots), exclude)
        xs = xs_slots[slot]
        t.dma("sp", lambda: nc.sync.dma_start(out=xs[:, :], in_=rows), stream=f"{stream}{slot}",
              w=[("xs", slot)])
        return slot

    def load_transpose(self, src_rows, xs_slots, nsub, dst, dkey, stream, evac=("act", "dve"), pre=(), s_off=0):
        nc, t = self.nc, self.t
        ident = self.consts[:, C_ID:C_ID + 128]
        pre = list(pre)
        for s_ in range(nsub):
            s = s_ + s_off
            if s_ < len(pre):
                slot = pre[s_]
            else:
                slot = self.lt_issue(src_rows(s), xs_slots, stream, exclude=pre[s_ + 1:])
            xs = xs_slots[slot]
            for hf in range(2):
                b = self.bank()
                for kk in range(4):
                    k = hf * 4 + kk
                    t.op("pe", lambda k=k, kk=kk: nc.tensor.transpose(
                        self.ps[b][:, kk * 128:(kk + 1) * 128], xs[:, k * 128:(k + 1) * 128], ident),
                        r=[("xs", slot)], w=[("ps", b)])
                src = self.ps[b][:, 0:512].rearrange("p (a b) -> p a b", a=4)
                dd = dst[:, hf * 4:hf * 4 + 4, s * 128:(s + 1) * 128]
                if evac[hf] == "act":
                    t.op("act", lambda: nc.scalar.copy(out=dd, in_=src), r=[("ps", b)], w=dkey(s, hf))
                else:
                    t.op("dve", lambda: nc.vector.tensor_copy(out=dd, in_=src), r=[("ps", b)], w=dkey(s, hf))

    def ucols(self, k, a0, n, step=1):
        if a0 < 2048:
            tt, o = self.uTh, a0
        else:
            tt, o = self.uTm, a0 - 2048
        assert o + (n - 1) * step < 2048
        return tt[:, k, o:o + (n - 1) * step + 1:step]

    def ukeys(self, a0, n, step=1):
        return [("uT", i) for i in range(a0 // 512, (a0 + (n - 1) * step) // 512 + 1)]

    def kcols(self, a0, n, step=1):
        if a0 < 2048:
            tt, o = self.kTh, a0
        else:
            tt, o = self.kTm, a0 - 2048
        assert o + (n - 1) * step < 2048
        return tt[:, o:o + (n - 1) * step + 1:step]

    def kkeys(self, a0, n, step=1):
        return [("kT", i) for i in range(a0 // 512, (a0 + (n - 1) * step) // 512 + 1)]

    def build(self):
        nc, t = self.nc, self.t
        self._xs_i = 0
        em = [WB_E + j for j in range(5)]
        for i in range(3):
            em += [WB_M + i, WB_E + 5 + i]
        em += [WB_M + 3]
        plan = list(range(WB_A, WB_A + 12)) + list(range(WB_C, WB_C + 3)) * 2 + em + list(range(WB_OUT, NWB)) * 2
        self.w_plan(plan)
        oU, oB, oC, oD = self.oU, self.oB, self.oC, self.oD
        self.uTh = self.at(oU, [128, 8, 2048], BF16)
        self.uTm = self.at(oU + 32768, [128, 8, 2048], BF16)
        self.attnT = self.at(oB, [128, 4, 2048], BF16)
        self.cT = self.at(oC, [128, 6, 2048], BF16)
        ropeC = self.at(oC, [64, 4096], F32)
        ropeS = self.at(oD + 57344, [64, 4096], F32)

        t.dma("sp", lambda: nc.sync.dma_start(out=self.consts[:, :], in_=self.constd), stream="c0", w=["consts"])
        t.dma("sp", lambda: nc.sync.dma_start(out=ropeC[:, :], in_=self.ropeCd), stream="c1", w=["ropeC"])
        t.op("dve", lambda: nc.vector.memset(self.ones_bf[:, :], 1.0), w=["ones"])
        t.op("act", lambda: nc.scalar.copy(out=self.ident_bf[:, :], in_=self.consts[:, C_ID:C_ID + 128]),
             r=["consts"], w=["identbf"])
        t.op("act", lambda: nc.scalar.copy(out=self.mask_bf[:, :], in_=self.consts[:, C_MASK:C_MASK + 256]),
             r=["consts"], w=["mask"])
        t.op("act", lambda: nc.scalar.copy(out=self.mask0_bf[:, :], in_=self.consts[:, C_MASK0:C_MASK0 + 256]),
             r=["consts"], w=["mask"])
        t.barrier()

        while self.wq_issued < 3:
            self._w_issue()
        t.label = "p0"
        xs0 = [self.at(oD + i * 4096, [128, 1024], F32) for i in range(3)]
        xTts = [self.at(oD + 12288 + i * 16384, [128, 8, 512], F32) for i in range(3)]
        sq = self.at(oD + 61440, [128, 8, 512], BF16)
        lnv = self.at(oD + 69632, [128, 512], F32)
        rstd = self.at(oD + 71680, [128, 512], F32)
        def p0_load(tt):
            xk_ = ("xTt", tt % 3)
            self.load_transpose(lambda s, tt=tt: self.xh[tt * 512 + s * 128: tt * 512 + (s + 1) * 128, :],
                                xs0, 4, xTts[tt % 3], lambda s, hf, xk_=xk_: [xk_ + (hf,)], "x", evac=("act", "act"))

        def p0_norm(tt):
            xTt = xTts[tt % 3]
            xk_ = ("xTt", tt % 3)
            dstT = self.uTh if tt < 4 else self.uTm
            c0 = (tt % 4) * 512
            return self.norm(xTt[:, :, :], lambda k: xTt[:, k, :], [xk_ + (0,), xk_ + (1,)], C_GMIX,
                             lambda k: dstT[:, k, c0:c0 + 512], [("uT", tt)], 512, sq, lnv, rstd, "p0", split=True,
                             pool_sq=(xTt[:, 0:4, :], xTt[:, 4:8, :]))

        p0_load(0)
        parts = p0_norm(0)
        parts[0]()
        for tt in range(8):
            if tt + 1 < 8:
                p0_load(tt + 1)
            parts[1]()
            if tt + 1 < 8:
                parts = p0_norm(tt + 1)
                parts[0]()
        if self.stop_after == "p0":
            return self.finish_dbg([(self.uTm, 8 * 2048, BF16)])
        t.barrier(waiters=("act", "dve", "sp"))

        t.dma("sp", lambda: nc.sync.dma_start(out=ropeS[:, :], in_=self.ropeSd), stream="c2", w=["ropeS"])
        self.qT = self.at(oD, [128, 2048], BF16)
        self.kTh = self.at(oD + 4096, [128, 2048], BF16)
        self.kTm = self.at(oD + 8192, [128, 2048], BF16)
        Vt = self.at(oD + 12288, [128, 32, 128], BF16)
        acc = self.at(oD + 20480, [128, 2, 2048], F32)
        a32 = [self.at(oD + 36864 + i * 2048, [128, 512], F32) for i in range(2)]
        tmp = [self.at(oD + 40960 + i * 2048, [128, 512], F32) for i in range(2)]
        pts = [self.at(oD + 45056 + i * 512, [128, 256], BF16) for i in range(4)]
        pms = [self.at(oD + 47104 + i * 512, [128, 256], BF16) for i in range(4)]
        for i in range(2):
            t.op("dve", lambda i=i: nc.vector.memset(tmp[i][:, :], 0.0), w=[("tmp", i)])
        rope_i = 0
        blk_i = 0
        scale = 1.0 / np.sqrt(128.0)
        for h4 in range(4):
            for g in range(3):
                d = DIL[g]
                halo = 128 * d
                wt, wkey = self.w_next()
                wq_ = lambda k: wt[:, k * 128:(k + 1) * 128]
                wk_ = lambda k: wt[:, 1024 + k * 128:1024 + (k + 1) * 128]
                wv_ = lambda k: wt[:, 2048 + k * 128:2048 + (k + 1) * 128]
                def emit_qk():
                    nonlocal rope_i
                    t.label = "A.qk"
                    jobs = []
                    for tt in range(4):
                        jobs.append(("q", 2048 + tt * 512, 512))
                    a = 2048 - halo
                    while a < 4096:
                        n = min(512, 4096 - a, 512 - (a % 512) if a % 512 else 512)
                        jobs.append(("k", a, n))
                        a += n
                    for (kind, a0, n) in jobs:
                        wsel = wq_ if kind == "q" else wk_
                        b = self.bank()
                        self.mm_group(b, n, [(wsel(k), self.ucols(k, a0, n)) for k in range(8)],
                                      [wkey] + self.ukeys(a0, n))
                        if kind == "q":
                            dT, dc = self.qT, a0 - 2048
                            dkeys = [("qT", (a0 - 2048) // 512)]
                        else:
                            dT, dc = (self.kTh, a0) if a0 < 2048 else (self.kTm, a0 - 2048)
                            dkeys = self.kkeys(a0, n)
                        z = self.ps[b]
                        sl = rope_i % 2
                        rope_i += 1
                        A, Tm = a32[sl], tmp[sl]
                        t.op("act", lambda: nc.scalar.copy(out=dT[64:128, dc:dc + n], in_=z[64:128, 0:n]),
                             r=[("ps", b)], w=[(kk, "hi") for kk in dkeys] + dkeys)
                        t.op("dve", lambda: nc.vector.tensor_tensor(out=A[0:64, 0:n], in0=z[0:64, 0:n],
                                                                    in1=ropeC[0:64, a0:a0 + n], op=ALU.mult),
                             r=[("ps", b), "ropeC"], w=[("a32", sl)])
                        t.op("dve", lambda: nc.vector.tensor_tensor(out=Tm[0:16, 0:n], in0=z[32:48, 0:n],
                                                                    in1=ropeS[32:48, a0:a0 + n], op=ALU.mult),
                             r=[("ps", b), "ropeS"], w=[("tmp", sl)])
                        t.op("dve", lambda: nc.vector.tensor_tensor(out=Tm[32:48, 0:n], in0=z[0:16, 0:n],
                                                                    in1=ropeS[0:16, a0:a0 + n], op=ALU.mult),
                             r=[("ps", b), "ropeS"], w=[("tmp", sl)])
                        t.op("pool", lambda: nc.gpsimd.tensor_tensor(out=dT[0:64, dc:dc + n], in0=A[0:64, 0:n],
                                                                     in1=Tm[0:64, 0:n], op=ALU.add),
                             r=[("a32", sl), ("tmp", sl)], w=dkeys)
                def emit_v():
                    t.label = "A.v"
                    nb = 16 // d
                    vlist = [(r, j) for r in range(d) for j in range(-1, nb)]
                    for v0 in range(0, len(vlist), 4):
                        grp = vlist[v0:v0 + 4]
                        b = self.bank()
                        rk = [wkey]
                        for gi, (r, j) in enumerate(grp):
                            a0 = 2048 + r + 128 * d * j
                            rk = rk + self.ukeys(a0, 128, d)
                            for k in range(8):
                                t.op("pe", lambda k=k, gi=gi, a0=a0: nc.tensor.matmul(
                                    self.ps[b][:, gi * 128:(gi + 1) * 128], lhsT=self.ucols(k, a0, 128, d),
                                    rhs=wv_(k), start=(k == 0), stop=(k == 7)),
                                    r=rk if k == 0 else (), w=[("ps", b)])
                        t._record(("pe", t.cnt["pe"]), rk, ())
                        ng = len(grp)
                        t.op("act", lambda v0=v0, ng=ng, b=b: nc.scalar.copy(
                            out=Vt[:, v0:v0 + ng, :],
                            in_=self.ps[b][:, 0:ng * 128].rearrange("p (a b) -> p a b", a=ng)),
                            r=[("ps", b)], w=[("V", v0 + i) for i in range(ng)])
                nb = 16 // d
                if h4 == 0 and g == 0:
                    emit_v()
                    emit_qk()
                else:
                    emit_qk()
                    emit_v()
                t.label = "A.attn"
                blocks = [(r, j) for r in range(d) for j in range(nb)]
                LAG = 3
                st = {}
                for it in range(len(blocks) + LAG):
                    if it < len(blocks):
                        r, j = blocks[it]
                        a0 = 2048 + r + 128 * d * j
                        ap_ = a0 - 128 * d
                        b = self.bank()
                        qv = self.qT[:, a0 - 2048:a0 - 2048 + 127 * d + 1:d]
                        qk = [("qT", i) for i in range((a0 - 2048) // 512, (a0 - 2048 + 127 * d) // 512 + 1)]
                        mk = self.mask0_bf if j == 0 else self.mask_bf
                        t.op("pe", lambda: nc.tensor.matmul(self.ps[b][:, 0:256], lhsT=self.ident_bf[:, :],
                                                            rhs=mk[:, :], start=True, stop=False),
                             r=["identbf", "mask"], w=[("ps", b)])
                        t.op("pe", lambda: nc.tensor.matmul(self.ps[b][:, 0:128], lhsT=self.kcols(ap_, 128, d),
                                                            rhs=qv, start=False, stop=False),
                             r=qk + self.kkeys(ap_, 128, d), w=[("ps", b)])
                        t.op("pe", lambda: nc.tensor.matmul(self.ps[b][:, 128:256], lhsT=self.kcols(a0, 128, d),
                                                            rhs=qv, start=False, stop=True),
                             r=qk + self.kkeys(a0, 128, d), w=[("ps", b)])
                        ps_i = blk_i % 4
                        blk_i += 1
                        pm = pms[ps_i]
                        t.op("act", lambda: nc.scalar.activation(out=pm[:, :], in_=self.ps[b][:, 0:256],
                                                                 func=AF.Exp, scale=float(scale)),
                             r=[("ps", b)], w=[("pm", ps_i)])
                        st[it] = (b, ps_i, r, j, a0)
                    if it >= LAG:
                        b, ps_i, r, j, a0 = st.pop(it - LAG)
                        pm = pms[ps_i]
                        vprev = r * (nb + 1) + j
                        vcur = vprev + 1
                        o = self.ps[b][:, 256:384]
                        dn = self.ps[b][:, 384:512]
                        t.op("pe", lambda: nc.tensor.matmul(o, lhsT=Vt[:, vprev, :], rhs=pm[:, 0:128],
                                                            start=True, stop=False),
                             r=[("V", vprev), ("pm", ps_i)], w=[("ps", b)])
                        t.op("pe", lambda: nc.tensor.matmul(o, lhsT=Vt[:, vcur, :], rhs=pm[:, 128:256],
                                                            start=False, stop=True),
                             r=[("V", vcur)], w=[("ps", b)])
                        t.op("pe", lambda: nc.tensor.matmul(dn, lhsT=self.ones_bf[:, :], rhs=pm[:, 0:128],
                                                            start=True, stop=False), r=["ones"], w=[("ps", b)])
                        t.op("pe", lambda: nc.tensor.matmul(dn, lhsT=self.ones_bf[:, :], rhs=pm[:, 128:256],
                                                            start=False, stop=True), r=[("pm", ps_i)], w=[("ps", b)])
                        q0 = a0 - 2048
                        dst = acc[:, :, q0:q0 + 127 * d + 1:d]
                        src = self.ps[b][:, 256:512].rearrange("p (a b) -> p a b", a=2)
                        akeys = [("acc", i) for i in range(q0 // 512, (q0 + 127 * d) // 512 + 1)]
                        if g == 0:
                            t.op("act", lambda: nc.scalar.copy(out=dst, in_=src), r=[("ps", b)], w=akeys)
                        else:
                            t.op("dve", lambda: nc.vector.tensor_tensor(out=dst, in0=src, in1=dst, op=ALU.add),
                                 r=[("ps", b)] + akeys, w=akeys)
            t.label = "A.fin"
            for tt in range(4):
                sl_ = slice(tt * 512, (tt + 1) * 512)
                t.op("act", lambda: nc.scalar.activation(out=acc[:, 1, sl_], in_=acc[:, 1, sl_], func=AF.Ln),
                     r=[("acc", tt)], w=[("acc", tt)])
                t.op("act", lambda: nc.scalar.activation(out=acc[:, 1, sl_], in_=acc[:, 1, sl_], func=AF.Exp,
                                                         scale=-1.0),
                     r=[("acc", tt)], w=[("acc", tt)])
                t.op("dve", lambda: nc.vector.tensor_tensor(out=self.attnT[:, h4, sl_], in0=acc[:, 0, sl_],
                                                            in1=acc[:, 1, sl_], op=ALU.mult),
                     r=[("acc", tt)], w=[("attnT", tt)])
        if self.stop_after == "pA":
            return self.finish_dbg([(self.attnT, 4 * 2048, BF16)])
        t.barrier(waiters=("act", "dve", "sp"))

        conv = self.at(oD, [128, 6, 1024], F32)
        cglu = [self.at(oD + 24576 + i * 2176, [128, 1056], BF16) for i in range(2)]
        diags = [self.at(oD + 28928 + i * 7936, [128, 31, 128], BF16) for i in range(2)]
        sg = [self.at(oD + 44800 + i * 2048, [128, 512], F32) for i in range(2)]
        xbs = [self.at(oD + 48896 + i * 1024, [128, 512], BF16) for i in range(4)]
        xsqs = [self.at(oD + 55040 + i * 1024, [128, 512], BF16) for i in range(4)]
        st_pend = []
        self.bank_pool = [0, 1, 2, 3]
        st_i = 0
        mean = self.at(oD + 61184, [128, 512], F32)
        var = self.at(oD + 63232, [128, 512], F32)
        lnv = self.at(oD + 65280, [128, 512], F32)
        rstd = self.at(oD + 67328, [128, 512], F32)
        t1 = [self.at(oD + 69376, [128, 512], F32), self.at(oD + 52992, [128, 512], F32)]
        t2 = [self.at(oD + 71424, [128, 512], F32), self.at(oD + 59136, [128, 512], F32)]
        ln_pending = []
        dg_i = 0
        sg_i = 0
        wst = {}
        for half in range(2):
            base_a = 2048 + half * 1024

            def glu_part(jj, half=half, base_a=base_a):
                nonlocal dg_i, sg_i
                if jj % 2 == 0:
                    wst["w"] = self.w_next()
                wt, wkey = wst["w"]
                wa = lambda k, o=(jj % 2) * 2048: wt[:, o + k * 128:o + (k + 1) * 128]
                wb = lambda k, o=(jj % 2) * 2048 + 1024: wt[:, o + k * 128:o + (k + 1) * 128]
                cg = cglu[jj % 2]
                ckey = ("cglu", jj % 2)
                t.label = "C.diag"
                diag = diags[dg_i % 2]
                dgk = ("diag", dg_i % 2)
                dg_i += 1
                t.op("dve", lambda: nc.vector.tensor_tensor(
                    out=diag[:, :, :], in0=self.ident_bf[:, :].unsqueeze(1).broadcast_to([128, 31, 128]),
                    in1=self.consts[:, C_CONVW + jj * 31:C_CONVW + (jj + 1) * 31].unsqueeze(2).broadcast_to(
                        [128, 31, 128]), op=ALU.mult),
                    r=["identbf", "consts"], w=[dgk])
                t.label = "C.glu"
                for (a0, n, c0) in ((base_a - 32, 32, 0), (base_a, 512, 32), (base_a + 512, 512, 544)):
                    ba = self.bank()
                    self.mm_group(ba, n, [(wa(k), self.ucols(k, a0, n)) for k in range(8)],
                                  [wkey] + self.ukeys(a0, n))
                    bb = self.bank()
                    self.mm_group(bb, n, [(wb(k), self.ucols(k, a0, n)) for k in range(8)],
                                  [wkey] + self.ukeys(a0, n))
                    s_ = sg[sg_i % 2]
                    sk = ("sg", sg_i % 2)
                    sg_i += 1
                    t.op("act", lambda: nc.scalar.activation(out=s_[:, 0:n], in_=self.ps[bb][:, 0:n],
                                                             func=AF.Sigmoid), r=[("ps", bb)], w=[sk])
                    t.op("dve", lambda: nc.vector.tensor_tensor(out=cg[:, c0:c0 + n], in0=self.ps[ba][:, 0:n],
                                                                in1=s_[:, 0:n], op=ALU.mult),
                         r=[("ps", ba), sk], w=[ckey])
                return cg, ckey, diag, dgk

            def conv_part(jj, ctx, mid_hook=None):
                nonlocal st_i
                cg, ckey, diag, dgk = ctx
                t.label = "C.conv"
                for tt in range(2):
                    b = self.bank()
                    self.mm_group(b, 512, [(diag[:, tap, :], cg[:, 2 + tt * 512 + tap: 2 + tt * 512 + tap + 512])
                                           for tap in range(31)], [dgk, ckey])
                    cv_ = conv[:, jj, tt * 512:(tt + 1) * 512]
                    t.op("act", lambda b=b: nc.scalar.activation(
                        out=cv_, in_=self.ps[b][:, :], func=AF.Identity,
                        bias=self.cc(C_CONVB + jj)), r=[("ps", b), "consts"], w=[("conv", tt)])
                    xb_, xq_ = xbs[st_i % 4], xsqs[st_i % 4]
                    kb_, kq_ = ("xb", st_i % 4), ("xsq", st_i % 4)
                    st_i += 1
                    t.op("act", lambda: nc.scalar.copy(out=xb_[:, :], in_=cv_), r=[("conv", tt)], w=[kb_])
                    t.op("act", lambda: nc.scalar.activation(out=xq_[:, :], in_=cv_, func=AF.Square),
                         r=[("conv", tt)], w=[kq_])

                    def _stats(tt=tt, jj=jj, xb_=xb_, xq_=xq_, kb_=kb_, kq_=kq_):
                        t.op("pe", lambda: nc.tensor.matmul(self.ps[4 + tt][:, :], lhsT=self.ones_bf[:, :],
                                                            rhs=xb_[:, :], start=(jj == 0), stop=(jj == 5)),
                             r=[kb_, "ones"], w=[("ps", 4 + tt)])
                        t.op("pe", lambda: nc.tensor.matmul(self.ps[6 + tt][:, :], lhsT=self.ones_bf[:, :],
                                                            rhs=xq_[:, :], start=(jj == 0), stop=(jj == 5)),
                             r=[kq_], w=[("ps", 6 + tt)])
                    st_pend.append(_stats)
                    if len(st_pend) > 2:
                        st_pend.pop(0)()
                    if tt == 0 and mid_hook is not None:
                        mid_hook()
                        t.label = "C.conv"

            ctx_next = glu_part(0)
            for jj in range(6):
                ctx = ctx_next
                if jj + 1 < 6:
                    ctx_next = glu_part(jj + 1)
                hook = None
                if jj == 0 and ln_pending:
                    ln_pending.pop(0)()
                    hook = ln_pending.pop(0)
                conv_part(jj, ctx, hook)
            while st_pend:
                st_pend.pop(0)()
            def ln_stage(tt, half=half):
                t.label = "C.ln"
                if True:
                    b1 = 4 + tt
                    b2 = 6 + tt
                    t.op("dve", lambda: nc.vector.tensor_scalar(out=mean[:, :], in0=self.ps[b1][:, :],
                                                                scalar1=1.0 / 768, scalar2=None, op0=ALU.mult),
                         r=[("ps", b1)], w=["mean"])
                    t.op("dve", lambda: nc.vector.tensor_tensor(out=var[:, :], in0=mean[:, :], in1=mean[:, :],
                                                                op=ALU.mult), r=["mean"], w=["var"])
                    t.op("dve", lambda: nc.vector.scalar_tensor_tensor(out=var[:, :], in0=self.ps[b2][:, :],
                                                                       scalar=1.0 / 768, in1=var[:, :],
                                                                       op0=ALU.mult, op1=ALU.subtract),
                         r=[("ps", b2), "var"], w=["var"])
                    t.op("act", lambda: nc.scalar.activation(out=lnv[:, :], in_=var[:, :], func=AF.Ln,
                                                             bias=self.cc(C_EPS)), r=["var"], w=["lnvc"])
                    t.op("act", lambda: nc.scalar.activation(out=rstd[:, :], in_=lnv[:, :], func=AF.Exp, scale=-0.5),
                         r=["lnvc"], w=["rstdc"])
                    for jj in range(6):
                        a_, b_ = t1[jj % 2], t2[jj % 2]
                        t.op("dve", lambda: nc.vector.tensor_tensor(out=a_[:, :], in0=conv[:, jj, tt * 512:(tt + 1) * 512],
                                                                    in1=mean[:, :], op=ALU.subtract),
                             r=[("conv", tt), "mean"], w=[("t1", jj % 2)])
                        t.op("dve", lambda: nc.vector.scalar_tensor_tensor(out=b_[:, :], in0=a_[:, :],
                                                                           scalar=self.cc(C_LNG + jj), in1=rstd[:, :],
                                                                           op0=ALU.mult, op1=ALU.mult),
                             r=[("t1", jj % 2), "rstdc"], w=[("t2", jj % 2)])
                        c0 = half * 1024 + tt * 512
                        t.op("act", lambda: nc.scalar.activation(out=self.cT[:, jj, c0:c0 + 512], in_=b_[:, :],
                                                                 func=AF.Silu, bias=self.cc(C_LNB + jj)),
                             r=[("t2", jj % 2)], w=[("cT", c0 // 512)])
            ln_pending.append(lambda f=ln_stage: f(0))
            ln_pending.append(lambda f=ln_stage: f(1))
        while len(ln_pending) > 1:
            ln_pending.pop(0)()
        self.bank_pool = [0, 1, 2, 3, 4, 6]
        self._bank = 0
        if self.stop_after == "pC":
            return self.finish_dbg([(self.cT, 6 * 2048, BF16)])
        t.barrier(waiters=("sp",))
        _selfsync(t)
        xs1 = [self.at(oD + 32768 + i * 4096, [128, 1024], F32) for i in range(4)]
        self._xs_i = 0
        pre_mem = [self.lt_issue(self.memd[s_ * 128:(s_ + 1) * 128, :], xs1, "xb") for s_ in range(2)]
        pre_x0 = [self.lt_issue(self.xh[HALO + s_ * 128: HALO + (s_ + 1) * 128, :], xs1, "xb") for s_ in range(2)]

        t.label = "E"
        mergedT = self.at(oU, [128, 8, 2048], BF16)
        sa = [self.at(oD + 24576, [128, 512], F32)] * 2
        sb = [self.at(oD + 26624, [128, 512], F32)] * 2
        e1 = [self.at(oD + 28672, [128, 512], F32)] * 2
        e2 = [self.at(oD + 30720, [128, 512], F32)] * 2
        ei = 0
        ckT = self.at(oD + 63488, [128, 8, 256], BF16)
        cV = self.at(oD + 67584, [128, 2, 1024], BF16)

        def m_block(mi):
            t.label = "M"
            wt_, wkey_ = self.w_next()
            if mi < 2:
                for jj in range(4):
                    j_ = mi * 4 + jj
                    b = self.bank()
                    self.mm_group(b, 256, [(wt_[:, jj * 1024 + k * 128: jj * 1024 + (k + 1) * 128], mT[:, k, :])
                                           for k in range(8)], [wkey_, "mT"])
                    t.op("act", lambda j_=j_, b=b: nc.scalar.copy(out=ckT[:, j_, :], in_=self.ps[b][:, 0:256]),
                         r=[("ps", b)], w=["ckT"])
            else:
                blk = mi - 2
                for mc in range(2):
                    b = self.bank()
                    self.mm_group(b, 512, [(mT[:, k, mc * 128:(mc + 1) * 128], wt_[:, k * 512:(k + 1) * 512])
                                           for k in range(8)], [wkey_, "mT"])
                    t.op("act", lambda mc=mc, b=b, blk=blk: nc.scalar.copy(
                        out=cV[:, mc, blk * 512:(blk + 1) * 512], in_=self.ps[b][:, :]), r=[("ps", b)], w=["cV"])
            t.label = "E"
        memT = self.at(oD + 0, [128, 8, 256], F32)
        mT = self.at(oD + 8192, [128, 8, 256], BF16)
        sq_m = self.at(oD + 49152, [128, 8, 512], BF16)
        lnv_m = self.at(oD + 57344, [128, 512], F32)
        rstd_m = self.at(oD + 59392, [128, 512], F32)
        for j in range(8):
            if j == 4:
                _selfsync(t)
                t.label = "M"
                self.load_transpose(lambda s: self.memd[s * 128:(s + 1) * 128, :], xs1, 2, memT,
                                    lambda s, hf: [("memT", hf)], "xb", pre=pre_mem)
                self.norm(memT[:, :, :], lambda k: memT[:, k, :], [("memT", 0), ("memT", 1)], C_GMEM,
                          lambda k: mT[:, k, :], ["mT"], 256, sq_m, lnv_m, rstd_m, "pm")
                t.label = "E"
            wt, wkey = self.w_next()
            wga = lambda k: wt[:, k * 128:(k + 1) * 128]
            wap = lambda k: wt[:, 1024 + k * 128:1024 + (k + 1) * 128]
            wgb = lambda k: wt[:, 1536 + k * 128:1536 + (k + 1) * 128]
            wcp = lambda k: wt[:, 2560 + k * 128:2560 + (k + 1) * 128]
            for tt in range(4):
                cs = slice(tt * 512, (tt + 1) * 512)
                bga = self.bank()
                self.mm_group(bga, 512, [(wga(k), self.uTm[:, k, cs]) for k in range(8)], [wkey, ("uT", 4 + tt)])
                bya = self.bank()
                self.mm_group(bya, 512, [(wap(k), self.attnT[:, k, cs]) for k in range(4)], [wkey, ("attnT", tt)])
                bgb = self.bank()
                self.mm_group(bgb, 512, [(wgb(k), self.uTm[:, k, cs]) for k in range(8)], [wkey, ("uT", 4 + tt)])
                byc = self.bank()
                self.mm_group(byc, 512, [(wcp(k), self.cT[:, k, cs]) for k in range(6)], [wkey, ("cT", tt)])
                s = 0
                ei += 1
                t.op("act", lambda: nc.scalar.activation(out=sa[s][:, :], in_=self.ps[bga][:, :], func=AF.Sigmoid,
                                                         bias=self.cc(C_BGA + j)), r=[("ps", bga)], w=[("sa", s)])
                t.op("act", lambda: nc.scalar.activation(out=sb[s][:, :], in_=self.ps[bgb][:, :], func=AF.Sigmoid,
                                                         bias=self.cc(C_BGB + j)), r=[("ps", bgb)], w=[("sb", s)])
                t.op("dve", lambda: nc.vector.tensor_tensor(out=e1[s][:, :], in0=self.ps[bya][:, :], in1=sa[s][:, :],
                                                            op=ALU.mult), r=[("ps", bya), ("sa", s)], w=[("e1", s)])
                t.op("dve", lambda: nc.vector.tensor_tensor(out=e2[s][:, :], in0=self.ps[byc][:, :], in1=sb[s][:, :],
                                                            op=ALU.mult), r=[("ps", byc), ("sb", s)], w=[("e2", s)])
                t.op("pool", lambda: nc.gpsimd.tensor_tensor(out=mergedT[:, j, cs], in0=e1[s][:, :], in1=e2[s][:, :],
                                                             op=ALU.add), r=[("e1", s), ("e2", s)], w=[("mg", tt)])
                if j == 0 and tt == 1 and ln_pending:
                    ln_pending.pop(0)()
                    t.label = "E"
                    self.bank_pool = list(range(8))
            if j >= 4:
                m_block(j - 4)
        if self.stop_after == "pE":
            return self.finish_dbg([(mergedT, 8 * 2048, BF16)])

        t.label = "M"
        hT = self.at(oD, [128, 16, 1024], BF16)
        bufA = self.at(oU + 32768, [128, 8, 1024], BF16)
        bufB = self.at(oU + 49152, [128, 8, 1024], BF16)
        xT = self.at(oB, [128, 8, 1024], F32)

        sq = self.at(oD + 49152, [128, 8, 512], BF16)
        lnv = self.at(oD + 57344, [128, 512], F32)
        rstdn = [self.at(oD + 59392 + i * 2048, [128, 512], F32) for i in range(2)]
        rstd = rstdn[0]
        ptc = [self.at(oB + 32768 + i * 1024, [128, 512], BF16) for i in range(4)]
        rl = [self.at(oB + 36864 + i * 2048, [128, 512], F32) for i in range(2)]
        rds = rl
        grow = self.at(oD + 71680, [128, 1024], F32)
        _selfsync(t)
        ri = 0
        TS = [slice(0, 512), slice(512, 1024)]
        for half in range(2):
            h0 = half * 1024
            t.label = "F0"
            f0_rows = lambda s: self.xh[HALO + h0 + s * 128: HALO + h0 + (s + 1) * 128, :]
            f0_keys = lambda s, hf: [("xTl", s // 4, hf), ("xT", s // 4)]
            self.load_transpose(f0_rows, xs1, 4, xT, f0_keys, "xb", pre=(pre_x0 if half == 0 else pre_x1))

            def f0_second():
                t.label = "F0"
                self.load_transpose(f0_rows, xs1, 4, xT, f0_keys, "xb", s_off=4)
                t.label = "F1"

            def proj_res(src, skeys, nk=8, tile_outer=True, mid_hook=None):
                per_blk = WBLK // (nk * 128)
                if tile_outer:
                    blks = self.w_take(8 // per_blk)
                    order = [(j, tt) for tt in range(2) for j in range(8)]
                else:
                    blks = None
                    order = [(j, tt) for j in range(8) for tt in range(2)]
                cur = None
                for (j, tt) in order:
                    if mid_hook is not None and tt == 1 and j == 0:
                        mid_hook()
                    if tile_outer:
                        wt_, wk_ = blks[j // per_blk]
                    else:
                        if j % per_blk == 0 and tt == 0:
                            cur = self.w_next()
                        wt_, wk_ = cur
                    o = (j % per_blk) * nk * 128
                    cs = TS[tt]
                    b = self.bank()
                    self.mm_group(b, 512, [(wt_[:, o + k * 128:o + (k + 1) * 128], src(k, cs)) for k in range(nk)],
                                  [wk_, skeys(tt)])
                    t.op("dve", lambda j=j, cs=cs, b=b: nc.vector.tensor_tensor(
                        out=xT[:, j, cs], in0=self.ps[b][:, :], in1=xT[:, j, cs], op=ALU.add),
                        r=[("ps", b), ("xT", tt), ("xTl", tt, 0), ("xTl", tt, 1)], w=[("xT", tt)])

            t.label = "F1"
            proj_res(lambda k, cs: mergedT[:, k, h0 + cs.start:h0 + cs.stop], lambda tt: ("mg", 0), mid_hook=f0_second)
            if self.stop_after == "F1":
                return self.finish_dbg([(xT, 8 * 1024, F32)])
            t.label = "F2"
            nb_ = {}
            for tt in range(2):
                cs = TS[tt]
                nb_[tt] = self.norm_split(xT[:, :, cs], lambda k, cs=cs: xT[:, k, cs], [("xT", tt)], C_GCROSS,
                                          lambda k, cs=cs: bufA[:, k, cs], [("bufA", tt)], 512, sq, lnv,
                                          rstdn[tt], ("rstdn", tt), "f2")
            nb_[0][0]()
            t.label = "F3"
            blks = self.w_take(2)
            for tt in range(2):
                cs = TS[tt]
                for j in range(8):
                    wt, wkey = blks[j // 4]
                    o = (j % 4) * 1024
                    b = self.bank()
                    self.mm_group(b, 512, [(wt[:, o + k * 128:o + (k + 1) * 128], bufA[:, k, cs]) for k in range(8)],
                                  [wkey, ("bufA", tt)])
                    if j == 0:
                        pend_ev = []
                    pend_ev.append(lambda j=j, cs=cs, b=b, tt=tt: t.op(
                        "dve", lambda: nc.vector.tensor_tensor(
                            out=bufB[:, j, cs], in0=self.ps[b][:, :], in1=rstdn[tt][:, :], op=ALU.mult),
                        r=[("ps", b), ("rstdn", tt)], w=[("bufB", tt)]))
                    if j == 2:
                        nb_[tt][1]()
                        if tt == 0:
                            nb_[1][0]()
                    if j >= 2:
                        while pend_ev:
                            pend_ev.pop(0)()
            t.label = "F4"
            items = [(tt, hc) for tt in range(2) for hc in range(4)]
            pend = {}
            pi = 0
            for it in range(len(items) + 1):
                if it < len(items):
                    tt, hc = items[it]
                    cs = TS[tt]
                    pp = []
                    for mc in range(2):
                        b = self.bank()
                        self.mm_group(b, 512, [(ckT[:, 2 * hc + e, mc * 128:(mc + 1) * 128], bufB[:, 2 * hc + e, cs])
                                               for e in range(2)], ["ckT", ("bufB", tt)])
                        p_ = ptc[pi % 4]
                        pk = ("ptc", pi % 4)
                        pi += 1
                        t.op("act", lambda p_=p_, b=b: nc.scalar.activation(out=p_[:, :], in_=self.ps[b][:, :],
                                                                            func=AF.Exp, scale=1.0 / 16.0),
                             r=[("ps", b)], w=[pk])
                        pp.append((p_, pk))
                    pend[it] = (tt, hc, pp)
                if it >= 1:
                    tt, hc, pp = pend.pop(it - 1)
                    cs = TS[tt]
                    bd = self.bank()
                    self.mm_group(bd, 512, [(self.ones_bf[:, :], pp[mc][0][:, :]) for mc in range(2)],
                                  [pp[0][1], pp[1][1], "ones"])
                    rdk = ("rd", it % 2)
                    rd_ = rds[it % 2]
                    t.op("act", lambda bd=bd, rd_=rd_: nc.scalar.activation(out=rd_[:, :], in_=self.ps[bd][:, :],
                                                                            func=AF.Ln), r=[("ps", bd)], w=[rdk])
                    t.op("act", lambda rd_=rd_: nc.scalar.activation(out=rd_[:, :], in_=rd_[:, :], func=AF.Exp,
                                                                     scale=-1.0), r=[rdk], w=[rdk])
                    for e in range(2):
                        bo = self.bank()
                        ec = 2 * hc + e
                        self.mm_group(bo, 512, [(cV[:, mc, ec * 128:(ec + 1) * 128], pp[mc][0][:, :])
                                                for mc in range(2)], ["cV", pp[0][1], pp[1][1]])
                        t.op("dve", lambda ec=ec, cs=cs, bo=bo, rd_=rd_: nc.vector.tensor_tensor(
                            out=bufA[:, ec, cs], in0=self.ps[bo][:, :], in1=rd_[:, :], op=ALU.mult),
                            r=[("ps", bo), rdk], w=[("bufA", tt)])
            t.label = "F5"
            proj_res(lambda k, cs: bufA[:, k, cs], lambda tt: ("bufA", tt))
            if self.stop_after == "F5":
                return self.finish_dbg([(xT, 8 * 1024, F32)])
            t.label = "G1"
            nb_ = {}
            for tt in range(2):
                cs = TS[tt]
                nb_[tt] = self.norm_split(xT[:, :, cs], lambda k, cs=cs: xT[:, k, cs], [("xT", tt)], C_GMLP,
                                          lambda k, cs=cs: bufB[:, k, cs], [("bufB", tt)], 512, sq, lnv,
                                          rstdn[tt], ("rstdn", tt), "g1")
            nb_[0][0]()
            r_i = 0
            for fh in range(2):
                t.label = "G2"
                for fb in range(4):
                    wt, wkey = self.w_next()
                    for tt in range(2):
                        cs = TS[tt]
                        for fc in range(4):
                            f = fb * 4 + fc
                            b = self.bank()
                            self.mm_group(b, 512, [(wt[:, fc * 1024 + k * 128: fc * 1024 + (k + 1) * 128],
                                                    bufB[:, k, cs]) for k in range(8)], [wkey, ("bufB", tt)])
                            first_ = (fh == 0 and fb == 0)
                            if fc == 0:
                                pend_ev = []

                            def _ev(b=b, tt=tt, f=f, cs=cs):
                                nonlocal r_i
                                r_ = rl[r_i % 2]
                                rk_ = ("rl", r_i % 2)
                                r_i += 1
                                t.op("dve", lambda: nc.vector.scalar_tensor_tensor(
                                    out=r_[:, :], in0=self.ps[b][:, :], scalar=0.0, in1=rstdn[tt][:, :],
                                    op0=ALU.max, op1=ALU.mult),
                                    r=[("ps", b), ("rstdn", tt)], w=[rk_])
                                t.op("act", lambda: nc.scalar.activation(
                                    out=hT[:, f, cs], in_=r_[:, :], func=AF.Square), r=[rk_], w=[("hT", tt)])
                            pend_ev.append(_ev)
                            if first_ and fc == 2:
                                nb_[tt][1]()
                                if tt == 0:
                                    nb_[1][0]()
                            if (not first_) or fc >= 2:
                                while pend_ev:
                                    pend_ev.pop(0)()
                t.label = "G3"
                proj_res(lambda k, cs: hT[:, k, cs], lambda tt: ("hT", tt), nk=16, tile_outer=False)
            if self.stop_after == "G3":
                return self.finish_dbg([(xT, 8 * 1024, F32)])
            if half == 0:
                t.dma("sp", lambda: nc.sync.dma_start(out=grow[:, :], in_=self.growd), stream="c3", w=["grow"])
                pre_x1 = [self.lt_issue(self.xh[HALO + 1024 + s_ * 128: HALO + 1024 + (s_ + 1) * 128, :], xs1, "xb")
                          for s_ in range(4)]
            t.label = "H"
            ident = self.consts[:, C_ID:C_ID + 128]
            sqh = [self.at(oU + 32768, [128, 8, 512], BF16), self.at(oU + 49152, [128, 8, 512], BF16)]
            ogs = [self.at(oU + 32768 + 8192 + i * 4096, [128, 1024], F32) for i in range(2)] + \
                  [self.at(oU + 49152 + 8192 + i * 4096, [128, 1024], F32) for i in range(2)]
            ogkeys = [[("og", i), ("bufA" if i < 2 else "bufB", 0), ("bufA" if i < 2 else "bufB", 1)]
                      for i in range(4)]
            self._og_i = 0
            for tt in range(2):
                t.op("act", lambda tt=tt: nc.scalar.activation(out=sqh[tt][:, :, :], in_=xT[:, :, TS[tt]],
                                                               func=AF.Square),
                     r=[("xT", tt)], w=[("sqh", tt)])
            for tt in range(2):
                cs = TS[tt]
                rs_ = rstdn[tt]
                rk = ("rstdn", tt)
                sqx = sqh[tt]

                def h_stats():
                    bs = self.bank()
                    self.mm_group(bs, 512, [(self.ones_bf[:, :], sqx[:, k, :]) for k in range(8)], [("sqh", tt)])
                    t.op("act", lambda: nc.scalar.activation(out=lnv[:, :], in_=self.ps[bs][:, :], func=AF.Ln,
                                                             bias=self.cc(C_EPS), scale=1.0 / D),
                         r=[("ps", bs)], w=[("lnv", "h")])
                    t.op("act", lambda: nc.scalar.activation(out=rs_[:, :], in_=lnv[:, :], func=AF.Exp, scale=-0.5),
                         r=[("lnv", "h")], w=[rk])

                def x_tr(s, tt=tt, cs=cs):
                    bl = []
                    for hf in range(2):
                        b = self.bank()
                        for kk in range(4):
                            k = hf * 4 + kk
                            c0 = cs.start + s * 128
                            t.op("pe", lambda k=k, kk=kk, b=b, c0=c0: nc.tensor.transpose(
                                self.ps[b][:, kk * 128:(kk + 1) * 128], xT[:, k, c0:c0 + 128], ident),
                                r=[("xT", tt)], w=[("ps", b)])
                        bl.append(b)
                    return bl

                def x_ev(s, bl, tt=tt):
                    osl = self._og_i % 4
                    self._og_i += 1
                    og = ogs[osl]
                    okeys = ogkeys[osl]
                    for hf in range(2):
                        b = bl[hf]
                        t.op("dve", lambda b=b, hf=hf: nc.vector.scalar_tensor_tensor(
                            out=og[:, hf * 512:(hf + 1) * 512], in0=self.ps[b][:, :],
                            scalar=self.rcols[:, tt * 4 + s:tt * 4 + s + 1],
                            in1=grow[:, hf * 512:(hf + 1) * 512], op0=ALU.mult, op1=ALU.mult),
                            r=[("ps", b), ("rcols", tt), "grow"], w=okeys)
                    r0 = h0 + tt * 512 + s * 128
                    t.dma("sp", lambda: nc.sync.dma_start(out=self.outd[r0:r0 + 128, :], in_=og[:, :]),
                          stream=f"o{osl}", r=okeys, w=[("outd", r0)])

                pend = [(s, x_tr(s)) for s in range(2)]
                h_stats()
                br = self.bank()
                for s in range(4):
                    t.op("pe", lambda s=s: nc.tensor.transpose(self.ps[br][:, s * 128:(s + 1) * 128],
                                                               rs_[:, s * 128:(s + 1) * 128], ident),
                         r=[rk], w=[("ps", br)])
                t.op("act", lambda: nc.scalar.copy(out=self.rcols[:, tt * 4:tt * 4 + 4],
                                                   in_=self.ps[br][:, 0:512:128]),
                     r=[("ps", br)], w=[("rcols", tt)])
                for s in range(2, 4):
                    s_, bl_ = pend.pop(0)
                    x_ev(s_, bl_)
                    pend.append((s, x_tr(s)))
                while pend:
                    s_, bl_ = pend.pop(0)
                    x_ev(s_, bl_)
        t.barrier(waiters=("sp",), skip_prefix="dma_w")
        return nc

    def finish_dbg(self, items):
        nc, t = self.nc, self.t
        t.barrier()
        stage = self.at(self.oD + 73728, [128, 512], F32)
        off = 0
        for (tens, n, dt) in items:
            for c0 in range(0, n, 512):
                m = min(512, n - c0)
                src = self._flat(tens, c0, m)
                t.op("act", lambda: nc.scalar.copy(out=stage[:, 0:m], in_=src), w=["stg"])
                t.dma("sp", lambda: nc.sync.dma_start(out=self.dbgd[:, off + c0: off + c0 + m], in_=stage[:, 0:m]),
                      stream="dbg", r=["stg"], w=[("dbgo", off + c0)])
                t.barrier()
            off += n
        t.barrier(waiters=("sp",))
        return nc

    def _flat(self, tens, c0, m):
        shp = list(tens.shape)
        if len(shp) == 2:
            return tens[:, c0:c0 + m]
        inner = shp[2]
        if inner >= m:
            assert inner % m == 0
            return tens[:, c0 // inner, (c0 % inner):(c0 % inner) + m]
        assert c0 % inner == 0 and m % inner == 0
        return tens[:, c0 // inner:(c0 + m) // inner, :]


def _chunk(W, c0, ncols=128, cols=None):
    if cols is None:
        cols = np.arange(c0, c0 + ncols)
    sub = W[:, cols]
    K = sub.shape[0]
    return sub.reshape(K // 128, 128, len(cols)).transpose(1, 0, 2).reshape(128, -1)


def _pack_weights(inp):
    w_in = inp["w_in"][0]
    blocks = np.zeros((NWB, 128, WBLK), np.float32)

    def put(bi, off, arr):
        blocks[bi, :, off:off + arr.shape[1]] = arr

    bi = WB_A
    for h4 in range(4):
        for g in range(3):
            head = g * 4 + h4
            put(bi, 0, _chunk(w_in, 0, cols=head * 128 + PERM))
            put(bi, 1024, _chunk(w_in, 0, cols=1536 + head * 128 + PERM))
            put(bi, 2048, _chunk(w_in, 3072 + head * 128))
            bi += 1
    for i in range(3):
        for jj in range(2):
            j = i * 2 + jj
            put(WB_C + i, jj * 2048, _chunk(w_in, 4608 + j * 128))
            put(WB_C + i, jj * 2048 + 1024, _chunk(w_in, 4608 + 768 + j * 128))
    wap = inp["w_attn_proj"][0]
    wcp = inp["w_conv_proj"][0]
    for j in range(8):
        put(WB_E + j, 0, _chunk(w_in, 6144 + j * 128))
        put(WB_E + j, 1024, _chunk(wap, j * 128))
        put(WB_E + j, 1536, _chunk(w_in, 7168 + j * 128))
        put(WB_E + j, 2560, _chunk(wcp, j * 128))
    wckv = inp["w_ckv"][0]
    for j in range(8):
        put(WB_M + j // 4, (j % 4) * 1024, _chunk(wckv, j * 128))
    for i in range(2):
        put(WB_M + 2 + i, 0, _chunk(wckv, 1024 + i * 512, ncols=512))
    for (wb, name) in ((WB_OUT, "w_out"), (WB_CQ, "w_cq"), (WB_CO, "w_co")):
        W = inp[name][0]
        for j in range(8):
            put(wb + j // 4, (j % 4) * 1024, _chunk(W, j * 128))
    wup = inp["w_up"][0]
    wdn = inp["w_down"][0]
    bi = WB_UP
    for fh in range(2):
        for fb in range(4):
            for fc in range(4):
                f = fh * 16 + fb * 4 + fc
                put(bi, fc * 1024, _chunk(wup, f * 128))
            bi += 1
        for jb in range(4):
            for jc in range(2):
                j = jb * 2 + jc
                put(bi, jc * 2048, _chunk(wdn[fh * 2048:(fh + 1) * 2048], j * 128))
            bi += 1
    assert bi == NWB
    return blocks


def _vec8(v):
    return v.reshape(-1, 128).T


def _consts(inp, flag):
    c = np.zeros((128, CW), np.float32)
    c[:, C_ID:C_ID + 128] = np.eye(128, dtype=np.float32)
    kk = np.arange(128)[:, None]
    qq = np.arange(128)[None, :]
    NEG = np.float32(-30000.0)
    prev = np.where(kk >= qq, np.float32(0.0), NEG).astype(np.float32)
    cur = np.where(kk <= qq, np.float32(0.0), NEG).astype(np.float32)
    c[:, C_MASK:C_MASK + 128] = prev
    c[:, C_MASK + 128:C_MASK + 256] = cur
    c[:, C_MASK0:C_MASK0 + 128] = prev if flag else NEG
    c[:, C_MASK0 + 128:C_MASK0 + 256] = cur
    c[:, C_EPS] = EPS
    c[:, C_GMIX:C_GMIX + 8] = _vec8(inp["g_mix"][0])
    c[:, C_GCROSS:C_GCROSS + 8] = _vec8(inp["g_cross"][0])
    c[:, C_GMEM:C_GMEM + 8] = _vec8(inp["g_mem"][0])
    c[:, C_GMLP:C_GMLP + 8] = _vec8(inp["g_mlp"][0])
    c[:, C_GFIN:C_GFIN + 8] = _vec8(inp["g_final"])
    c[:, C_BGA:C_BGA + 8] = _vec8(inp["b_gate"][0][:1024])
    c[:, C_BGB:C_BGB + 8] = _vec8(inp["b_gate"][0][1024:])
    c[:, C_CONVB:C_CONVB + 6] = _vec8(inp["conv_b"][0])
    c[:, C_LNG:C_LNG + 6] = _vec8(inp["conv_ln_g"][0])
    c[:, C_LNB:C_LNB + 6] = _vec8(inp["conv_ln_b"][0])
    cw = inp["conv_w"][0]
    for j in range(6):
        c[:, C_CONVW + j * 31:C_CONVW + (j + 1) * 31] = cw[:, j * 128:(j + 1) * 128].T
    return c


def _rope_tables(pos0):
    pos = (pos0 + np.arange(4096)).astype(np.float32)
    inv = (np.float32(500000.0) ** (-np.arange(0, 32, 2, dtype=np.float32) / np.float32(32))).astype(np.float32)
    ang = (pos[None, :] * inv[:, None]).astype(np.float32)
    cs, sn = np.cos(ang).astype(np.float32), np.sin(ang).astype(np.float32)
    C = np.ones((64, 4096), np.float32)
    S = np.zeros((64, 4096), np.float32)
    C[0:16] = cs
    C[32:48] = cs
    S[0:16] = sn
    S[32:48] = -sn
    return C, S


_CACHE = {}


def _get_nc(stop_after=None, dbg=False):
    key = (stop_after, dbg)
    if key not in _CACHE:
        _CACHE[key] = Kern(stop_after, dbg).build()
    return _CACHE[key]


def _in_maps(inputs):
    inp = {k: np.asarray(v, dtype=np.float32) for k, v in inputs.items()}
    wp = _pack_weights(inp)
    x, mem = inp["x"], inp["mem"]
    maps = []
    for c in range(NCORES):
        b, q = c // 4, c % 4
        main = x[b, q * T:(q + 1) * T]
        halo = x[b, (q - 1) * T:q * T] if q > 0 else np.zeros((HALO, D), np.float32)
        C, S = _rope_tables(q * T - HALO)
        maps.append({
            "xh": np.ascontiguousarray(np.concatenate([halo, main], axis=0)),
            "memb": np.ascontiguousarray(mem[b]),
            "consts": _consts(inp, q > 0),
            "ropeC": C, "ropeS": S,
            "wpack": wp,
            "grow": np.ascontiguousarray(np.broadcast_to(inp["g_final"][None, :], (128, D))),
        })
    return maps


def kernel(**inputs):
    nc = _get_nc()
    maps = _in_maps(inputs)
    res = run_bass_kernel_spmd(nc, maps, core_ids=list(range(NCORES)))
    out = np.zeros((2, 4 * T, D), np.float32)
    for c in range(NCORES):
        out[c // 4, (c % 4) * T:(c % 4 + 1) * T] = res.results[c]["out"]
    return out
```

```python
import numpy as np
import ml_dtypes
import concourse.bass as bass
import concourse.mybir as mybir
from concourse.bass_utils import run_bass_kernel_spmd

F32 = mybir.dt.float32
BF16 = mybir.dt.bfloat16
AF = mybir.ActivationFunctionType
ALU = mybir.AluOpType

NCORES = 8
T = 2048
HALO = 2048
D = 1024
EPS = 1e-6
WBLK = 4096
PERM = np.array(list(range(0, 16)) + list(range(32, 48)) + list(range(16, 32)) + list(range(48, 128)))
DIL = (1, 4, 16)

C_ID = 0
C_MASK = 128
C_MASK0 = 384
C_EPS = 640
C_GMIX = 641
C_GCROSS = 649
C_GMEM = 657
C_GMLP = 665
C_GFIN = 673
C_BGA = 681
C_BGB = 689
C_CONVB = 697
C_LNG = 703
C_LNB = 709
C_CONVW = 715
CW = 715 + 186

WB_A = 0
WB_C = 12
WB_E = 15
WB_M = 23
WB_OUT = 27
WB_CQ = 29
WB_CO = 31
WB_UP = 33
WB_DN = 41
NWB = 49


class Trk:
    def __init__(self, nc):
        self.nc = nc
        self.eng = {"pe": nc.tensor, "act": nc.scalar, "dve": nc.vector, "pool": nc.gpsimd, "sp": nc.sync}
        self.sems = {}
        self.cnt = {}
        for e in ("pe", "act", "dve", "pool"):
            self.cnt[e] = 0
        self.seen = {e: {} for e in self.eng}
        self.last_w = {}
        self.readers = {}
        self.dma_cnt = {}
        self.label = ""
        self.log = {e: [] for e in self.eng}

    def _sem(self, name):
        if self.sems.get(name) is None:
            cm = self.nc.semaphore(f"s_{name}")
            self.sems[name] = cm.__enter__()
        return self.sems[name]

    def _deps(self, eng, r, w):
        deps = {}

        def add(tok):
            if tok is None:
                return
            s, v = tok
            if deps.get(s, 0) < v:
                deps[s] = v

        for k in r:
            add(self.last_w.get(k))
        for k in w:
            add(self.last_w.get(k))
            for tok in self.readers.get(k, ()):
                add(tok)
        need = []
        for s, v in deps.items():
            if s == "pe" and eng == "pe":
                continue
            if self.seen[eng].get(s, 0) < v:
                need.append((s, v))
        return need

    def _emit(self, eng, fn, need):
        e = self.eng[eng]
        for s, v in need[:-1]:
            e.wait_ge(self._sem(s), v)
        ins = fn()
        if need:
            s, v = need[-1]
            ins._wait_ge(self._sem(s), v)
        for s, v in need:
            self.seen[eng][s] = v
        return ins

    def _record(self, tok, r, w):
        for k in w:
            self.last_w[k] = tok
            self.readers[k] = []
        for k in r:
            self.readers.setdefault(k, []).append(tok)

    def op(self, eng, fn, r=(), w=()):
        need = self._deps(eng, r, w)
        ins = self._emit(eng, fn, need)
        self.cnt[eng] += 1
        self.log[eng].append(self.label)
        ins.then_inc(self._sem(eng), 1)
        self._record((eng, self.cnt[eng]), r, w)

    def dma(self, q, fn, stream, r=(), w=()):
        need = self._deps(q, r, w)
        ins = self._emit(q, fn, need)
        s = "dma_" + stream
        self.dma_cnt[s] = self.dma_cnt.get(s, 0) + 16
        ins.then_inc(self._sem(s), 16)
        self._record((s, self.dma_cnt[s]), r, w)

    def barrier(self, waiters=("pe", "act", "dve", "sp"), skip_prefix="dma_w"):
        toks = [(e, self.cnt[e]) for e in ("pe", "act", "dve", "pool") if self.cnt[e] > 0]
        toks += [(s, v) for s, v in self.dma_cnt.items() if not s.startswith(skip_prefix)]
        for wtr in waiters:
            for s, v in toks:
                if self.seen[wtr].get(s, 0) < v:
                    self.eng[wtr].wait_ge(self._sem(s), v)
                    self.seen[wtr][s] = v


def _selfsync(t, engines=("act", "dve", "pool")):
    for e in engines:
        v = t.cnt[e]
        if v > 0 and t.seen[e].get(e, 0) < v:
            t.eng[e].wait_ge(t._sem(e), v)
            t.seen[e][e] = v


class Kern:
    def __init__(self, stop_after=None, dbg=False):
        self.stop_after = stop_after
        nc = self.nc = bass.Bass("TRN2", target_bir_lowering=False)
        self.xh = nc.dram_tensor("xh", [HALO + T, D], F32, kind="ExternalInput").ap()
        self.memd = nc.dram_tensor("memb", [256, D], F32, kind="ExternalInput").ap()
        self.constd = nc.dram_tensor("consts", [128, CW], F32, kind="ExternalInput").ap()
        self.ropeCd = nc.dram_tensor("ropeC", [64, 4096], F32, kind="ExternalInput").ap()
        self.ropeSd = nc.dram_tensor("ropeS", [64, 4096], F32, kind="ExternalInput").ap()
        self.wpack = nc.dram_tensor("wpack", [NWB, 128, WBLK], F32, kind="ExternalInput").ap()
        self.growd = nc.dram_tensor("grow", [128, D], F32, kind="ExternalInput").ap()
        self.outd = nc.dram_tensor("out", [T, D], F32, kind="ExternalOutput").ap()
        self.dbg = dbg
        if dbg:
            self.dbgd = nc.dram_tensor("dbg", [128, 8 * 4096], F32, kind="ExternalOutput").ap()
        self.t = Trk(nc)
        self.consts = nc.alloc_sbuf_tensor("consts_sb", [128, CW], F32)
        self.ones_bf = nc.alloc_sbuf_tensor("ones_bf", [128, 128], BF16)
        self.ident_bf = nc.alloc_sbuf_tensor("ident_bf", [128, 128], BF16)
        self.mask_bf = nc.alloc_sbuf_tensor("mask_bf", [128, 256], BF16)
        self.mask0_bf = nc.alloc_sbuf_tensor("mask0_bf", [128, 256], BF16)
        self.rcols = nc.alloc_sbuf_tensor("rcols", [128, 8], F32)
        base = nc.SBUF_PARTITION_SIZE_BYTES - nc.sbuf_bytes_remaining
        base = (base + 63) // 64 * 64
        self.arena_total = 24576 + 65536 + 16384 + 24576 + 73728 + 2048
        nc.alloc_sbuf_tensor("arena", [128, self.arena_total + 64], mybir.dt.uint8)
        self.oW = base
        self.oU = self.oW + 24576
        self.oB = self.oU + 65536
        self.oC = self.oB + 16384
        self.oD = self.oC + 24576
        self._n = 0
        self.ps = [nc.alloc_psum_tensor(f"psb{i}", [128, 512], F32) for i in range(8)]
        self._bank = 0
        self.bank_pool = list(range(8))
        self.wslots = [self.at(self.oW + i * 8192, [128, WBLK], BF16) for i in range(3)]
        self.wq = []
        self.wq_issued = 0
        self.wq_pos = 0

    def at(self, off, shape, dt):
        self._n += 1
        assert off % 32 == 0, off
        return self.nc.alloc_sbuf_tensor_at(f"m{self._n}", shape, dt, offset=off)

    def bank(self):
        pool = self.bank_pool
        b = pool[self._bank % len(pool)]
        self._bank += 1
        return b

    def cc(self, col, n=1):
        return self.consts[:, col:col + n]

    def w_plan(self, blocks):
        self.wq.extend(blocks)

    def _w_issue(self):
        i = self.wq_issued
        blk = self.wq[i]
        slot = i % 3
        dst = self.wslots[slot]
        self.t.dma("pool", lambda: self.nc.gpsimd.dma_start(out=dst[:, :], in_=self.wpack[blk]),
                   stream=f"w{slot}", w=[("w", slot)])
        self.wq_issued += 1

    def w_take(self, n):
        first = self.wq_pos
        while self.wq_issued < min(len(self.wq), first + 3):
            self._w_issue()
        out = []
        for i in range(n):
            slot = (first + i) % 3
            out.append((self.wslots[slot], ("w", slot)))
        self.wq_pos += n
        return out

    def w_next(self):
        return self.w_take(1)[0]

    def mm_group(self, b, ncols, pairs, r_keys, col0=0):
        n = len(pairs)
        out = self.ps[b][:, col0:col0 + ncols]
        for i, (lh, rh) in enumerate(pairs):
            self.t.op("pe", lambda lh=lh, rh=rh, i=i: self.nc.tensor.matmul(
                out, lhsT=lh, rhs=rh, start=(i == 0), stop=(i == n - 1)),
                r=r_keys if i == 0 else (), w=[("ps", b)])
        self.t._record(("pe", self.t.cnt["pe"]), r_keys, ())

    def norm(self, xall, xk, xkeys, gcol, out_fn, okeys, ntok, sq, lnv, rstd, tag, split=False, pool_sq=None):
        nc, t = self.nc, self.t

        def part_sq():
            if pool_sq is None:
                t.op("act", lambda: nc.scalar.activation(out=sq[:, :, 0:ntok], in_=xall, func=AF.Square),
                     r=xkeys, w=[("sq", tag)])
            else:
                xlo, xhi = pool_sq
                t.op("act", lambda: nc.scalar.activation(out=sq[:, 0:4, 0:ntok], in_=xlo, func=AF.Square),
                     r=xkeys, w=[("sq", tag)])
                t.op("pool", lambda: nc.gpsimd.tensor_tensor(out=sq[:, 4:8, 0:ntok], in0=xhi, in1=xhi,
                                                             op=ALU.mult),
                     r=xkeys, w=[("sq", tag, 1)])

        def part_rest():
            self._norm_rest(xk, xkeys, gcol, out_fn, okeys, ntok, sq, lnv, rstd, tag)
        if split:
            return part_sq, part_rest
        part_sq()
        part_rest()

    def _norm_rest(self, xk, xkeys, gcol, out_fn, okeys, ntok, sq, lnv, rstd, tag):
        nc, t = self.nc, self.t
        b = self.bank()
        self.mm_group(b, ntok, [(self.ones_bf[:, :], sq[:, k, 0:ntok]) for k in range(8)],
                      [("sq", tag), ("sq", tag, 1)])
        t.op("act", lambda: nc.scalar.activation(out=lnv[:, 0:ntok], in_=self.ps[b][:, 0:ntok], func=AF.Ln,
                                                 bias=self.cc(C_EPS), scale=1.0 / D),
             r=[("ps", b)], w=[("lnv", tag)])
        t.op("act", lambda: nc.scalar.activation(out=rstd[:, 0:ntok], in_=lnv[:, 0:ntok], func=AF.Exp, scale=-0.5),
             r=[("lnv", tag)], w=[("rstd", tag)])
        for k in range(8):
            t.op("dve", lambda k=k: nc.vector.scalar_tensor_tensor(
                out=out_fn(k), in0=xk(k), scalar=self.cc(gcol + k), in1=rstd[:, 0:ntok],
                op0=ALU.mult, op1=ALU.mult),
                r=xkeys + [("rstd", tag)], w=okeys)

    def norm_split(self, xall, xk, xkeys, gcol, ug_fn, ugkeys, ntok, sq, lnv, rstd_out, rkey, tag):
        nc, t = self.nc, self.t
        for k in range(8):
            if k < 4:
                t.op("act", lambda k=k: nc.scalar.activation(out=ug_fn(k), in_=xk(k), func=AF.Copy,
                                                             scale=self.cc(gcol + k)),
                     r=xkeys + ["consts"], w=ugkeys)
            else:
                t.op("dve", lambda k=k: nc.vector.tensor_scalar(out=ug_fn(k), in0=xk(k), scalar1=self.cc(gcol + k),
                                                                scalar2=None, op0=ALU.mult),
                     r=xkeys + ["consts"], w=ugkeys)
        def part_sq():
            t.op("act", lambda: nc.scalar.activation(out=sq[:, :, 0:ntok], in_=xall, func=AF.Square),
                 r=xkeys, w=[("sq", tag)])

        def part_b():
            b = self.bank()
            self.mm_group(b, ntok, [(self.ones_bf[:, :], sq[:, k, 0:ntok]) for k in range(8)], [("sq", tag)])
            t.op("act", lambda: nc.scalar.activation(out=lnv[:, 0:ntok], in_=self.ps[b][:, 0:ntok], func=AF.Ln,
                                                     bias=self.cc(C_EPS), scale=1.0 / D),
                 r=[("ps", b)], w=[("lnv", tag)])
            t.op("act", lambda: nc.scalar.activation(out=rstd_out[:, 0:ntok], in_=lnv[:, 0:ntok], func=AF.Exp,
                                                     scale=-0.5),
                 r=[("lnv", tag)], w=[rkey])
        return part_sq, part_b

    def io_alloc(self, nslots, exclude=()):
        while True:
            slot = self._xs_i % nslots
            self._xs_i += 1
            if slot not in exclude:
                return slot

    def lt_issue(self, rows, xs_slots, stream, exclude=()):
        nc, t = self.nc, self.t
        slot = self.io_alloc(len(xs_slots), exclude)
        xs = xs_slots[slot]
        t.dma("sp", lambda: nc.sync.dma_start(out=xs[:, :], in_=rows), stream=f"{stream}{slot}",
              w=[("xs", slot)])
        return slot

    def load_transpose(self, src_rows, xs_slots, nsub, dst, dkey, stream, evac=("act", "dve"), pre=(), s_off=0):
        nc, t = self.nc, self.t
        ident = self.consts[:, C_ID:C_ID + 128]
        pre = list(pre)
        for s_ in range(nsub):
            s = s_ + s_off
            if s_ < len(pre):
                slot = pre[s_]
            else:
                slot = self.lt_issue(src_rows(s), xs_slots, stream, exclude=pre[s_ + 1:])
            xs = xs_slots[slot]
            for hf in range(2):
                b = self.bank()
                for kk in range(4):
                    k = hf * 4 + kk
                    t.op("pe", lambda k=k, kk=kk: nc.tensor.transpose(
                        self.ps[b][:, kk * 128:(kk + 1) * 128], xs[:, k * 128:(k + 1) * 128], ident),
                        r=[("xs", slot)], w=[("ps", b)])
                src = self.ps[b][:, 0:512].rearrange("p (a b) -> p a b", a=4)
                dd = dst[:, hf * 4:hf * 4 + 4, s * 128:(s + 1) * 128]
                if evac[hf] == "act":
                    t.op("act", lambda: nc.scalar.copy(out=dd, in_=src), r=[("ps", b)], w=dkey(s, hf))
                else:
                    t.op("dve", lambda: nc.vector.tensor_copy(out=dd, in_=src), r=[("ps", b)], w=dkey(s, hf))

    def ucols(self, k, a0, n, step=1):
        if a0 < 2048:
            tt, o = self.uTh, a0
        else:
            tt, o = self.uTm, a0 - 2048
        assert o + (n - 1) * step < 2048
        return tt[:, k, o:o + (n - 1) * step + 1:step]

    def ukeys(self, a0, n, step=1):
        return [("uT", i) for i in range(a0 // 512, (a0 + (n - 1) * step) // 512 + 1)]

    def kcols(self, a0, n, step=1):
        if a0 < 2048:
            tt, o = self.kTh, a0
        else:
            tt, o = self.kTm, a0 - 2048
        assert o + (n - 1) * step < 2048
        return tt[:, o:o + (n - 1) * step + 1:step]

    def kkeys(self, a0, n, step=1):
        return [("kT", i) for i in range(a0 // 512, (a0 + (n - 1) * step) // 512 + 1)]

    def build(self):
        nc, t = self.nc, self.t
        self._xs_i = 0
        em = [WB_E + j for j in range(5)]
        for i in range(3):
            em += [WB_M + i, WB_E + 5 + i]
        em += [WB_M + 3]
        plan = list(range(WB_A, WB_A + 12)) + list(range(WB_C, WB_C + 3)) * 2 + em + list(range(WB_OUT, NWB)) * 2
        self.w_plan(plan)
        oU, oB, oC, oD = self.oU, self.oB, self.oC, self.oD
        self.uTh = self.at(oU, [128, 8, 2048], BF16)
        self.uTm = self.at(oU + 32768, [128, 8, 2048], BF16)
        self.attnT = self.at(oB, [128, 4, 2048], BF16)
        self.cT = self.at(oC, [128, 6, 2048], BF16)
        ropeC = self.at(oC, [64, 4096], F32)
        ropeS = self.at(oD + 57344, [64, 4096], F32)

        t.dma("sp", lambda: nc.sync.dma_start(out=self.consts[:, :], in_=self.constd), stream="c0", w=["consts"])
        t.dma("sp", lambda: nc.sync.dma_start(out=ropeC[:, :], in_=self.ropeCd), stream="c1", w=["ropeC"])
        t.op("dve", lambda: nc.vector.memset(self.ones_bf[:, :], 1.0), w=["ones"])
        t.op("act", lambda: nc.scalar.copy(out=self.ident_bf[:, :], in_=self.consts[:, C_ID:C_ID + 128]),
             r=["consts"], w=["identbf"])
        t.op("act", lambda: nc.scalar.copy(out=self.mask_bf[:, :], in_=self.consts[:, C_MASK:C_MASK + 256]),
             r=["consts"], w=["mask"])
        t.op("act", lambda: nc.scalar.copy(out=self.mask0_bf[:, :], in_=self.consts[:, C_MASK0:C_MASK0 + 256]),
             r=["consts"], w=["mask"])
        t.barrier()

        while self.wq_issued < 3:
            self._w_issue()
        t.label = "p0"
        xs0 = [self.at(oD + i * 4096, [128, 1024], F32) for i in range(3)]
        xTts = [self.at(oD + 12288 + i * 16384, [128, 8, 512], F32) for i in range(3)]
        sq = self.at(oD + 61440, [128, 8, 512], BF16)
        lnv = self.at(oD + 69632, [128, 512], F32)
        rstd = self.at(oD + 71680, [128, 512], F32)
        def p0_load(tt):
            xk_ = ("xTt", tt % 3)
            self.load_transpose(lambda s, tt=tt: self.xh[tt * 512 + s * 128: tt * 512 + (s + 1) * 128, :],
                                xs0, 4, xTts[tt % 3], lambda s, hf, xk_=xk_: [xk_ + (hf,)], "x", evac=("act", "act"))

        def p0_norm(tt):
            xTt = xTts[tt % 3]
            xk_ = ("xTt", tt % 3)
            dstT = self.uTh if tt < 4 else self.uTm
            c0 = (tt % 4) * 512
            return self.norm(xTt[:, :, :], lambda k: xTt[:, k, :], [xk_ + (0,), xk_ + (1,)], C_GMIX,
                             lambda k: dstT[:, k, c0:c0 + 512], [("uT", tt)], 512, sq, lnv, rstd, "p0", split=True,
                             pool_sq=(xTt[:, 0:4, :], xTt[:, 4:8, :]))

        p0_load(0)
        parts = p0_norm(0)
        parts[0]()
        for tt in range(8):
            if tt + 1 < 8:
                p0_load(tt + 1)
            parts[1]()
            if tt + 1 < 8:
                parts = p0_norm(tt + 1)
                parts[0]()
        if self.stop_after == "p0":
            return self.finish_dbg([(self.uTm, 8 * 2048, BF16)])
        t.barrier(waiters=("act", "dve", "sp"))

        t.dma("sp", lambda: nc.sync.dma_start(out=ropeS[:, :], in_=self.ropeSd), stream="c2", w=["ropeS"])
        self.qT = self.at(oD, [128, 2048], BF16)
        self.kTh = self.at(oD + 4096, [128, 2048], BF16)
        self.kTm = self.at(oD + 8192, [128, 2048], BF16)
        Vt = self.at(oD + 12288, [128, 32, 128], BF16)
        acc = self.at(oD + 20480, [128, 2, 2048], F32)
        a32 = [self.at(oD + 36864 + i * 2048, [128, 512], F32) for i in range(2)]
        tmp = [self.at(oD + 40960 + i * 2048, [128, 512], F32) for i in range(2)]
        pts = [self.at(oD + 45056 + i * 512, [128, 256], BF16) for i in range(4)]
        pms = [self.at(oD + 47104 + i * 512, [128, 256], BF16) for i in range(4)]
        for i in range(2):
            t.op("dve", lambda i=i: nc.vector.memset(tmp[i][:, :], 0.0), w=[("tmp", i)])
        rope_i = 0
        blk_i = 0
        scale = 1.0 / np.sqrt(128.0)
        for h4 in range(4):
            for g in range(3):
                d = DIL[g]
                halo = 128 * d
                wt, wkey = self.w_next()
                wq_ = lambda k: wt[:, k * 128:(k + 1) * 128]
                wk_ = lambda k: wt[:, 1024 + k * 128:1024 + (k + 1) * 128]
                wv_ = lambda k: wt[:, 2048 + k * 128:2048 + (k + 1) * 128]
                def emit_qk():
                    nonlocal rope_i
                    t.label = "A.qk"
                    jobs = []
                    for tt in range(4):
                        jobs.append(("q", 2048 + tt * 512, 512))
                    a = 2048 - halo
                    while a < 4096:
                        n = min(512, 4096 - a, 512 - (a % 512) if a % 512 else 512)
                        jobs.append(("k", a, n))
                        a += n
                    for (kind, a0, n) in jobs:
                        wsel = wq_ if kind == "q" else wk_
                        b = self.bank()
                        self.mm_group(b, n, [(wsel(k), self.ucols(k, a0, n)) for k in range(8)],
                                      [wkey] + self.ukeys(a0, n))
                        if kind == "q":
                            dT, dc = self.qT, a0 - 2048
                            dkeys = [("qT", (a0 - 2048) // 512)]
                        else:
                            dT, dc = (self.kTh, a0) if a0 < 2048 else (self.kTm, a0 - 2048)
                            dkeys = self.kkeys(a0, n)
                        z = self.ps[b]
                        sl = rope_i % 2
                        rope_i += 1
                        A, Tm = a32[sl], tmp[sl]
                        t.op("act", lambda: nc.scalar.copy(out=dT[64:128, dc:dc + n], in_=z[64:128, 0:n]),
                             r=[("ps", b)], w=[(kk, "hi") for kk in dkeys] + dkeys)
                        t.op("dve", lambda: nc.vector.tensor_tensor(out=A[0:64, 0:n], in0=z[0:64, 0:n],
                                                                    in1=ropeC[0:64, a0:a0 + n], op=ALU.mult),
                             r=[("ps", b), "ropeC"], w=[("a32", sl)])
                        t.op("dve", lambda: nc.vector.tensor_tensor(out=Tm[0:16, 0:n], in0=z[32:48, 0:n],
                                                                    in1=ropeS[32:48, a0:a0 + n], op=ALU.mult),
                             r=[("ps", b), "ropeS"], w=[("tmp", sl)])
                        t.op("dve", lambda: nc.vector.tensor_tensor(out=Tm[32:48, 0:n], in0=z[0:16, 0:n],
                                                                    in1=ropeS[0:16, a0:a0 + n], op=ALU.mult),
                             r=[("ps", b), "ropeS"], w=[("tmp", sl)])
                        t.op("pool", lambda: nc.gpsimd.tensor_tensor(out=dT[0:64, dc:dc + n], in0=A[0:64, 0:n],
                                                                     in1=Tm[0:64, 0:n], op=ALU.add),
                             r=[("a32", sl), ("tmp", sl)], w=dkeys)
                def emit_v():
                    t.label = "A.v"
                    nb = 16 // d
                    vlist = [(r, j) for r in range(d) for j in range(-1, nb)]
                    for v0 in range(0, len(vlist), 4):
                        grp = vlist[v0:v0 + 4]
                        b = self.bank()
                        rk = [wkey]
                        for gi, (r, j) in enumerate(grp):
                            a0 = 2048 + r + 128 * d * j
                            rk = rk + self.ukeys(a0, 128, d)
                            for k in range(8):
                                t.op("pe", lambda k=k, gi=gi, a0=a0: nc.tensor.matmul(
                                    self.ps[b][:, gi * 128:(gi + 1) * 128], lhsT=self.ucols(k, a0, 128, d),
                                    rhs=wv_(k), start=(k == 0), stop=(k == 7)),
                                    r=rk if k == 0 else (), w=[("ps", b)])
                        t._record(("pe", t.cnt["pe"]), rk, ())
                        ng = len(grp)
                        t.op("act", lambda v0=v0, ng=ng, b=b: nc.scalar.copy(
                            out=Vt[:, v0:v0 + ng, :],
                            in_=self.ps[b][:, 0:ng * 128].rearrange("p (a b) -> p a b", a=ng)),
                            r=[("ps", b)], w=[("V", v0 + i) for i in range(ng)])
                nb = 16 // d
                if h4 == 0 and g == 0:
                    emit_v()
                    emit_qk()
                else:
                    emit_qk()
                    emit_v()
                t.label = "A.attn"
                blocks = [(r, j) for r in range(d) for j in range(nb)]
                LAG = 3
                st = {}
                for it in range(len(blocks) + LAG):
                    if it < len(blocks):
                        r, j = blocks[it]
                        a0 = 2048 + r + 128 * d * j
                        ap_ = a0 - 128 * d
                        b = self.bank()
                        qv = self.qT[:, a0 - 2048:a0 - 2048 + 127 * d + 1:d]
                        qk = [("qT", i) for i in range((a0 - 2048) // 512, (a0 - 2048 + 127 * d) // 512 + 1)]
                        mk = self.mask0_bf if j == 0 else self.mask_bf
                        t.op("pe", lambda: nc.tensor.matmul(self.ps[b][:, 0:256], lhsT=self.ident_bf[:, :],
                                                            rhs=mk[:, :], start=True, stop=False),
                             r=["identbf", "mask"], w=[("ps", b)])
                        t.op("pe", lambda: nc.tensor.matmul(self.ps[b][:, 0:128], lhsT=self.kcols(ap_, 128, d),
                                                            rhs=qv, start=False, stop=False),
                             r=qk + self.kkeys(ap_, 128, d), w=[("ps", b)])
                        t.op("pe", lambda: nc.tensor.matmul(self.ps[b][:, 128:256], lhsT=self.kcols(a0, 128, d),
                                                            rhs=qv, start=False, stop=True),
                             r=qk + self.kkeys(a0, 128, d), w=[("ps", b)])
                        ps_i = blk_i % 4
                        blk_i += 1
                        pm = pms[ps_i]
                        t.op("act", lambda: nc.scalar.activation(out=pm[:, :], in_=self.ps[b][:, 0:256],
                                                                 func=AF.Exp, scale=float(scale)),
                             r=[("ps", b)], w=[("pm", ps_i)])
                        st[it] = (b, ps_i, r, j, a0)
                    if it >= LAG:
                        b, ps_i, r, j, a0 = st.pop(it - LAG)
                        pm = pms[ps_i]
                        vprev = r * (nb + 1) + j
                        vcur = vprev + 1
                        o = self.ps[b][:, 256:384]
                        dn = self.ps[b][:, 384:512]
                        t.op("pe", lambda: nc.tensor.matmul(o, lhsT=Vt[:, vprev, :], rhs=pm[:, 0:128],
                                                            start=True, stop=False),
                             r=[("V", vprev), ("pm", ps_i)], w=[("ps", b)])
                        t.op("pe", lambda: nc.tensor.matmul(o, lhsT=Vt[:, vcur, :], rhs=pm[:, 128:256],
                                                            start=False, stop=True),
                             r=[("V", vcur)], w=[("ps", b)])
                        t.op("pe", lambda: nc.tensor.matmul(dn, lhsT=self.ones_bf[:, :], rhs=pm[:, 0:128],
                                                            start=True, stop=False), r=["ones"], w=[("ps", b)])
                        t.op("pe", lambda: nc.tensor.matmul(dn, lhsT=self.ones_bf[:, :], rhs=pm[:, 128:256],
                                                            start=False, stop=True), r=[("pm", ps_i)], w=[("ps", b)])
                        q0 = a0 - 2048
                        dst = acc[:, :, q0:q0 + 127 * d + 1:d]
                        src = self.ps[b][:, 256:512].rearrange("p (a b) -> p a b", a=2)
                        akeys = [("acc", i) for i in range(q0 // 512, (q0 + 127 * d) // 512 + 1)]
                        if g == 0:
                            t.op("act", lambda: nc.scalar.copy(out=dst, in_=src), r=[("ps", b)], w=akeys)
                        else:
                            t.op("dve", lambda: nc.vector.tensor_tensor(out=dst, in0=src, in1=dst, op=ALU.add),
                                 r=[("ps", b)] + akeys, w=akeys)
            t.label = "A.fin"
            for tt in range(4):
                sl_ = slice(tt * 512, (tt + 1) * 512)
                t.op("act", lambda: nc.scalar.activation(out=acc[:, 1, sl_], in_=acc[:, 1, sl_], func=AF.Ln),
                     r=[("acc", tt)], w=[("acc", tt)])
                t.op("act", lambda: nc.scalar.activation(out=acc[:, 1, sl_], in_=acc[:, 1, sl_], func=AF.Exp,
                                                         scale=-1.0),
                     r=[("acc", tt)], w=[("acc", tt)])
                t.op("dve", lambda: nc.vector.tensor_tensor(out=self.attnT[:, h4, sl_], in0=acc[:, 0, sl_],
                                                            in1=acc[:, 1, sl_], op=ALU.mult),
                     r=[("acc", tt)], w=[("attnT", tt)])
        if self.stop_after == "pA":
            return self.finish_dbg([(self.attnT, 4 * 2048, BF16)])
        t.barrier(waiters=("act", "dve", "sp"))

        conv = self.at(oD, [128, 6, 1024], F32)
        cglu = [self.at(oD + 24576 + i * 2176, [128, 1056], BF16) for i in range(2)]
        diags = [self.at(oD + 28928 + i * 7936, [128, 31, 128], BF16) for i in range(2)]
        sg = [self.at(oD + 44800 + i * 2048, [128, 512], F32) for i in range(2)]
        xbs = [self.at(oD + 48896 + i * 1024, [128, 512], BF16) for i in range(4)]
        xsqs = [self.at(oD + 55040 + i * 1024, [128, 512], BF16) for i in range(4)]
        st_pend = []
        self.bank_pool = [0, 1, 2, 3]
        st_i = 0
        mean = self.at(oD + 61184, [128, 512], F32)
        var = self.at(oD + 63232, [128, 512], F32)
        lnv = self.at(oD + 65280, [128, 512], F32)
        rstd = self.at(oD + 67328, [128, 512], F32)
        t1 = [self.at(oD + 69376, [128, 512], F32), self.at(oD + 52992, [128, 512], F32)]
        t2 = [self.at(oD + 71424, [128, 512], F32), self.at(oD + 59136, [128, 512], F32)]
        ln_pending = []
        dg_i = 0
        sg_i = 0
        wst = {}
        for half in range(2):
            base_a = 2048 + half * 1024

            def glu_part(jj, half=half, base_a=base_a):
                nonlocal dg_i, sg_i
                if jj % 2 == 0:
                    wst["w"] = self.w_next()
                wt, wkey = wst["w"]
                wa = lambda k, o=(jj % 2) * 2048: wt[:, o + k * 128:o + (k + 1) * 128]
                wb = lambda k, o=(jj % 2) * 2048 + 1024: wt[:, o + k * 128:o + (k + 1) * 128]
                cg = cglu[jj % 2]
                ckey = ("cglu", jj % 2)
                t.label = "C.diag"
                diag = diags[dg_i % 2]
                dgk = ("diag", dg_i % 2)
                dg_i += 1
                t.op("dve", lambda: nc.vector.tensor_tensor(
                    out=diag[:, :, :], in0=self.ident_bf[:, :].unsqueeze(1).broadcast_to([128, 31, 128]),
                    in1=self.consts[:, C_CONVW + jj * 31:C_CONVW + (jj + 1) * 31].unsqueeze(2).broadcast_to(
                        [128, 31, 128]), op=ALU.mult),
                    r=["identbf", "consts"], w=[dgk])
                t.label = "C.glu"
                for (a0, n, c0) in ((base_a - 32, 32, 0), (base_a, 512, 32), (base_a + 512, 512, 544)):
                    ba = self.bank()
                    self.mm_group(ba, n, [(wa(k), self.ucols(k, a0, n)) for k in range(8)],
                                  [wkey] + self.ukeys(a0, n))
                    bb = self.bank()
                    self.mm_group(bb, n, [(wb(k), self.ucols(k, a0, n)) for k in range(8)],
                                  [wkey] + self.ukeys(a0, n))
                    s_ = sg[sg_i % 2]
                    sk = ("sg", sg_i % 2)
                    sg_i += 1
                    t.op("act", lambda: nc.scalar.activation(out=s_[:, 0:n], in_=self.ps[bb][:, 0:n],
                                                             func=AF.Sigmoid), r=[("ps", bb)], w=[sk])
                    t.op("dve", lambda: nc.vector.tensor_tensor(out=cg[:, c0:c0 + n], in0=self.ps[ba][:, 0:n],
                                                                in1=s_[:, 0:n], op=ALU.mult),
                         r=[("ps", ba), sk], w=[ckey])
                return cg, ckey, diag, dgk

            def conv_part(jj, ctx, mid_hook=None):
                nonlocal st_i
                cg, ckey, diag, dgk = ctx
                t.label = "C.conv"
                for tt in range(2):
                    b = self.bank()
                    self.mm_group(b, 512, [(diag[:, tap, :], cg[:, 2 + tt * 512 + tap: 2 + tt * 512 + tap + 512])
                                           for tap in range(31)], [dgk, ckey])
                    cv_ = conv[:, jj, tt * 512:(tt + 1) * 512]
                    t.op("act", lambda b=b: nc.scalar.activation(
                        out=cv_, in_=self.ps[b][:, :], func=AF.Identity,
                        bias=self.cc(C_CONVB + jj)), r=[("ps", b), "consts"], w=[("conv", tt)])
                    xb_, xq_ = xbs[st_i % 4], xsqs[st_i % 4]
                    kb_, kq_ = ("xb", st_i % 4), ("xsq", st_i % 4)
                    st_i += 1
                    t.op("act", lambda: nc.scalar.copy(out=xb_[:, :], in_=cv_), r=[("conv", tt)], w=[kb_])
                    t.op("act", lambda: nc.scalar.activation(out=xq_[:, :], in_=cv_, func=AF.Square),
                         r=[("conv", tt)], w=[kq_])

                    def _stats(tt=tt, jj=jj, xb_=xb_, xq_=xq_, kb_=kb_, kq_=kq_):
                        t.op("pe", lambda: nc.tensor.matmul(self.ps[4 + tt][:, :], lhsT=self.ones_bf[:, :],
                                                            rhs=xb_[:, :], start=(jj == 0), stop=(jj == 5)),
                             r=[kb_, "ones"], w=[("ps", 4 + tt)])
                        t.op("pe", lambda: nc.tensor.matmul(self.ps[6 + tt][:, :], lhsT=self.ones_bf[:, :],
                                                            rhs=xq_[:, :], start=(jj == 0), stop=(jj == 5)),
                             r=[kq_], w=[("ps", 6 + tt)])
                    st_pend.append(_stats)
                    if len(st_pend) > 2:
                        st_pend.pop(0)()
                    if tt == 0 and mid_hook is not None:
                        mid_hook()
                        t.label = "C.conv"

            ctx_next = glu_part(0)
            for jj in range(6):
                ctx = ctx_next
                if jj + 1 < 6:
                    ctx_next = glu_part(jj + 1)
                hook = None
                if jj == 0 and ln_pending:
                    ln_pending.pop(0)()
                    hook = ln_pending.pop(0)
                conv_part(jj, ctx, hook)
            while st_pend:
                st_pend.pop(0)()
            def ln_stage(tt, half=half):
                t.label = "C.ln"
                if True:
                    b1 = 4 + tt
                    b2 = 6 + tt
                    t.op("dve", lambda: nc.vector.tensor_scalar(out=mean[:, :], in0=self.ps[b1][:, :],
                                                                scalar1=1.0 / 768, scalar2=None, op0=ALU.mult),
                         r=[("ps", b1)], w=["mean"])
                    t.op("dve", lambda: nc.vector.tensor_tensor(out=var[:, :], in0=mean[:, :], in1=mean[:, :],
                                                                op=ALU.mult), r=["mean"], w=["var"])
                    t.op("dve", lambda: nc.vector.scalar_tensor_tensor(out=var[:, :], in0=self.ps[b2][:, :],
                                                                       scalar=1.0 / 768, in1=var[:, :],
                                                                       op0=ALU.mult, op1=ALU.subtract),
                         r=[("ps", b2), "var"], w=["var"])
                    t.op("act", lambda: nc.scalar.activation(out=lnv[:, :], in_=var[:, :], func=AF.Ln,
                                                             bias=self.cc(C_EPS)), r=["var"], w=["lnvc"])
                    t.op("act", lambda: nc.scalar.activation(out=rstd[:, :], in_=lnv[:, :], func=AF.Exp, scale=-0.5),
                         r=["lnvc"], w=["rstdc"])
                    for jj in range(6):
                        a_, b_ = t1[jj % 2], t2[jj % 2]
                        t.op("dve", lambda: nc.vector.tensor_tensor(out=a_[:, :], in0=conv[:, jj, tt * 512:(tt + 1) * 512],
                                                                    in1=mean[:, :], op=ALU.subtract),
                             r=[("conv", tt), "mean"], w=[("t1", jj % 2)])
                        t.op("dve", lambda: nc.vector.scalar_tensor_tensor(out=b_[:, :], in0=a_[:, :],
                                                                           scalar=self.cc(C_LNG + jj), in1=rstd[:, :],
                                                                           op0=ALU.mult, op1=ALU.mult),
                             r=[("t1", jj % 2), "rstdc"], w=[("t2", jj % 2)])
                        c0 = half * 1024 + tt * 512
                        t.op("act", lambda: nc.scalar.activation(out=self.cT[:, jj, c0:c0 + 512], in_=b_[:, :],
                                                                 func=AF.Silu, bias=self.cc(C_LNB + jj)),
                             r=[("t2", jj % 2)], w=[("cT", c0 // 512)])
            ln_pending.append(lambda f=ln_stage: f(0))
            ln_pending.append(lambda f=ln_stage: f(1))
        while len(ln_pending) > 1:
            ln_pending.pop(0)()
        self.bank_pool = [0, 1, 2, 3, 4, 6]
        self._bank = 0
        if self.stop_after == "pC":
            return self.finish_dbg([(self.cT, 6 * 2048, BF16)])
        t.barrier(waiters=("sp",))
        _selfsync(t)
        xs1 = [self.at(oD + 32768 + i * 4096, [128, 1024], F32) for i in range(4)]
        self._xs_i = 0
        pre_mem = [self.lt_issue(self.memd[s_ * 128:(s_ + 1) * 128, :], xs1, "xb") for s_ in range(2)]
        pre_x0 = [self.lt_issue(self.xh[HALO + s_ * 128: HALO + (s_ + 1) * 128, :], xs1, "xb") for s_ in range(2)]

        t.label = "E"
        mergedT = self.at(oU, [128, 8, 2048], BF16)
        sa = [self.at(oD + 24576, [128, 512], F32)] * 2
        sb = [self.at(oD + 26624, [128, 512], F32)] * 2
        e1 = [self.at(oD + 28672, [128, 512], F32)] * 2
        e2 = [self.at(oD + 30720, [128, 512], F32)] * 2
        ei = 0
        ckT = self.at(oD + 63488, [128, 8, 256], BF16)
        cV = self.at(oD + 67584, [128, 2, 1024], BF16)

        def m_block(mi):
            t.label = "M"
            wt_, wkey_ = self.w_next()
            if mi < 2:
                for jj in range(4):
                    j_ = mi * 4 + jj
                    b = self.bank()
                    self.mm_group(b, 256, [(wt_[:, jj * 1024 + k * 128: jj * 1024 + (k + 1) * 128], mT[:, k, :])
                                           for k in range(8)], [wkey_, "mT"])
                    t.op("act", lambda j_=j_, b=b: nc.scalar.copy(out=ckT[:, j_, :], in_=self.ps[b][:, 0:256]),
                         r=[("ps", b)], w=["ckT"])
            else:
                blk = mi - 2
                for mc in range(2):
                    b = self.bank()
                    self.mm_group(b, 512, [(mT[:, k, mc * 128:(mc + 1) * 128], wt_[:, k * 512:(k + 1) * 512])
                                           for k in range(8)], [wkey_, "mT"])
                    t.op("act", lambda mc=mc, b=b, blk=blk: nc.scalar.copy(
                        out=cV[:, mc, blk * 512:(blk + 1) * 512], in_=self.ps[b][:, :]), r=[("ps", b)], w=["cV"])
            t.label = "E"
        memT = self.at(oD + 0, [128, 8, 256], F32)
        mT = self.at(oD + 8192, [128, 8, 256], BF16)
        sq_m = self.at(oD + 49152, [128, 8, 512], BF16)
        lnv_m = self.at(oD + 57344, [128, 512], F32)
        rstd_m = self.at(oD + 59392, [128, 512], F32)
        for j in range(8):
            if j == 4:
                _selfsync(t)
                t.label = "M"
                self.load_transpose(lambda s: self.memd[s * 128:(s + 1) * 128, :], xs1, 2, memT,
                                    lambda s, hf: [("memT", hf)], "xb", pre=pre_mem)
                self.norm(memT[:, :, :], lambda k: memT[:, k, :], [("memT", 0), ("memT", 1)], C_GMEM,
                          lambda k: mT[:, k, :], ["mT"], 256, sq_m, lnv_m, rstd_m, "pm")
                t.label = "E"
            wt, wkey = self.w_next()
            wga = lambda k: wt[:, k * 128:(k + 1) * 128]
            wap = lambda k: wt[:, 1024 + k * 128:1024 + (k + 1) * 128]
            wgb = lambda k: wt[:, 1536 + k * 128:1536 + (k + 1) * 128]
            wcp = lambda k: wt[:, 2560 + k * 128:2560 + (k + 1) * 128]
            for tt in range(4):
                cs = slice(tt * 512, (tt + 1) * 512)
                bga = self.bank()
                self.mm_group(bga, 512, [(wga(k), self.uTm[:, k, cs]) for k in range(8)], [wkey, ("uT", 4 + tt)])
                bya = self.bank()
                self.mm_group(bya, 512, [(wap(k), self.attnT[:, k, cs]) for k in range(4)], [wkey, ("attnT", tt)])
                bgb = self.bank()
                self.mm_group(bgb, 512, [(wgb(k), self.uTm[:, k, cs]) for k in range(8)], [wkey, ("uT", 4 + tt)])
                byc = self.bank()
                self.mm_group(byc, 512, [(wcp(k), self.cT[:, k, cs]) for k in range(6)], [wkey, ("cT", tt)])
                s = 0
                ei += 1
                t.op("act", lambda: nc.scalar.activation(out=sa[s][:, :], in_=self.ps[bga][:, :], func=AF.Sigmoid,
                                                         bias=self.cc(C_BGA + j)), r=[("ps", bga)], w=[("sa", s)])
                t.op("act", lambda: nc.scalar.activation(out=sb[s][:, :], in_=self.ps[bgb][:, :], func=AF.Sigmoid,
                                                         bias=self.cc(C_BGB + j)), r=[("ps", bgb)], w=[("sb", s)])
                t.op("dve", lambda: nc.vector.tensor_tensor(out=e1[s][:, :], in0=self.ps[bya][:, :], in1=sa[s][:, :],
                                                            op=ALU.mult), r=[("ps", bya), ("sa", s)], w=[("e1", s)])
                t.op("dve", lambda: nc.vector.tensor_tensor(out=e2[s][:, :], in0=self.ps[byc][:, :], in1=sb[s][:, :],
                                                            op=ALU.mult), r=[("ps", byc), ("sb", s)], w=[("e2", s)])
                t.op("pool", lambda: nc.gpsimd.tensor_tensor(out=mergedT[:, j, cs], in0=e1[s][:, :], in1=e2[s][:, :],
                                                             op=ALU.add), r=[("e1", s), ("e2", s)], w=[("mg", tt)])
                if j == 0 and tt == 1 and ln_pending:
                    ln_pending.pop(0)()
                    t.label = "E"
                    self.bank_pool = list(range(8))
            if j >= 4:
                m_block(j - 4)
        if self.stop_after == "pE":
            return self.finish_dbg([(mergedT, 8 * 2048, BF16)])

        t.label = "M"
        hT = self.at(oD, [128, 16, 1024], BF16)
        bufA = self.at(oU + 32768, [128, 8, 1024], BF16)
        bufB = self.at(oU + 49152, [128, 8, 1024], BF16)
        xT = self.at(oB, [128, 8, 1024], F32)

        sq = self.at(oD + 49152, [128, 8, 512], BF16)
        lnv = self.at(oD + 57344, [128, 512], F32)
        rstdn = [self.at(oD + 59392 + i * 2048, [128, 512], F32) for i in range(2)]
        rstd = rstdn[0]
        ptc = [self.at(oB + 32768 + i * 1024, [128, 512], BF16) for i in range(4)]
        rl = [self.at(oB + 36864 + i * 2048, [128, 512], F32) for i in range(2)]
        rds = rl
        grow = self.at(oD + 71680, [128, 1024], F32)
        _selfsync(t)
        ri = 0
        TS = [slice(0, 512), slice(512, 1024)]
        for half in range(2):
            h0 = half * 1024
            t.label = "F0"
            f0_rows = lambda s: self.xh[HALO + h0 + s * 128: HALO + h0 + (s + 1) * 128, :]
            f0_keys = lambda s, hf: [("xTl", s // 4, hf), ("xT", s // 4)]
            self.load_transpose(f0_rows, xs1, 4, xT, f0_keys, "xb", pre=(pre_x0 if half == 0 else pre_x1))

            def f0_second():
                t.label = "F0"
                self.load_transpose(f0_rows, xs1, 4, xT, f0_keys, "xb", s_off=4)
                t.label = "F1"

            def proj_res(src, skeys, nk=8, tile_outer=True, mid_hook=None):
                per_blk = WBLK // (nk * 128)
                if tile_outer:
                    blks = self.w_take(8 // per_blk)
                    order = [(j, tt) for tt in range(2) for j in range(8)]
                else:
                    blks = None
                    order = [(j, tt) for j in range(8) for tt in range(2)]
                cur = None
                for (j, tt) in order:
                    if mid_hook is not None and tt == 1 and j == 0:
                        mid_hook()
                    if tile_outer:
                        wt_, wk_ = blks[j // per_blk]
                    else:
                        if j % per_blk == 0 and tt == 0:
                            cur = self.w_next()
                        wt_, wk_ = cur
                    o = (j % per_blk) * nk * 128
                    cs = TS[tt]
                    b = self.bank()
                    self.mm_group(b, 512, [(wt_[:, o + k * 128:o + (k + 1) * 128], src(k, cs)) for k in range(nk)],
                                  [wk_, skeys(tt)])
                    t.op("dve", lambda j=j, cs=cs, b=b: nc.vector.tensor_tensor(
                        out=xT[:, j, cs], in0=self.ps[b][:, :], in1=xT[:, j, cs], op=ALU.add),
                        r=[("ps", b), ("xT", tt), ("xTl", tt, 0), ("xTl", tt, 1)], w=[("xT", tt)])

            t.label = "F1"
            proj_res(lambda k, cs: mergedT[:, k, h0 + cs.start:h0 + cs.stop], lambda tt: ("mg", 0), mid_hook=f0_second)
            if self.stop_after == "F1":
                return self.finish_dbg([(xT, 8 * 1024, F32)])
            t.label = "F2"
            nb_ = {}
            for tt in range(2):
                cs = TS[tt]
                nb_[tt] = self.norm_split(xT[:, :, cs], lambda k, cs=cs: xT[:, k, cs], [("xT", tt)], C_GCROSS,
                                          lambda k, cs=cs: bufA[:, k, cs], [("bufA", tt)], 512, sq, lnv,
                                          rstdn[tt], ("rstdn", tt), "f2")
            nb_[0][0]()
            t.label = "F3"
            blks = self.w_take(2)
            for tt in range(2):
                cs = TS[tt]
                for j in range(8):
                    wt, wkey = blks[j // 4]
                    o = (j % 4) * 1024
                    b = self.bank()
                    self.mm_group(b, 512, [(wt[:, o + k * 128:o + (k + 1) * 128], bufA[:, k, cs]) for k in range(8)],
                                  [wkey, ("bufA", tt)])
                    if j == 0:
                        pend_ev = []
                    pend_ev.append(lambda j=j, cs=cs, b=b, tt=tt: t.op(
                        "dve", lambda: nc.vector.tensor_tensor(
                            out=bufB[:, j, cs], in0=self.ps[b][:, :], in1=rstdn[tt][:, :], op=ALU.mult),
                        r=[("ps", b), ("rstdn", tt)], w=[("bufB", tt)]))
                    if j == 2:
                        nb_[tt][1]()
                        if tt == 0:
                            nb_[1][0]()
                    if j >= 2:
                        while pend_ev:
                            pend_ev.pop(0)()
            t.label = "F4"
            items = [(tt, hc) for tt in range(2) for hc in range(4)]
            pend = {}
            pi = 0
            for it in range(len(items) + 1):
                if it < len(items):
                    tt, hc = items[it]
                    cs = TS[tt]
                    pp = []
                    for mc in range(2):
                        b = self.bank()
                        self.mm_group(b, 512, [(ckT[:, 2 * hc + e, mc * 128:(mc + 1) * 128], bufB[:, 2 * hc + e, cs])
                                               for e in range(2)], ["ckT", ("bufB", tt)])
                        p_ = ptc[pi % 4]
                        pk = ("ptc", pi % 4)
                        pi += 1
                        t.op("act", lambda p_=p_, b=b: nc.scalar.activation(out=p_[:, :], in_=self.ps[b][:, :],
                                                                            func=AF.Exp, scale=1.0 / 16.0),
                             r=[("ps", b)], w=[pk])
                        pp.append((p_, pk))
                    pend[it] = (tt, hc, pp)
                if it >= 1:
                    tt, hc, pp = pend.pop(it - 1)
                    cs = TS[tt]
                    bd = self.bank()
                    self.mm_group(bd, 512, [(self.ones_bf[:, :], pp[mc][0][:, :]) for mc in range(2)],
                                  [pp[0][1], pp[1][1], "ones"])
                    rdk = ("rd", it % 2)
                    rd_ = rds[it % 2]
                    t.op("act", lambda bd=bd, rd_=rd_: nc.scalar.activation(out=rd_[:, :], in_=self.ps[bd][:, :],
                                                                            func=AF.Ln), r=[("ps", bd)], w=[rdk])
                    t.op("act", lambda rd_=rd_: nc.scalar.activation(out=rd_[:, :], in_=rd_[:, :], func=AF.Exp,
                                                                     scale=-1.0), r=[rdk], w=[rdk])
                    for e in range(2):
                        bo = self.bank()
                        ec = 2 * hc + e
                        self.mm_group(bo, 512, [(cV[:, mc, ec * 128:(ec + 1) * 128], pp[mc][0][:, :])
                                                for mc in range(2)], ["cV", pp[0][1], pp[1][1]])
                        t.op("dve", lambda ec=ec, cs=cs, bo=bo, rd_=rd_: nc.vector.tensor_tensor(
                            out=bufA[:, ec, cs], in0=self.ps[bo][:, :], in1=rd_[:, :], op=ALU.mult),
                            r=[("ps", bo), rdk], w=[("bufA", tt)])
            t.label = "F5"
            proj_res(lambda k, cs: bufA[:, k, cs], lambda tt: ("bufA", tt))
            if self.stop_after == "F5":
                return self.finish_dbg([(xT, 8 * 1024, F32)])
            t.label = "G1"
            nb_ = {}
            for tt in range(2):
                cs = TS[tt]
                nb_[tt] = self.norm_split(xT[:, :, cs], lambda k, cs=cs: xT[:, k, cs], [("xT", tt)], C_GMLP,
                                          lambda k, cs=cs: bufB[:, k, cs], [("bufB", tt)], 512, sq, lnv,
                                          rstdn[tt], ("rstdn", tt), "g1")
            nb_[0][0]()
            r_i = 0
            for fh in range(2):
                t.label = "G2"
                for fb in range(4):
                    wt, wkey = self.w_next()
                    for tt in range(2):
                        cs = TS[tt]
                        for fc in range(4):
                            f = fb * 4 + fc
                            b = self.bank()
                            self.mm_group(b, 512, [(wt[:, fc * 1024 + k * 128: fc * 1024 + (k + 1) * 128],
                                                    bufB[:, k, cs]) for k in range(8)], [wkey, ("bufB", tt)])
                            first_ = (fh == 0 and fb == 0)
                            if fc == 0:
                                pend_ev = []

                            def _ev(b=b, tt=tt, f=f, cs=cs):
                                nonlocal r_i
                                r_ = rl[r_i % 2]
                                rk_ = ("rl", r_i % 2)
                                r_i += 1
                                t.op("dve", lambda: nc.vector.scalar_tensor_tensor(
                                    out=r_[:, :], in0=self.ps[b][:, :], scalar=0.0, in1=rstdn[tt][:, :],
                                    op0=ALU.max, op1=ALU.mult),
                                    r=[("ps", b), ("rstdn", tt)], w=[rk_])
                                t.op("act", lambda: nc.scalar.activation(
                                    out=hT[:, f, cs], in_=r_[:, :], func=AF.Square), r=[rk_], w=[("hT", tt)])
                            pend_ev.append(_ev)
                            if first_ and fc == 2:
                                nb_[tt][1]()
                                if tt == 0:
                                    nb_[1][0]()
                            if (not first_) or fc >= 2:
                                while pend_ev:
                                    pend_ev.pop(0)()
                t.label = "G3"
                proj_res(lambda k, cs: hT[:, k, cs], lambda tt: ("hT", tt), nk=16, tile_outer=False)
            if self.stop_after == "G3":
                return self.finish_dbg([(xT, 8 * 1024, F32)])
            if half == 0:
                t.dma("sp", lambda: nc.sync.dma_start(out=grow[:, :], in_=self.growd), stream="c3", w=["grow"])
                pre_x1 = [self.lt_issue(self.xh[HALO + 1024 + s_ * 128: HALO + 1024 + (s_ + 1) * 128, :], xs1, "xb")
                          for s_ in range(4)]
            t.label = "H"
            ident = self.consts[:, C_ID:C_ID + 128]
            sqh = [self.at(oU + 32768, [128, 8, 512], BF16), self.at(oU + 49152, [128, 8, 512], BF16)]
            ogs = [self.at(oU + 32768 + 8192 + i * 4096, [128, 1024], F32) for i in range(2)] + \
                  [self.at(oU + 49152 + 8192 + i * 4096, [128, 1024], F32) for i in range(2)]
            ogkeys = [[("og", i), ("bufA" if i < 2 else "bufB", 0), ("bufA" if i < 2 else "bufB", 1)]
                      for i in range(4)]
            self._og_i = 0
            for tt in range(2):
                t.op("act", lambda tt=tt: nc.scalar.activation(out=sqh[tt][:, :, :], in_=xT[:, :, TS[tt]],
                                                               func=AF.Square),
                     r=[("xT", tt)], w=[("sqh", tt)])
            for tt in range(2):
                cs = TS[tt]
                rs_ = rstdn[tt]
                rk = ("rstdn", tt)
                sqx = sqh[tt]

                def h_stats():
                    bs = self.bank()
                    self.mm_group(bs, 512, [(self.ones_bf[:, :], sqx[:, k, :]) for k in range(8)], [("sqh", tt)])
                    t.op("act", lambda: nc.scalar.activation(out=lnv[:, :], in_=self.ps[bs][:, :], func=AF.Ln,
                                                             bias=self.cc(C_EPS), scale=1.0 / D),
                         r=[("ps", bs)], w=[("lnv", "h")])
                    t.op("act", lambda: nc.scalar.activation(out=rs_[:, :], in_=lnv[:, :], func=AF.Exp, scale=-0.5),
                         r=[("lnv", "h")], w=[rk])

                def x_tr(s, tt=tt, cs=cs):
                    bl = []
                    for hf in range(2):
                        b = self.bank()
                        for kk in range(4):
                            k = hf * 4 + kk
                            c0 = cs.start + s * 128
                            t.op("pe", lambda k=k, kk=kk, b=b, c0=c0: nc.tensor.transpose(
                                self.ps[b][:, kk * 128:(kk + 1) * 128], xT[:, k, c0:c0 + 128], ident),
                                r=[("xT", tt)], w=[("ps", b)])
                        bl.append(b)
                    return bl

                def x_ev(s, bl, tt=tt):
                    osl = self._og_i % 4
                    self._og_i += 1
                    og = ogs[osl]
                    okeys = ogkeys[osl]
                    for hf in range(2):
                        b = bl[hf]
                        t.op("dve", lambda b=b, hf=hf: nc.vector.scalar_tensor_tensor(
                            out=og[:, hf * 512:(hf + 1) * 512], in0=self.ps[b][:, :],
                            scalar=self.rcols[:, tt * 4 + s:tt * 4 + s + 1],
                            in1=grow[:, hf * 512:(hf + 1) * 512], op0=ALU.mult, op1=ALU.mult),
                            r=[("ps", b), ("rcols", tt), "grow"], w=okeys)
                    r0 = h0 + tt * 512 + s * 128
                    t.dma("sp", lambda: nc.sync.dma_start(out=self.outd[r0:r0 + 128, :], in_=og[:, :]),
                          stream=f"o{osl}", r=okeys, w=[("outd", r0)])

                pend = [(s, x_tr(s)) for s in range(2)]
                h_stats()
                br = self.bank()
                for s in range(4):
                    t.op("pe", lambda s=s: nc.tensor.transpose(self.ps[br][:, s * 128:(s + 1) * 128],
                                                               rs_[:, s * 128:(s + 1) * 128], ident),
                         r=[rk], w=[("ps", br)])
                t.op("act", lambda: nc.scalar.copy(out=self.rcols[:, tt * 4:tt * 4 + 4],
                                                   in_=self.ps[br][:, 0:512:128]),
                     r=[("ps", br)], w=[("rcols", tt)])
                for s in range(2, 4):
                    s_, bl_ = pend.pop(0)
                    x_ev(s_, bl_)
                    pend.append((s, x_tr(s)))
                while pend:
                    s_, bl_ = pend.pop(0)
                    x_ev(s_, bl_)
        t.barrier(waiters=("sp",), skip_prefix="dma_w")
        return nc

    def finish_dbg(self, items):
        nc, t = self.nc, self.t
        t.barrier()
        stage = self.at(self.oD + 73728, [128, 512], F32)
        off = 0
        for (tens, n, dt) in items:
            for c0 in range(0, n, 512):
                m = min(512, n - c0)
                src = self._flat(tens, c0, m)
                t.op("act", lambda: nc.scalar.copy(out=stage[:, 0:m], in_=src), w=["stg"])
                t.dma("sp", lambda: nc.sync.dma_start(out=self.dbgd[:, off + c0: off + c0 + m], in_=stage[:, 0:m]),
                      stream="dbg", r=["stg"], w=[("dbgo", off + c0)])
                t.barrier()
            off += n
        t.barrier(waiters=("sp",))
        return nc

    def _flat(self, tens, c0, m):
        shp = list(tens.shape)
        if len(shp) == 2:
            return tens[:, c0:c0 + m]
        inner = shp[2]
        if inner >= m:
            assert inner % m == 0
            return tens[:, c0 // inner, (c0 % inner):(c0 % inner) + m]
        assert c0 % inner == 0 and m % inner == 0
        return tens[:, c0 // inner:(c0 + m) // inner, :]


def _chunk(W, c0, ncols=128, cols=None):
    if cols is None:
        cols = np.arange(c0, c0 + ncols)
    sub = W[:, cols]
    K = sub.shape[0]
    return sub.reshape(K // 128, 128, len(cols)).transpose(1, 0, 2).reshape(128, -1)


def _pack_weights(inp):
    w_in = inp["w_in"][0]
    blocks = np.zeros((NWB, 128, WBLK), np.float32)

    def put(bi, off, arr):
        blocks[bi, :, off:off + arr.shape[1]] = arr

    bi = WB_A
    for h4 in range(4):
        for g in range(3):
            head = g * 4 + h4
            put(bi, 0, _chunk(w_in, 0, cols=head * 128 + PERM))
            put(bi, 1024, _chunk(w_in, 0, cols=1536 + head * 128 + PERM))
            put(bi, 2048, _chunk(w_in, 3072 + head * 128))
            bi += 1
    for i in range(3):
        for jj in range(2):
            j = i * 2 + jj
            put(WB_C + i, jj * 2048, _chunk(w_in, 4608 + j * 128))
            put(WB_C + i, jj * 2048 + 1024, _chunk(w_in, 4608 + 768 + j * 128))
    wap = inp["w_attn_proj"][0]
    wcp = inp["w_conv_proj"][0]
    for j in range(8):
        put(WB_E + j, 0, _chunk(w_in, 6144 + j * 128))
        put(WB_E + j, 1024, _chunk(wap, j * 128))
        put(WB_E + j, 1536, _chunk(w_in, 7168 + j * 128))
        put(WB_E + j, 2560, _chunk(wcp, j * 128))
    wckv = inp["w_ckv"][0]
    for j in range(8):
        put(WB_M + j // 4, (j % 4) * 1024, _chunk(wckv, j * 128))
    for i in range(2):
        put(WB_M + 2 + i, 0, _chunk(wckv, 1024 + i * 512, ncols=512))
    for (wb, name) in ((WB_OUT, "w_out"), (WB_CQ, "w_cq"), (WB_CO, "w_co")):
        W = inp[name][0]
        for j in range(8):
            put(wb + j // 4, (j % 4) * 1024, _chunk(W, j * 128))
    wup = inp["w_up"][0]
    wdn = inp["w_down"][0]
    bi = WB_UP
    for fh in range(2):
        for fb in range(4):
            for fc in range(4):
                f = fh * 16 + fb * 4 + fc
                put(bi, fc * 1024, _chunk(wup, f * 128))
            bi += 1
        for jb in range(4):
            for jc in range(2):
                j = jb * 2 + jc
                put(bi, jc * 2048, _chunk(wdn[fh * 2048:(fh + 1) * 2048], j * 128))
            bi += 1
    assert bi == NWB
    return blocks


def _vec8(v):
    return v.reshape(-1, 128).T


def _consts(inp, flag):
    c = np.zeros((128, CW), np.float32)
    c[:, C_ID:C_ID + 128] = np.eye(128, dtype=np.float32)
    kk = np.arange(128)[:, None]
    qq = np.arange(128)[None, :]
    NEG = np.float32(-30000.0)
    prev = np.where(kk >= qq, np.float32(0.0), NEG).astype(np.float32)
    cur = np.where(kk <= qq, np.float32(0.0), NEG).astype(np.float32)
    c[:, C_MASK:C_MASK + 128] = prev
    c[:, C_MASK + 128:C_MASK + 256] = cur
    c[:, C_MASK0:C_MASK0 + 128] = prev if flag else NEG
    c[:, C_MASK0 + 128:C_MASK0 + 256] = cur
    c[:, C_EPS] = EPS
    c[:, C_GMIX:C_GMIX + 8] = _vec8(inp["g_mix"][0])
    c[:, C_GCROSS:C_GCROSS + 8] = _vec8(inp["g_cross"][0])
    c[:, C_GMEM:C_GMEM + 8] = _vec8(inp["g_mem"][0])
    c[:, C_GMLP:C_GMLP + 8] = _vec8(inp["g_mlp"][0])
    c[:, C_GFIN:C_GFIN + 8] = _vec8(inp["g_final"])
    c[:, C_BGA:C_BGA + 8] = _vec8(inp["b_gate"][0][:1024])
    c[:, C_BGB:C_BGB + 8] = _vec8(inp["b_gate"][0][1024:])
    c[:, C_CONVB:C_CONVB + 6] = _vec8(inp["conv_b"][0])
    c[:, C_LNG:C_LNG + 6] = _vec8(inp["conv_ln_g"][0])
    c[:, C_LNB:C_LNB + 6] = _vec8(inp["conv_ln_b"][0])
    cw = inp["conv_w"][0]
    for j in range(6):
        c[:, C_CONVW + j * 31:C_CONVW + (j + 1) * 31] = cw[:, j * 128:(j + 1) * 128].T
    return c


def _rope_tables(pos0):
    pos = (pos0 + np.arange(4096)).astype(np.float64)
    inv = 500000.0 ** (-np.arange(0, 32, 2, dtype=np.float64) / 32.0)
    ang = pos[None, :] * inv[:, None]
    cs, sn = np.cos(ang).astype(np.float32), np.sin(ang).astype(np.float32)
    C = np.ones((64, 4096), np.float32)
    S = np.zeros((64, 4096), np.float32)
    C[0:16] = cs
    C[32:48] = cs
    S[0:16] = sn
    S[32:48] = -sn
    return C, S


_CACHE = {}


def _get_nc(stop_after=None, dbg=False):
    key = (stop_after, dbg)
    if key not in _CACHE:
        _CACHE[key] = Kern(stop_after, dbg).build()
    return _CACHE[key]


def _in_maps(inputs):
    inp = {k: np.asarray(v, dtype=np.float32) for k, v in inputs.items()}
    wp = _pack_weights(inp)
    x, mem = inp["x"], inp["mem"]
    maps = []
    for c in range(NCORES):
        b, q = c // 4, c % 4
        main = x[b, q * T:(q + 1) * T]
        halo = x[b, (q - 1) * T:q * T] if q > 0 else np.zeros((HALO, D), np.float32)
        C, S = _rope_tables(q * T - HALO)
        maps.append({
            "xh": np.ascontiguousarray(np.concatenate([halo, main], axis=0)),
            "memb": np.ascontiguousarray(mem[b]),
            "consts": _consts(inp, q > 0),
            "ropeC": C, "ropeS": S,
            "wpack": wp,
            "grow": np.ascontiguousarray(np.broadcast_to(inp["g_final"][None, :], (128, D))),
        })
    return maps


def kernel(**inputs):
    nc = _get_nc()
    maps = _in_maps(inputs)
    res = run_bass_kernel_spmd(nc, maps, core_ids=list(range(NCORES)))
    out = np.zeros((2, 4 * T, D), np.float32)
    for c in range(NCORES):
        out[c // 4, (c % 4) * T:(c % 4 + 1) * T] = res.results[c]["out"]
    return out
```

```python
import numpy as np
import ml_dtypes
import concourse.bass as bass
import concourse.mybir as mybir
from concourse.bass_utils import run_bass_kernel_spmd

F32 = mybir.dt.float32
BF16 = mybir.dt.bfloat16
AF = mybir.ActivationFunctionType
ALU = mybir.AluOpType

NCORES = 8
T = 2048
HALO = 2048
D = 1024
EPS = 1e-6
WBLK = 4096
PERM = np.array(list(range(0, 16)) + list(range(32, 48)) + list(range(16, 32)) + list(range(48, 128)))
DIL = (1, 4, 16)
NTAP_PE = 25

C_ID = 0
C_MASK = 128
C_MASK0 = 384
C_EPS = 640
C_GMIX = 641
C_GCROSS = 649
C_GMEM = 657
C_GMLP = 665
C_GFIN = 673
C_BGA = 681
C_BGB = 689
C_CONVB = 697
C_LNG = 703
C_LNB = 709
C_CONVW = 715
CW = 715 + 186

WB_A = 0
WB_C = 12
WB_E = 15
WB_M = 23
WB_OUT = 27
WB_CQ = 29
WB_CO = 31
WB_UP = 33
WB_DN = 41
NWB = 49


class Trk:
    def __init__(self, nc):
        self.nc = nc
        self.eng = {"pe": nc.tensor, "act": nc.scalar, "dve": nc.vector, "pool": nc.gpsimd, "sp": nc.sync}
        self.sems = {}
        self.cnt = {}
        for e in ("pe", "act", "dve", "pool"):
            self.cnt[e] = 0
        self.seen = {e: {} for e in self.eng}
        self.last_w = {}
        self.readers = {}
        self.dma_cnt = {}
        self.label = ""
        self.log = {e: [] for e in self.eng}

    def _sem(self, name):
        if self.sems.get(name) is None:
            cm = self.nc.semaphore(f"s_{name}")
            self.sems[name] = cm.__enter__()
        return self.sems[name]

    def _deps(self, eng, r, w):
        deps = {}

        def add(tok):
            if tok is None:
                return
            s, v = tok
            if deps.get(s, 0) < v:
                deps[s] = v

        for k in r:
            add(self.last_w.get(k))
        for k in w:
            add(self.last_w.get(k))
            for tok in self.readers.get(k, ()):
                add(tok)
        need = []
        for s, v in deps.items():
            if s == "pe" and eng == "pe":
                continue
            if self.seen[eng].get(s, 0) < v:
                need.append((s, v))
        return need

    def _emit(self, eng, fn, need):
        e = self.eng[eng]
        for s, v in need[:-1]:
            e.wait_ge(self._sem(s), v)
        ins = fn()
        if need:
            s, v = need[-1]
            ins._wait_ge(self._sem(s), v)
        for s, v in need:
            self.seen[eng][s] = v
        return ins

    def _record(self, tok, r, w):
        for k in w:
            self.last_w[k] = tok
            self.readers[k] = []
        for k in r:
            self.readers.setdefault(k, []).append(tok)

    def op(self, eng, fn, r=(), w=()):
        need = self._deps(eng, r, w)
        ins = self._emit(eng, fn, need)
        self.cnt[eng] += 1
        self.log[eng].append(self.label)
        ins.then_inc(self._sem(eng), 1)
        self._record((eng, self.cnt[eng]), r, w)

    def dma(self, q, fn, stream, r=(), w=()):
        need = self._deps(q, r, w)
        ins = self._emit(q, fn, need)
        s = "dma_" + stream
        self.dma_cnt[s] = self.dma_cnt.get(s, 0) + 16
        ins.then_inc(self._sem(s), 16)
        self._record((s, self.dma_cnt[s]), r, w)

    def barrier(self, waiters=("pe", "act", "dve", "sp"), skip_prefix="dma_w"):
        toks = [(e, self.cnt[e]) for e in ("pe", "act", "dve", "pool") if self.cnt[e] > 0]
        toks += [(s, v) for s, v in self.dma_cnt.items() if not s.startswith(skip_prefix)]
        for wtr in waiters:
            for s, v in toks:
                if self.seen[wtr].get(s, 0) < v:
                    self.eng[wtr].wait_ge(self._sem(s), v)
                    self.seen[wtr][s] = v


def _selfsync(t, engines=("act", "dve", "pool")):
    for e in engines:
        v = t.cnt[e]
        if v > 0 and t.seen[e].get(e, 0) < v:
            t.eng[e].wait_ge(t._sem(e), v)
            t.seen[e][e] = v


class Kern:
    def __init__(self, stop_after=None, dbg=False):
        self.stop_after = stop_after
        nc = self.nc = bass.Bass("TRN2", target_bir_lowering=False)
        self.xh = nc.dram_tensor("xh", [HALO + T, D], F32, kind="ExternalInput").ap()
        self.memd = nc.dram_tensor("memb", [256, D], F32, kind="ExternalInput").ap()
        self.constd = nc.dram_tensor("consts", [128, CW], F32, kind="ExternalInput").ap()
        self.ropeCd = nc.dram_tensor("ropeC", [64, 4096], F32, kind="ExternalInput").ap()
        self.ropeSd = nc.dram_tensor("ropeS", [64, 4096], F32, kind="ExternalInput").ap()
        self.wpack = nc.dram_tensor("wpack", [NWB, 128, WBLK], F32, kind="ExternalInput").ap()
        self.growd = nc.dram_tensor("grow", [128, D], F32, kind="ExternalInput").ap()
        self.outd = nc.dram_tensor("out", [T, D], F32, kind="ExternalOutput").ap()
        self.dbg = dbg
        if dbg:
            self.dbgd = nc.dram_tensor("dbg", [128, 8 * 4096], F32, kind="ExternalOutput").ap()
        self.t = Trk(nc)
        self.consts = nc.alloc_sbuf_tensor("consts_sb", [128, CW], F32)
        self.ones_bf = nc.alloc_sbuf_tensor("ones_bf", [128, 128], BF16)
        self.ident_bf = nc.alloc_sbuf_tensor("ident_bf", [128, 128], BF16)
        self.mask_bf = nc.alloc_sbuf_tensor("mask_bf", [128, 256], BF16)
        self.mask0_bf = nc.alloc_sbuf_tensor("mask0_bf", [128, 256], BF16)
        self.rcols = nc.alloc_sbuf_tensor("rcols", [128, 8], F32)
        base = nc.SBUF_PARTITION_SIZE_BYTES - nc.sbuf_bytes_remaining
        base = (base + 63) // 64 * 64
        self.arena_total = 24576 + 65536 + 16384 + 24576 + 73728 + 2048
        nc.alloc_sbuf_tensor("arena", [128, self.arena_total + 64], mybir.dt.uint8)
        self.oW = base
        self.oU = self.oW + 24576
        self.oB = self.oU + 65536
        self.oC = self.oB + 16384
        self.oD = self.oC + 24576
        self._n = 0
        self.ps = [nc.alloc_psum_tensor(f"psb{i}", [128, 512], F32) for i in range(8)]
        self._bank = 0
        self.bank_pool = list(range(8))
        self.wslots = [self.at(self.oW + i * 8192, [128, WBLK], BF16) for i in range(3)]
        self.wq = []
        self.wq_issued = 0
        self.wq_pos = 0

    def at(self, off, shape, dt):
        self._n += 1
        assert off % 32 == 0, off
        return self.nc.alloc_sbuf_tensor_at(f"m{self._n}", shape, dt, offset=off)

    def bank(self):
        pool = self.bank_pool
        b = pool[self._bank % len(pool)]
        self._bank += 1
        return b

    def cc(self, col, n=1):
        return self.consts[:, col:col + n]

    def w_plan(self, blocks):
        self.wq.extend(blocks)

    def _w_issue(self):
        i = self.wq_issued
        blk = self.wq[i]
        slot = i % 3
        dst = self.wslots[slot]
        self.t.dma("pool", lambda: self.nc.gpsimd.dma_start(out=dst[:, :], in_=self.wpack[blk]),
                   stream=f"w{slot}", w=[("w", slot)])
        self.wq_issued += 1

    def w_take(self, n):
        first = self.wq_pos
        while self.wq_issued < min(len(self.wq), first + 3):
            self._w_issue()
        out = []
        for i in range(n):
            slot = (first + i) % 3
            out.append((self.wslots[slot], ("w", slot)))
        self.wq_pos += n
        return out

    def w_next(self):
        return self.w_take(1)[0]

    def mm_group(self, b, ncols, pairs, r_keys, col0=0):
        n = len(pairs)
        out = self.ps[b][:, col0:col0 + ncols]
        for i, (lh, rh) in enumerate(pairs):
            self.t.op("pe", lambda lh=lh, rh=rh, i=i: self.nc.tensor.matmul(
                out, lhsT=lh, rhs=rh, start=(i == 0), stop=(i == n - 1)),
                r=r_keys if i == 0 else (), w=[("ps", b)])
        self.t._record(("pe", self.t.cnt["pe"]), r_keys, ())

    def norm(self, xall, xk, xkeys, gcol, out_fn, okeys, ntok, sq, lnv, rstd, tag, split=False, pool_sq=None):
        nc, t = self.nc, self.t

        def part_sq():
            if pool_sq is None:
                t.op("act", lambda: nc.scalar.activation(out=sq[:, :, 0:ntok], in_=xall, func=AF.Square),
                     r=xkeys, w=[("sq", tag)])
            else:
                xlo, xhi = pool_sq
                t.op("act", lambda: nc.scalar.activation(out=sq[:, 0:4, 0:ntok], in_=xlo, func=AF.Square),
                     r=xkeys, w=[("sq", tag)])
                t.op("pool", lambda: nc.gpsimd.tensor_tensor(out=sq[:, 4:8, 0:ntok], in0=xhi, in1=xhi,
                                                             op=ALU.mult),
                     r=xkeys, w=[("sq", tag, 1)])

        def part_rest():
            self._norm_rest(xk, xkeys, gcol, out_fn, okeys, ntok, sq, lnv, rstd, tag)
        if split:
            return part_sq, part_rest
        part_sq()
        part_rest()

    def _norm_rest(self, xk, xkeys, gcol, out_fn, okeys, ntok, sq, lnv, rstd, tag):
        nc, t = self.nc, self.t
        b = self.bank()
        self.mm_group(b, ntok, [(self.ones_bf[:, :], sq[:, k, 0:ntok]) for k in range(8)],
                      [("sq", tag), ("sq", tag, 1)])
        t.op("act", lambda: nc.scalar.activation(out=lnv[:, 0:ntok], in_=self.ps[b][:, 0:ntok], func=AF.Ln,
                                                 bias=self.cc(C_EPS), scale=1.0 / D),
             r=[("ps", b)], w=[("lnv", tag)])
        t.op("act", lambda: nc.scalar.activation(out=rstd[:, 0:ntok], in_=lnv[:, 0:ntok], func=AF.Exp, scale=-0.5),
             r=[("lnv", tag)], w=[("rstd", tag)])
        for k in range(8):
            t.op("dve", lambda k=k: nc.vector.scalar_tensor_tensor(
                out=out_fn(k), in0=xk(k), scalar=self.cc(gcol + k), in1=rstd[:, 0:ntok],
                op0=ALU.mult, op1=ALU.mult),
                r=xkeys + [("rstd", tag)], w=okeys)

    def norm_split(self, xall, xk, xkeys, gcol, ug_fn, ugkeys, ntok, sq, lnv, rstd_out, rkey, tag):
        nc, t = self.nc, self.t
        for k in range(8):
            if k < 4:
                t.op("act", lambda k=k: nc.scalar.activation(out=ug_fn(k), in_=xk(k), func=AF.Copy,
                                                             scale=self.cc(gcol + k)),
                     r=xkeys + ["consts"], w=ugkeys)
            else:
                t.op("dve", lambda k=k: nc.vector.tensor_scalar(out=ug_fn(k), in0=xk(k), scalar1=self.cc(gcol + k),
                                                                scalar2=None, op0=ALU.mult),
                     r=xkeys + ["consts"], w=ugkeys)
        def part_sq():
            t.op("act", lambda: nc.scalar.activation(out=sq[:, :, 0:ntok], in_=xall, func=AF.Square),
                 r=xkeys, w=[("sq", tag)])

        def part_b():
            b = self.bank()
            self.mm_group(b, ntok, [(self.ones_bf[:, :], sq[:, k, 0:ntok]) for k in range(8)], [("sq", tag)])
            t.op("act", lambda: nc.scalar.activation(out=lnv[:, 0:ntok], in_=self.ps[b][:, 0:ntok], func=AF.Ln,
                                                     bias=self.cc(C_EPS), scale=1.0 / D),
                 r=[("ps", b)], w=[("lnv", tag)])
            t.op("act", lambda: nc.scalar.activation(out=rstd_out[:, 0:ntok], in_=lnv[:, 0:ntok], func=AF.Exp,
                                                     scale=-0.5),
                 r=[("lnv", tag)], w=[rkey])
        return part_sq, part_b

    def io_alloc(self, nslots, exclude=()):
        while True:
            slot = self._xs_i % nslots
            self._xs_i += 1
            if slot not in exclude:
                return slot

    def lt_issue(self, rows, xs_slots, stream, exclude=()):
        nc, t = self.nc, self.t
        slot = self.io_alloc(len(xs_slots), exclude)
        xs = xs_slots[slot]
        t.dma("sp", lambda: nc.sync.dma_start(out=xs[:, :], in_=rows), stream=f"{stream}{slot}",
              w=[("xs", slot)])
        return slot

    def load_transpose(self, src_rows, xs_slots, nsub, dst, dkey, stream, evac=("act", "dve"), pre=(), s_off=0):
        nc, t = self.nc, self.t
        ident = self.consts[:, C_ID:C_ID + 128]
        pre = list(pre)
        for s_ in range(nsub):
            s = s_ + s_off
            if s_ < len(pre):
                slot = pre[s_]
            else:
                slot = self.lt_issue(src_rows(s), xs_slots, stream, exclude=pre[s_ + 1:])
            xs = xs_slots[slot]
            for hf in range(2):
                b = self.bank()
                for kk in range(4):
                    k = hf * 4 + kk
                    t.op("pe", lambda k=k, kk=kk: nc.tensor.transpose(
                        self.ps[b][:, kk * 128:(kk + 1) * 128], xs[:, k * 128:(k + 1) * 128], ident),
                        r=[("xs", slot)], w=[("ps", b)])
                src = self.ps[b][:, 0:512].rearrange("p (a b) -> p a b", a=4)
                dd = dst[:, hf * 4:hf * 4 + 4, s * 128:(s + 1) * 128]
                if evac[hf] == "act":
                    t.op("act", lambda: nc.scalar.copy(out=dd, in_=src), r=[("ps", b)], w=dkey(s, hf))
                else:
                    t.op("dve", lambda: nc.vector.tensor_copy(out=dd, in_=src), r=[("ps", b)], w=dkey(s, hf))

    def ucols(self, k, a0, n, step=1):
        if a0 < 2048:
            tt, o = self.uTh, a0
        else:
            tt, o = self.uTm, a0 - 2048
        assert o + (n - 1) * step < 2048
        return tt[:, k, o:o + (n - 1) * step + 1:step]

    def ukeys(self, a0, n, step=1):
        return [("uT", i) for i in range(a0 // 512, (a0 + (n - 1) * step) // 512 + 1)]

    def kcols(self, a0, n, step=1):
        if a0 < 2048:
            tt, o = self.kTh, a0
        else:
            tt, o = self.kTm, a0 - 2048
        assert o + (n - 1) * step < 2048
        return tt[:, o:o + (n - 1) * step + 1:step]

    def kkeys(self, a0, n, step=1):
        return [("kT", i) for i in range(a0 // 512, (a0 + (n - 1) * step) // 512 + 1)]

    def build(self):
        nc, t = self.nc, self.t
        self._xs_i = 0
        em = [WB_E + j for j in range(5)]
        for i in range(3):
            em += [WB_M + i, WB_E + 5 + i]
        em += [WB_M + 3]
        plan = list(range(WB_A, WB_A + 12)) + list(range(WB_C, WB_C + 3)) * 2 + em + list(range(WB_OUT, NWB)) * 2
        self.w_plan(plan)
        oU, oB, oC, oD = self.oU, self.oB, self.oC, self.oD
        self.uTh = self.at(oU, [128, 8, 2048], BF16)
        self.uTm = self.at(oU + 32768, [128, 8, 2048], BF16)
        self.attnT = self.at(oB, [128, 4, 2048], BF16)
        self.cT = self.at(oC, [128, 6, 2048], BF16)
        ropeC = self.at(oC, [64, 4096], F32)
        ropeS = self.at(oD + 57344, [64, 4096], F32)

        t.dma("sp", lambda: nc.sync.dma_start(out=self.consts[:, :], in_=self.constd), stream="c0", w=["consts"])
        t.dma("sp", lambda: nc.sync.dma_start(out=ropeC[:, :], in_=self.ropeCd), stream="c1", w=["ropeC"])
        t.op("dve", lambda: nc.vector.memset(self.ones_bf[:, :], 1.0), w=["ones"])
        t.op("act", lambda: nc.scalar.copy(out=self.ident_bf[:, :], in_=self.consts[:, C_ID:C_ID + 128]),
             r=["consts"], w=["identbf"])
        t.op("act", lambda: nc.scalar.copy(out=self.mask_bf[:, :], in_=self.consts[:, C_MASK:C_MASK + 256]),
             r=["consts"], w=["mask"])
        t.op("act", lambda: nc.scalar.copy(out=self.mask0_bf[:, :], in_=self.consts[:, C_MASK0:C_MASK0 + 256]),
             r=["consts"], w=["mask"])
        t.barrier()

        while self.wq_issued < 3:
            self._w_issue()
        t.label = "p0"
        xs0 = [self.at(oD + i * 4096, [128, 1024], F32) for i in range(3)]
        xTts = [self.at(oD + 12288 + i * 16384, [128, 8, 512], F32) for i in range(3)]
        sq = self.at(oD + 61440, [128, 8, 512], BF16)
        lnv = self.at(oD + 69632, [128, 512], F32)
        rstd = self.at(oD + 71680, [128, 512], F32)
        def p0_load(tt):
            xk_ = ("xTt", tt % 3)
            self.load_transpose(lambda s, tt=tt: self.xh[tt * 512 + s * 128: tt * 512 + (s + 1) * 128, :],
                                xs0, 4, xTts[tt % 3], lambda s, hf, xk_=xk_: [xk_ + (hf,)], "x", evac=("act", "act"))

        def p0_norm(tt):
            xTt = xTts[tt % 3]
            xk_ = ("xTt", tt % 3)
            dstT = self.uTh if tt < 4 else self.uTm
            c0 = (tt % 4) * 512
            return self.norm(xTt[:, :, :], lambda k: xTt[:, k, :], [xk_ + (0,), xk_ + (1,)], C_GMIX,
                             lambda k: dstT[:, k, c0:c0 + 512], [("uT", tt)], 512, sq, lnv, rstd, "p0", split=True,
                             pool_sq=(xTt[:, 0:4, :], xTt[:, 4:8, :]))

        p0_load(0)
        parts = p0_norm(0)
        parts[0]()
        for tt in range(8):
            if tt + 1 < 8:
                p0_load(tt + 1)
            parts[1]()
            if tt + 1 < 8:
                parts = p0_norm(tt + 1)
                parts[0]()
        if self.stop_after == "p0":
            return self.finish_dbg([(self.uTm, 8 * 2048, BF16)])
        t.barrier(waiters=("act", "dve", "sp"))

        t.dma("sp", lambda: nc.sync.dma_start(out=ropeS[:, :], in_=self.ropeSd), stream="c2", w=["ropeS"])
        self.qT = self.at(oD, [128, 2048], BF16)
        self.kTh = self.at(oD + 4096, [128, 2048], BF16)
        self.kTm = self.at(oD + 8192, [128, 2048], BF16)
        Vt = self.at(oD + 12288, [128, 32, 128], BF16)
        acc = self.at(oD + 20480, [128, 2, 2048], F32)
        a32 = [self.at(oD + 36864 + i * 2048, [128, 512], F32) for i in range(2)]
        tmp = [self.at(oD + 40960 + i * 2048, [128, 512], F32) for i in range(2)]
        pts = [self.at(oD + 45056 + i * 512, [128, 256], BF16) for i in range(4)]
        pms = [self.at(oD + 47104 + i * 512, [128, 256], BF16) for i in range(4)]
        for i in range(2):
            t.op("dve", lambda i=i: nc.vector.memset(tmp[i][:, :], 0.0), w=[("tmp", i)])
        rope_i = 0
        blk_i = 0
        scale = 1.0 / np.sqrt(128.0)
        for h4 in range(4):
            for g in range(3):
                d = DIL[g]
                halo = 128 * d
                wt, wkey = self.w_next()
                wq_ = lambda k: wt[:, k * 128:(k + 1) * 128]
                wk_ = lambda k: wt[:, 1024 + k * 128:1024 + (k + 1) * 128]
                wv_ = lambda k: wt[:, 2048 + k * 128:2048 + (k + 1) * 128]
                def emit_qk():
                    nonlocal rope_i
                    t.label = "A.qk"
                    jobs = []
                    for tt in range(4):
                        jobs.append(("q", 2048 + tt * 512, 512))
                    a = 2048 - halo
                    while a < 4096:
                        n = min(512, 4096 - a, 512 - (a % 512) if a % 512 else 512)
                        jobs.append(("k", a, n))
                        a += n
                    for (kind, a0, n) in jobs:
                        wsel = wq_ if kind == "q" else wk_
                        b = self.bank()
                        self.mm_group(b, n, [(wsel(k), self.ucols(k, a0, n)) for k in range(8)],
                                      [wkey] + self.ukeys(a0, n))
                        if kind == "q":
                            dT, dc = self.qT, a0 - 2048
                            dkeys = [("qT", (a0 - 2048) // 512)]
                        else:
                            dT, dc = (self.kTh, a0) if a0 < 2048 else (self.kTm, a0 - 2048)
                            dkeys = self.kkeys(a0, n)
                        z = self.ps[b]
                        sl = rope_i % 2
                        rope_i += 1
                        A, Tm = a32[sl], tmp[sl]
                        t.op("act", lambda: nc.scalar.copy(out=dT[64:128, dc:dc + n], in_=z[64:128, 0:n]),
                             r=[("ps", b)], w=[(kk, "hi") for kk in dkeys] + dkeys)
                        t.op("dve", lambda: nc.vector.tensor_tensor(out=A[0:64, 0:n], in0=z[0:64, 0:n],
                                                                    in1=ropeC[0:64, a0:a0 + n], op=ALU.mult),
                             r=[("ps", b), "ropeC"], w=[("a32", sl)])
                        t.op("dve", lambda: nc.vector.tensor_tensor(out=Tm[0:16, 0:n], in0=z[32:48, 0:n],
                                                                    in1=ropeS[32:48, a0:a0 + n], op=ALU.mult),
                             r=[("ps", b), "ropeS"], w=[("tmp", sl)])
                        t.op("dve", lambda: nc.vector.tensor_tensor(out=Tm[32:48, 0:n], in0=z[0:16, 0:n],
                                                                    in1=ropeS[0:16, a0:a0 + n], op=ALU.mult),
                             r=[("ps", b), "ropeS"], w=[("tmp", sl)])
                        t.op("pool", lambda: nc.gpsimd.tensor_tensor(out=dT[0:64, dc:dc + n], in0=A[0:64, 0:n],
                                                                     in1=Tm[0:64, 0:n], op=ALU.add),
                             r=[("a32", sl), ("tmp", sl)], w=dkeys)
                def emit_v():
                    t.label = "A.v"
                    nb = 16 // d
                    vlist = [(r, j) for r in range(d) for j in range(-1, nb)]
                    for v0 in range(0, len(vlist), 4):
                        grp = vlist[v0:v0 + 4]
                        b = self.bank()
                        rk = [wkey]
                        for gi, (r, j) in enumerate(grp):
                            a0 = 2048 + r + 128 * d * j
                            rk = rk + self.ukeys(a0, 128, d)
                            for k in range(8):
                                t.op("pe", lambda k=k, gi=gi, a0=a0: nc.tensor.matmul(
                                    self.ps[b][:, gi * 128:(gi + 1) * 128], lhsT=self.ucols(k, a0, 128, d),
                                    rhs=wv_(k), start=(k == 0), stop=(k == 7)),
                                    r=rk if k == 0 else (), w=[("ps", b)])
                        t._record(("pe", t.cnt["pe"]), rk, ())
                        ng = len(grp)
                        t.op("act", lambda v0=v0, ng=ng, b=b: nc.scalar.copy(
                            out=Vt[:, v0:v0 + ng, :],
                            in_=self.ps[b][:, 0:ng * 128].rearrange("p (a b) -> p a b", a=ng)),
                            r=[("ps", b)], w=[("V", v0 + i) for i in range(ng)])
                nb = 16 // d
                if h4 == 0 and g == 0:
                    emit_v()
                    emit_qk()
                else:
                    emit_qk()
                    emit_v()
                t.label = "A.attn"
                blocks = [(r, j) for r in range(d) for j in range(nb)]
                LAG = 3
                st = {}
                for it in range(len(blocks) + LAG):
                    if it < len(blocks):
                        r, j = blocks[it]
                        a0 = 2048 + r + 128 * d * j
                        ap_ = a0 - 128 * d
                        b = self.bank()
                        qv = self.qT[:, a0 - 2048:a0 - 2048 + 127 * d + 1:d]
                        qk = [("qT", i) for i in range((a0 - 2048) // 512, (a0 - 2048 + 127 * d) // 512 + 1)]
                        mk = self.mask0_bf if j == 0 else self.mask_bf
                        t.op("pe", lambda: nc.tensor.matmul(self.ps[b][:, 0:256], lhsT=self.ident_bf[:, :],
                                                            rhs=mk[:, :], start=True, stop=False),
                             r=["identbf", "mask"], w=[("ps", b)])
                        t.op("pe", lambda: nc.tensor.matmul(self.ps[b][:, 0:128], lhsT=self.kcols(ap_, 128, d),
                                                            rhs=qv, start=False, stop=False),
                             r=qk + self.kkeys(ap_, 128, d), w=[("ps", b)])
                        t.op("pe", lambda: nc.tensor.matmul(self.ps[b][:, 128:256], lhsT=self.kcols(a0, 128, d),
                                                            rhs=qv, start=False, stop=True),
                             r=qk + self.kkeys(a0, 128, d), w=[("ps", b)])
                        ps_i = blk_i % 4
                        blk_i += 1
                        pm = pms[ps_i]
                        t.op("act", lambda: nc.scalar.activation(out=pm[:, :], in_=self.ps[b][:, 0:256],
                                                                 func=AF.Exp, scale=float(scale)),
                             r=[("ps", b)], w=[("pm", ps_i)])
                        st[it] = (b, ps_i, r, j, a0)
                    if it >= LAG:
                        b, ps_i, r, j, a0 = st.pop(it - LAG)
                        pm = pms[ps_i]
                        vprev = r * (nb + 1) + j
                        vcur = vprev + 1
                        o = self.ps[b][:, 256:384]
                        dn = self.ps[b][:, 384:512]
                        t.op("pe", lambda: nc.tensor.matmul(o, lhsT=Vt[:, vprev, :], rhs=pm[:, 0:128],
                                                            start=True, stop=False),
                             r=[("V", vprev), ("pm", ps_i)], w=[("ps", b)])
                        t.op("pe", lambda: nc.tensor.matmul(o, lhsT=Vt[:, vcur, :], rhs=pm[:, 128:256],
                                                            start=False, stop=True),
                             r=[("V", vcur)], w=[("ps", b)])
                        t.op("pe", lambda: nc.tensor.matmul(dn, lhsT=self.ones_bf[:, :], rhs=pm[:, 0:128],
                                                            start=True, stop=False), r=["ones"], w=[("ps", b)])
                        t.op("pe", lambda: nc.tensor.matmul(dn, lhsT=self.ones_bf[:, :], rhs=pm[:, 128:256],
                                                            start=False, stop=True), r=[("pm", ps_i)], w=[("ps", b)])
                        q0 = a0 - 2048
                        dst = acc[:, :, q0:q0 + 127 * d + 1:d]
                        src = self.ps[b][:, 256:512].rearrange("p (a b) -> p a b", a=2)
                        akeys = [("acc", i) for i in range(q0 // 512, (q0 + 127 * d) // 512 + 1)]
                        if g == 0:
                            t.op("act", lambda: nc.scalar.copy(out=dst, in_=src), r=[("ps", b)], w=akeys)
                        else:
                            t.op("dve", lambda: nc.vector.tensor_tensor(out=dst, in0=src, in1=dst, op=ALU.add),
                                 r=[("ps", b)] + akeys, w=akeys)
            t.label = "A.fin"
            for tt in range(4):
                sl_ = slice(tt * 512, (tt + 1) * 512)
                t.op("act", lambda: nc.scalar.activation(out=acc[:, 1, sl_], in_=acc[:, 1, sl_], func=AF.Ln),
                     r=[("acc", tt)], w=[("acc", tt)])
                t.op("act", lambda: nc.scalar.activation(out=acc[:, 1, sl_], in_=acc[:, 1, sl_], func=AF.Exp,
                                                         scale=-1.0),
                     r=[("acc", tt)], w=[("acc", tt)])
                t.op("dve", lambda: nc.vector.tensor_tensor(out=self.attnT[:, h4, sl_], in0=acc[:, 0, sl_],
                                                            in1=acc[:, 1, sl_], op=ALU.mult),
                     r=[("acc", tt)], w=[("attnT", tt)])
        if self.stop_after == "pA":
            return self.finish_dbg([(self.attnT, 4 * 2048, BF16)])
        t.barrier(waiters=("act", "dve", "sp"))

        conv = self.at(oD, [128, 6, 1024], F32)
        cglu = [self.at(oD + 24576 + i * 2176, [128, 1056], BF16) for i in range(2)]
        diags = [self.at(oD + 28928 + i * 7936, [128, 31, 128], BF16) for i in range(2)]
        sg = [self.at(oD + 44800 + i * 2048, [128, 512], F32) for i in range(2)]
        xbs = [self.at(oD + 48896 + i * 1024, [128, 512], BF16) for i in range(4)]
        xsqs = [self.at(oD + 55040 + i * 1024, [128, 512], BF16) for i in range(4)]
        st_pend = []
        self.bank_pool = [0, 1, 2, 3]
        st_i = 0
        mean = self.at(oD + 61184, [128, 512], F32)
        var = self.at(oD + 63232, [128, 512], F32)
        lnv = self.at(oD + 65280, [128, 512], F32)
        rstd = self.at(oD + 67328, [128, 512], F32)
        t1 = [self.at(oD + 69376, [128, 512], F32), self.at(oD + 52992, [128, 512], F32)]
        t2 = [self.at(oD + 71424, [128, 512], F32), self.at(oD + 59136, [128, 512], F32)]
        ln_pending = []
        dg_i = 0
        sg_i = 0
        wst = {}
        for half in range(2):
            base_a = 2048 + half * 1024

            def glu_part(jj, half=half, base_a=base_a):
                nonlocal dg_i, sg_i
                if jj % 2 == 0:
                    wst["w"] = self.w_next()
                wt, wkey = wst["w"]
                wa = lambda k, o=(jj % 2) * 2048: wt[:, o + k * 128:o + (k + 1) * 128]
                wb = lambda k, o=(jj % 2) * 2048 + 1024: wt[:, o + k * 128:o + (k + 1) * 128]
                cg = cglu[jj % 2]
                ckey = ("cglu", jj % 2)
                t.label = "C.diag"
                diag = diags[dg_i % 2]
                dgk = ("diag", dg_i % 2)
                dg_i += 1
                t.op("dve", lambda: nc.vector.tensor_tensor(
                    out=diag[:, :, :], in0=self.ident_bf[:, :].unsqueeze(1).broadcast_to([128, 31, 128]),
                    in1=self.consts[:, C_CONVW + jj * 31:C_CONVW + (jj + 1) * 31].unsqueeze(2).broadcast_to(
                        [128, 31, 128]), op=ALU.mult),
                    r=["identbf", "consts"], w=[dgk])
                t.label = "C.glu"
                for (a0, n, c0) in ((base_a - 32, 32, 0), (base_a, 512, 32), (base_a + 512, 512, 544)):
                    ba = self.bank()
                    self.mm_group(ba, n, [(wa(k), self.ucols(k, a0, n)) for k in range(8)],
                                  [wkey] + self.ukeys(a0, n))
                    bb = self.bank()
                    self.mm_group(bb, n, [(wb(k), self.ucols(k, a0, n)) for k in range(8)],
                                  [wkey] + self.ukeys(a0, n))
                    s_ = sg[sg_i % 2]
                    sk = ("sg", sg_i % 2)
                    sg_i += 1
                    t.op("act", lambda: nc.scalar.activation(out=s_[:, 0:n], in_=self.ps[bb][:, 0:n],
                                                             func=AF.Sigmoid), r=[("ps", bb)], w=[sk])
                    t.op("dve", lambda: nc.vector.tensor_tensor(out=cg[:, c0:c0 + n], in0=self.ps[ba][:, 0:n],
                                                                in1=s_[:, 0:n], op=ALU.mult),
                         r=[("ps", ba), sk], w=[ckey])
                return cg, ckey, diag, dgk

            def conv_part(jj, ctx, mid_hook=None):
                nonlocal st_i
                cg, ckey, diag, dgk = ctx
                t.label = "C.conv"
                for tt in range(2):
                    b = self.bank()
                    self.mm_group(b, 512, [(diag[:, tap, :], cg[:, 2 + tt * 512 + tap: 2 + tt * 512 + tap + 512])
                                           for tap in range(NTAP_PE)], [dgk, ckey])
                    cv_ = conv[:, jj, tt * 512:(tt + 1) * 512]
                    t.op("act", lambda b=b: nc.scalar.activation(
                        out=cv_, in_=self.ps[b][:, :], func=AF.Identity,
                        bias=self.cc(C_CONVB + jj)), r=[("ps", b), "consts"], w=[("conv", tt)])
                    for tap in range(NTAP_PE, 31):
                        t.op("dve", lambda tap=tap: nc.vector.scalar_tensor_tensor(
                            out=cv_, in0=cg[:, 2 + tt * 512 + tap: 2 + tt * 512 + tap + 512],
                            scalar=self.cc(C_CONVW + jj * 31 + tap), in1=cv_, op0=ALU.mult, op1=ALU.add),
                            r=[ckey, ("conv", tt), "consts"], w=[("conv", tt)])
                    xb_, xq_ = xbs[st_i % 4], xsqs[st_i % 4]
                    kb_, kq_ = ("xb", st_i % 4), ("xsq", st_i % 4)
                    st_i += 1
                    t.op("act", lambda: nc.scalar.copy(out=xb_[:, :], in_=cv_), r=[("conv", tt)], w=[kb_])
                    t.op("act", lambda: nc.scalar.activation(out=xq_[:, :], in_=cv_, func=AF.Square),
                         r=[("conv", tt)], w=[kq_])

                    def _stats(tt=tt, jj=jj, xb_=xb_, xq_=xq_, kb_=kb_, kq_=kq_):
                        t.op("pe", lambda: nc.tensor.matmul(self.ps[4 + tt][:, :], lhsT=self.ones_bf[:, :],
                                                            rhs=xb_[:, :], start=(jj == 0), stop=(jj == 5)),
                             r=[kb_, "ones"], w=[("ps", 4 + tt)])
                        t.op("pe", lambda: nc.tensor.matmul(self.ps[6 + tt][:, :], lhsT=self.ones_bf[:, :],
                                                            rhs=xq_[:, :], start=(jj == 0), stop=(jj == 5)),
                             r=[kq_], w=[("ps", 6 + tt)])
                    st_pend.append(_stats)
                    if len(st_pend) > 2:
                        st_pend.pop(0)()
                    if tt == 0 and mid_hook is not None:
                        mid_hook()
                        t.label = "C.conv"

            ctx_next = glu_part(0)
            for jj in range(6):
                ctx = ctx_next
                if jj + 1 < 6:
                    ctx_next = glu_part(jj + 1)
                hook = None
                if jj == 0 and ln_pending:
                    ln_pending.pop(0)()
                    hook = ln_pending.pop(0)
                conv_part(jj, ctx, hook)
            while st_pend:
                st_pend.pop(0)()
            def ln_stage(tt, half=half):
                t.label = "C.ln"
                if True:
                    b1 = 4 + tt
                    b2 = 6 + tt
                    t.op("dve", lambda: nc.vector.tensor_scalar(out=mean[:, :], in0=self.ps[b1][:, :],
                                                                scalar1=1.0 / 768, scalar2=None, op0=ALU.mult),
                         r=[("ps", b1)], w=["mean"])
                    t.op("dve", lambda: nc.vector.tensor_tensor(out=var[:, :], in0=mean[:, :], in1=mean[:, :],
                                                                op=ALU.mult), r=["mean"], w=["var"])
                    t.op("dve", lambda: nc.vector.scalar_tensor_tensor(out=var[:, :], in0=self.ps[b2][:, :],
                                                                       scalar=1.0 / 768, in1=var[:, :],
                                                                       op0=ALU.mult, op1=ALU.subtract),
                         r=[("ps", b2), "var"], w=["var"])
                    t.op("act", lambda: nc.scalar.activation(out=lnv[:, :], in_=var[:, :], func=AF.Ln,
                                                             bias=self.cc(C_EPS)), r=["var"], w=["lnvc"])
                    t.op("act", lambda: nc.scalar.activation(out=rstd[:, :], in_=lnv[:, :], func=AF.Exp, scale=-0.5),
                         r=["lnvc"], w=["rstdc"])
                    for jj in range(6):
                        a_, b_ = t1[jj % 2], t2[jj % 2]
                        t.op("dve", lambda: nc.vector.tensor_tensor(out=a_[:, :], in0=conv[:, jj, tt * 512:(tt + 1) * 512],
                                                                    in1=mean[:, :], op=ALU.subtract),
                             r=[("conv", tt), "mean"], w=[("t1", jj % 2)])
                        t.op("dve", lambda: nc.vector.scalar_tensor_tensor(out=b_[:, :], in0=a_[:, :],
                                                                           scalar=self.cc(C_LNG + jj), in1=rstd[:, :],
                                                                           op0=ALU.mult, op1=ALU.mult),
                             r=[("t1", jj % 2), "rstdc"], w=[("t2", jj % 2)])
                        c0 = half * 1024 + tt * 512
                        t.op("act", lambda: nc.scalar.activation(out=self.cT[:, jj, c0:c0 + 512], in_=b_[:, :],
                                                                 func=AF.Silu, bias=self.cc(C_LNB + jj)),
                             r=[("t2", jj % 2)], w=[("cT", c0 // 512)])
            ln_pending.append(lambda f=ln_stage: f(0))
            ln_pending.append(lambda f=ln_stage: f(1))
        while len(ln_pending) > 1:
            ln_pending.pop(0)()
        self.bank_pool = [0, 1, 2, 3, 4, 6]
        self._bank = 0
        if self.stop_after == "pC":
            return self.finish_dbg([(self.cT, 6 * 2048, BF16)])
        t.barrier(waiters=("sp",))
        _selfsync(t)
        xs1 = [self.at(oD + 32768 + i * 4096, [128, 1024], F32) for i in range(4)]
        self._xs_i = 0
        pre_mem = [self.lt_issue(self.memd[s_ * 128:(s_ + 1) * 128, :], xs1, "xb") for s_ in range(2)]
        pre_x0 = [self.lt_issue(self.xh[HALO + s_ * 128: HALO + (s_ + 1) * 128, :], xs1, "xb") for s_ in range(2)]

        t.label = "E"
        mergedT = self.at(oU, [128, 8, 2048], BF16)
        sa = [self.at(oD + 24576, [128, 512], F32)] * 2
        sb = [self.at(oD + 26624, [128, 512], F32)] * 2
        e1 = [self.at(oD + 28672, [128, 512], F32)] * 2
        e2 = [self.at(oD + 30720, [128, 512], F32)] * 2
        ei = 0
        ckT = self.at(oD + 63488, [128, 8, 256], BF16)
        cV = self.at(oD + 67584, [128, 2, 1024], BF16)

        def m_block(mi):
            t.label = "M"
            wt_, wkey_ = self.w_next()
            if mi < 2:
                for jj in range(4):
                    j_ = mi * 4 + jj
                    b = self.bank()
                    self.mm_group(b, 256, [(wt_[:, jj * 1024 + k * 128: jj * 1024 + (k + 1) * 128], mT[:, k, :])
                                           for k in range(8)], [wkey_, "mT"])
                    t.op("act", lambda j_=j_, b=b: nc.scalar.copy(out=ckT[:, j_, :], in_=self.ps[b][:, 0:256]),
                         r=[("ps", b)], w=["ckT"])
            else:
                blk = mi - 2
                for mc in range(2):
                    b = self.bank()
                    self.mm_group(b, 512, [(mT[:, k, mc * 128:(mc + 1) * 128], wt_[:, k * 512:(k + 1) * 512])
                                           for k in range(8)], [wkey_, "mT"])
                    t.op("act", lambda mc=mc, b=b, blk=blk: nc.scalar.copy(
                        out=cV[:, mc, blk * 512:(blk + 1) * 512], in_=self.ps[b][:, :]), r=[("ps", b)], w=["cV"])
            t.label = "E"
        memT = self.at(oD + 0, [128, 8, 256], F32)
        mT = self.at(oD + 8192, [128, 8, 256], BF16)
        sq_m = self.at(oD + 49152, [128, 8, 512], BF16)
        lnv_m = self.at(oD + 57344, [128, 512], F32)
        rstd_m = self.at(oD + 59392, [128, 512], F32)
        for j in range(8):
            if j == 4:
                _selfsync(t)
                t.label = "M"
                self.load_transpose(lambda s: self.memd[s * 128:(s + 1) * 128, :], xs1, 2, memT,
                                    lambda s, hf: [("memT", hf)], "xb", pre=pre_mem)
                self.norm(memT[:, :, :], lambda k: memT[:, k, :], [("memT", 0), ("memT", 1)], C_GMEM,
                          lambda k: mT[:, k, :], ["mT"], 256, sq_m, lnv_m, rstd_m, "pm")
                t.label = "E"
            wt, wkey = self.w_next()
            wga = lambda k: wt[:, k * 128:(k + 1) * 128]
            wap = lambda k: wt[:, 1024 + k * 128:1024 + (k + 1) * 128]
            wgb = lambda k: wt[:, 1536 + k * 128:1536 + (k + 1) * 128]
            wcp = lambda k: wt[:, 2560 + k * 128:2560 + (k + 1) * 128]
            for tt in range(4):
                cs = slice(tt * 512, (tt + 1) * 512)
                bga = self.bank()
                self.mm_group(bga, 512, [(wga(k), self.uTm[:, k, cs]) for k in range(8)], [wkey, ("uT", 4 + tt)])
                bya = self.bank()
                self.mm_group(bya, 512, [(wap(k), self.attnT[:, k, cs]) for k in range(4)], [wkey, ("attnT", tt)])
                bgb = self.bank()
                self.mm_group(bgb, 512, [(wgb(k), self.uTm[:, k, cs]) for k in range(8)], [wkey, ("uT", 4 + tt)])
                byc = self.bank()
                self.mm_group(byc, 512, [(wcp(k), self.cT[:, k, cs]) for k in range(6)], [wkey, ("cT", tt)])
                s = 0
                ei += 1
                t.op("act", lambda: nc.scalar.activation(out=sa[s][:, :], in_=self.ps[bga][:, :], func=AF.Sigmoid,
                                                         bias=self.cc(C_BGA + j)), r=[("ps", bga)], w=[("sa", s)])
                t.op("act", lambda: nc.scalar.activation(out=sb[s][:, :], in_=self.ps[bgb][:, :], func=AF.Sigmoid,
                                                         bias=self.cc(C_BGB + j)), r=[("ps", bgb)], w=[("sb", s)])
                t.op("dve", lambda: nc.vector.tensor_tensor(out=e1[s][:, :], in0=self.ps[bya][:, :], in1=sa[s][:, :],
                                                            op=ALU.mult), r=[("ps", bya), ("sa", s)], w=[("e1", s)])
                t.op("dve", lambda: nc.vector.tensor_tensor(out=e2[s][:, :], in0=self.ps[byc][:, :], in1=sb[s][:, :],
                                                            op=ALU.mult), r=[("ps", byc), ("sb", s)], w=[("e2", s)])
                t.op("pool", lambda: nc.gpsimd.tensor_tensor(out=mergedT[:, j, cs], in0=e1[s][:, :], in1=e2[s][:, :],
                                                             op=ALU.add), r=[("e1", s), ("e2", s)], w=[("mg", tt)])
                if j == 0 and tt == 1 and ln_pending:
                    ln_pending.pop(0)()
                    t.label = "E"
                    self.bank_pool = list(range(8))
            if j >= 4:
                m_block(j - 4)
        if self.stop_after == "pE":
            return self.finish_dbg([(mergedT, 8 * 2048, BF16)])

        t.label = "M"
        hT = self.at(oD, [128, 16, 1024], BF16)
        bufA = self.at(oU + 32768, [128, 8, 1024], BF16)
        bufB = self.at(oU + 49152, [128, 8, 1024], BF16)
        xT = self.at(oB, [128, 8, 1024], F32)

        sq = self.at(oD + 49152, [128, 8, 512], BF16)
        lnv = self.at(oD + 57344, [128, 512], F32)
        rstdn = [self.at(oD + 59392 + i * 2048, [128, 512], F32) for i in range(2)]
        rstd = rstdn[0]
        ptc = [self.at(oB + 32768 + i * 1024, [128, 512], BF16) for i in range(4)]
        rl = [self.at(oB + 36864 + i * 2048, [128, 512], F32) for i in range(2)]
        rds = rl
        grow = self.at(oD + 71680, [128, 1024], F32)
        _selfsync(t)
        ri = 0
        TS = [slice(0, 512), slice(512, 1024)]
        for half in range(2):
            h0 = half * 1024
            t.label = "F0"
            f0_rows = lambda s: self.xh[HALO + h0 + s * 128: HALO + h0 + (s + 1) * 128, :]
            f0_keys = lambda s, hf: [("xTl", s // 4, hf), ("xT", s // 4)]
            self.load_transpose(f0_rows, xs1, 4, xT, f0_keys, "xb", pre=(pre_x0 if half == 0 else pre_x1))

            def f0_second():
                t.label = "F0"
                self.load_transpose(f0_rows, xs1, 4, xT, f0_keys, "xb", s_off=4)
                t.label = "F1"

            def proj_res(src, skeys, nk=8, tile_outer=True, mid_hook=None):
                per_blk = WBLK // (nk * 128)
                if tile_outer:
                    blks = self.w_take(8 // per_blk)
                    order = [(j, tt) for tt in range(2) for j in range(8)]
                else:
                    blks = None
                    order = [(j, tt) for j in range(8) for tt in range(2)]
                cur = None
                for (j, tt) in order:
                    if mid_hook is not None and tt == 1 and j == 0:
                        mid_hook()
                    if tile_outer:
                        wt_, wk_ = blks[j // per_blk]
                    else:
                        if j % per_blk == 0 and tt == 0:
                            cur = self.w_next()
                        wt_, wk_ = cur
                    o = (j % per_blk) * nk * 128
                    cs = TS[tt]
                    b = self.bank()
                    self.mm_group(b, 512, [(wt_[:, o + k * 128:o + (k + 1) * 128], src(k, cs)) for k in range(nk)],
                                  [wk_, skeys(tt)])
                    t.op("dve", lambda j=j, cs=cs, b=b: nc.vector.tensor_tensor(
                        out=xT[:, j, cs], in0=self.ps[b][:, :], in1=xT[:, j, cs], op=ALU.add),
                        r=[("ps", b), ("xT", tt), ("xTl", tt, 0), ("xTl", tt, 1)], w=[("xT", tt)])

            t.label = "F1"
            proj_res(lambda k, cs: mergedT[:, k, h0 + cs.start:h0 + cs.stop], lambda tt: ("mg", 0), mid_hook=f0_second)
            if self.stop_after == "F1":
                return self.finish_dbg([(xT, 8 * 1024, F32)])
            t.label = "F2"
            nb_ = {}
            for tt in range(2):
                cs = TS[tt]
                nb_[tt] = self.norm_split(xT[:, :, cs], lambda k, cs=cs: xT[:, k, cs], [("xT", tt)], C_GCROSS,
                                          lambda k, cs=cs: bufA[:, k, cs], [("bufA", tt)], 512, sq, lnv,
                                          rstdn[tt], ("rstdn", tt), "f2")
            nb_[0][0]()
            t.label = "F3"
            blks = self.w_take(2)
            for tt in range(2):
                cs = TS[tt]
                for j in range(8):
                    wt, wkey = blks[j // 4]
                    o = (j % 4) * 1024
                    b = self.bank()
                    self.mm_group(b, 512, [(wt[:, o + k * 128:o + (k + 1) * 128], bufA[:, k, cs]) for k in range(8)],
                                  [wkey, ("bufA", tt)])
                    if j == 0:
                        pend_ev = []
                    pend_ev.append(lambda j=j, cs=cs, b=b, tt=tt: t.op(
                        "dve", lambda: nc.vector.tensor_tensor(
                            out=bufB[:, j, cs], in0=self.ps[b][:, :], in1=rstdn[tt][:, :], op=ALU.mult),
                        r=[("ps", b), ("rstdn", tt)], w=[("bufB", tt)]))
                    if j == 2:
                        nb_[tt][1]()
                        if tt == 0:
                            nb_[1][0]()
                    if j >= 2:
                        while pend_ev:
                            pend_ev.pop(0)()
            t.label = "F4"
            items = [(tt, hc) for tt in range(2) for hc in range(4)]
            pend = {}
            pi = 0
            for it in range(len(items) + 1):
                if it < len(items):
                    tt, hc = items[it]
                    cs = TS[tt]
                    pp = []
                    for mc in range(2):
                        b = self.bank()
                        self.mm_group(b, 512, [(ckT[:, 2 * hc + e, mc * 128:(mc + 1) * 128], bufB[:, 2 * hc + e, cs])
                                               for e in range(2)], ["ckT", ("bufB", tt)])
                        p_ = ptc[pi % 4]
                        pk = ("ptc", pi % 4)
                        pi += 1
                        t.op("act", lambda p_=p_, b=b: nc.scalar.activation(out=p_[:, :], in_=self.ps[b][:, :],
                                                                            func=AF.Exp, scale=1.0 / 16.0),
                             r=[("ps", b)], w=[pk])
                        pp.append((p_, pk))
                    pend[it] = (tt, hc, pp)
                if it >= 1:
                    tt, hc, pp = pend.pop(it - 1)
                    cs = TS[tt]
                    bd = self.bank()
                    self.mm_group(bd, 512, [(self.ones_bf[:, :], pp[mc][0][:, :]) for mc in range(2)],
                                  [pp[0][1], pp[1][1], "ones"])
                    rdk = ("rd", it % 2)
                    rd_ = rds[it % 2]
                    t.op("act", lambda bd=bd, rd_=rd_: nc.scalar.activation(out=rd_[:, :], in_=self.ps[bd][:, :],
                                                                            func=AF.Ln), r=[("ps", bd)], w=[rdk])
                    t.op("act", lambda rd_=rd_: nc.scalar.activation(out=rd_[:, :], in_=rd_[:, :], func=AF.Exp,
                                                                     scale=-1.0), r=[rdk], w=[rdk])
                    for e in range(2):
                        bo = self.bank()
                        ec = 2 * hc + e
                        self.mm_group(bo, 512, [(cV[:, mc, ec * 128:(ec + 1) * 128], pp[mc][0][:, :])
                                                for mc in range(2)], ["cV", pp[0][1], pp[1][1]])
                        t.op("dve", lambda ec=ec, cs=cs, bo=bo, rd_=rd_: nc.vector.tensor_tensor(
                            out=bufA[:, ec, cs], in0=self.ps[bo][:, :], in1=rd_[:, :], op=ALU.mult),
                            r=[("ps", bo), rdk], w=[("bufA", tt)])
            t.label = "F5"
            proj_res(lambda k, cs: bufA[:, k, cs], lambda tt: ("bufA", tt))
            if self.stop_after == "F5":
                return self.finish_dbg([(xT, 8 * 1024, F32)])
            t.label = "G1"
            nb_ = {}
            for tt in range(2):
                cs = TS[tt]
                nb_[tt] = self.norm_split(xT[:, :, cs], lambda k, cs=cs: xT[:, k, cs], [("xT", tt)], C_GMLP,
                                          lambda k, cs=cs: bufB[:, k, cs], [("bufB", tt)], 512, sq, lnv,
                                          rstdn[tt], ("rstdn", tt), "g1")
            nb_[0][0]()
            r_i = 0
            for fh in range(2):
                t.label = "G2"
                for fb in range(4):
                    wt, wkey = self.w_next()
                    for tt in range(2):
                        cs = TS[tt]
                        for fc in range(4):
                            f = fb * 4 + fc
                            b = self.bank()
                            self.mm_group(b, 512, [(wt[:, fc * 1024 + k * 128: fc * 1024 + (k + 1) * 128],
                                                    bufB[:, k, cs]) for k in range(8)], [wkey, ("bufB", tt)])
                            first_ = (fh == 0 and fb == 0)
                            if fc == 0:
                                pend_ev = []

                            def _ev(b=b, tt=tt, f=f, cs=cs):
                                nonlocal r_i
                                r_ = rl[r_i % 2]
                                rk_ = ("rl", r_i % 2)
                                r_i += 1
                                t.op("dve", lambda: nc.vector.scalar_tensor_tensor(
                                    out=r_[:, :], in0=self.ps[b][:, :], scalar=0.0, in1=rstdn[tt][:, :],
                                    op0=ALU.max, op1=ALU.mult),
                                    r=[("ps", b), ("rstdn", tt)], w=[rk_])
                                t.op("act", lambda: nc.scalar.activation(
                                    out=hT[:, f, cs], in_=r_[:, :], func=AF.Square), r=[rk_], w=[("hT", tt)])
                            pend_ev.append(_ev)
                            if first_ and fc == 2:
                                nb_[tt][1]()
                                if tt == 0:
                                    nb_[1][0]()
                            if (not first_) or fc >= 2:
                                while pend_ev:
                                    pend_ev.pop(0)()
                t.label = "G3"
                proj_res(lambda k, cs: hT[:, k, cs], lambda tt: ("hT", tt), nk=16, tile_outer=False)
            if self.stop_after == "G3":
                return self.finish_dbg([(xT, 8 * 1024, F32)])
            if half == 0:
                t.dma("sp", lambda: nc.sync.dma_start(out=grow[:, :], in_=self.growd), stream="c3", w=["grow"])
                pre_x1 = [self.lt_issue(self.xh[HALO + 1024 + s_ * 128: HALO + 1024 + (s_ + 1) * 128, :], xs1, "xb")
                          for s_ in range(4)]
            t.label = "H"
            ident = self.consts[:, C_ID:C_ID + 128]
            sqh = [self.at(oU + 32768, [128, 8, 512], BF16), self.at(oU + 49152, [128, 8, 512], BF16)]
            ogs = [self.at(oU + 32768 + 8192 + i * 4096, [128, 1024], F32) for i in range(2)] + \
                  [self.at(oU + 49152 + 8192 + i * 4096, [128, 1024], F32) for i in range(2)]
            ogkeys = [[("og", i), ("bufA" if i < 2 else "bufB", 0), ("bufA" if i < 2 else "bufB", 1)]
                      for i in range(4)]
            self._og_i = 0
            for tt in range(2):
                t.op("act", lambda tt=tt: nc.scalar.activation(out=sqh[tt][:, :, :], in_=xT[:, :, TS[tt]],
                                                               func=AF.Square),
                     r=[("xT", tt)], w=[("sqh", tt)])
            for tt in range(2):
                cs = TS[tt]
                rs_ = rstdn[tt]
                rk = ("rstdn", tt)
                sqx = sqh[tt]

                def h_stats():
                    bs = self.bank()
                    self.mm_group(bs, 512, [(self.ones_bf[:, :], sqx[:, k, :]) for k in range(8)], [("sqh", tt)])
                    t.op("act", lambda: nc.scalar.activation(out=lnv[:, :], in_=self.ps[bs][:, :], func=AF.Ln,
                                                             bias=self.cc(C_EPS), scale=1.0 / D),
                         r=[("ps", bs)], w=[("lnv", "h")])
                    t.op("act", lambda: nc.scalar.activation(out=rs_[:, :], in_=lnv[:, :], func=AF.Exp, scale=-0.5),
                         r=[("lnv", "h")], w=[rk])

                def x_tr(s, tt=tt, cs=cs):
                    bl = []
                    for hf in range(2):
                        b = self.bank()
                        for kk in range(4):
                            k = hf * 4 + kk
                            c0 = cs.start + s * 128
                            t.op("pe", lambda k=k, kk=kk, b=b, c0=c0: nc.tensor.transpose(
                                self.ps[b][:, kk * 128:(kk + 1) * 128], xT[:, k, c0:c0 + 128], ident),
                                r=[("xT", tt)], w=[("ps", b)])
                        bl.append(b)
                    return bl

                def x_ev(s, bl, tt=tt):
                    osl = self._og_i % 4
                    self._og_i += 1
                    og = ogs[osl]
                    okeys = ogkeys[osl]
                    for hf in range(2):
                        b = bl[hf]
                        t.op("dve", lambda b=b, hf=hf: nc.vector.scalar_tensor_tensor(
                            out=og[:, hf * 512:(hf + 1) * 512], in0=self.ps[b][:, :],
                            scalar=self.rcols[:, tt * 4 + s:tt * 4 + s + 1],
                            in1=grow[:, hf * 512:(hf + 1) * 512], op0=ALU.mult, op1=ALU.mult),
                            r=[("ps", b), ("rcols", tt), "grow"], w=okeys)
                    r0 = h0 + tt * 512 + s * 128
                    t.dma("sp", lambda: nc.sync.dma_start(out=self.outd[r0:r0 + 128, :], in_=og[:, :]),
                          stream=f"o{osl}", r=okeys, w=[("outd", r0)])

                pend = [(s, x_tr(s)) for s in range(2)]
                h_stats()
                br = self.bank()
                for s in range(4):
                    t.op("pe", lambda s=s: nc.tensor.transpose(self.ps[br][:, s * 128:(s + 1) * 128],
                                                               rs_[:, s * 128:(s + 1) * 128], ident),
                         r=[rk], w=[("ps", br)])
                t.op("act", lambda: nc.scalar.copy(out=self.rcols[:, tt * 4:tt * 4 + 4],
                                                   in_=self.ps[br][:, 0:512:128]),
                     r=[("ps", br)], w=[("rcols", tt)])
                for s in range(2, 4):
                    s_, bl_ = pend.pop(0)
                    x_ev(s_, bl_)
                    pend.append((s, x_tr(s)))
                while pend:
                    s_, bl_ = pend.pop(0)
                    x_ev(s_, bl_)
        t.barrier(waiters=("sp",), skip_prefix="dma_w")
        return nc

    def finish_dbg(self, items):
        nc, t = self.nc, self.t
        t.barrier()
        stage = self.at(self.oD + 73728, [128, 512], F32)
        off = 0
        for (tens, n, dt) in items:
            for c0 in range(0, n, 512):
                m = min(512, n - c0)
                src = self._flat(tens, c0, m)
                t.op("act", lambda: nc.scalar.copy(out=stage[:, 0:m], in_=src), w=["stg"])
                t.dma("sp", lambda: nc.sync.dma_start(out=self.dbgd[:, off + c0: off + c0 + m], in_=stage[:, 0:m]),
                      stream="dbg", r=["stg"], w=[("dbgo", off + c0)])
                t.barrier()
            off += n
        t.barrier(waiters=("sp",))
        return nc

    def _flat(self, tens, c0, m):
        shp = list(tens.shape)
        if len(shp) == 2:
            return tens[:, c0:c0 + m]
        inner = shp[2]
        if inner >= m:
            assert inner % m == 0
            return tens[:, c0 // inner, (c0 % inner):(c0 % inner) + m]
        assert c0 % inner == 0 and m % inner == 0
        return tens[:, c0 // inner:(c0 + m) // inner, :]


def _chunk(W, c0, ncols=128, cols=None):
    if cols is None:
        cols = np.arange(c0, c0 + ncols)
    sub = W[:, cols]
    K = sub.shape[0]
    return sub.reshape(K // 128, 128, len(cols)).transpose(1, 0, 2).reshape(128, -1)


def _pack_weights(inp):
    w_in = inp["w_in"][0]
    blocks = np.zeros((NWB, 128, WBLK), np.float32)

    def put(bi, off, arr):
        blocks[bi, :, off:off + arr.shape[1]] = arr

    bi = WB_A
    for h4 in range(4):
        for g in range(3):
            head = g * 4 + h4
            put(bi, 0, _chunk(w_in, 0, cols=head * 128 + PERM))
            put(bi, 1024, _chunk(w_in, 0, cols=1536 + head * 128 + PERM))
            put(bi, 2048, _chunk(w_in, 3072 + head * 128))
            bi += 1
    for i in range(3):
        for jj in range(2):
            j = i * 2 + jj
            put(WB_C + i, jj * 2048, _chunk(w_in, 4608 + j * 128))
            put(WB_C + i, jj * 2048 + 1024, _chunk(w_in, 4608 + 768 + j * 128))
    wap = inp["w_attn_proj"][0]
    wcp = inp["w_conv_proj"][0]
    for j in range(8):
        put(WB_E + j, 0, _chunk(w_in, 6144 + j * 128))
        put(WB_E + j, 1024, _chunk(wap, j * 128))
        put(WB_E + j, 1536, _chunk(w_in, 7168 + j * 128))
        put(WB_E + j, 2560, _chunk(wcp, j * 128))
    wckv = inp["w_ckv"][0]
    for j in range(8):
        put(WB_M + j // 4, (j % 4) * 1024, _chunk(wckv, j * 128))
    for i in range(2):
        put(WB_M + 2 + i, 0, _chunk(wckv, 1024 + i * 512, ncols=512))
    for (wb, name) in ((WB_OUT, "w_out"), (WB_CQ, "w_cq"), (WB_CO, "w_co")):
        W = inp[name][0]
        for j in range(8):
            put(wb + j // 4, (j % 4) * 1024, _chunk(W, j * 128))
    wup = inp["w_up"][0]
    wdn = inp["w_down"][0]
    bi = WB_UP
    for fh in range(2):
        for fb in range(4):
            for fc in range(4):
                f = fh * 16 + fb * 4 + fc
                put(bi, fc * 1024, _chunk(wup, f * 128))
            bi += 1
        for jb in range(4):
            for jc in range(2):
                j = jb * 2 + jc
                put(bi, jc * 2048, _chunk(wdn[fh * 2048:(fh + 1) * 2048], j * 128))
            bi += 1
    assert bi == NWB
    return blocks


def _vec8(v):
    return v.reshape(-1, 128).T


def _consts(inp, flag):
    c = np.zeros((128, CW), np.float32)
    c[:, C_ID:C_ID + 128] = np.eye(128, dtype=np.float32)
    kk = np.arange(128)[:, None]
    qq = np.arange(128)[None, :]
    NEG = np.float32(-30000.0)
    prev = np.where(kk >= qq, np.float32(0.0), NEG).astype(np.float32)
    cur = np.where(kk <= qq, np.float32(0.0), NEG).astype(np.float32)
    c[:, C_MASK:C_MASK + 128] = prev
    c[:, C_MASK + 128:C_MASK + 256] = cur
    c[:, C_MASK0:C_MASK0 + 128] = prev if flag else NEG
    c[:, C_MASK0 + 128:C_MASK0 + 256] = cur
    c[:, C_EPS] = EPS
    c[:, C_GMIX:C_GMIX + 8] = _vec8(inp["g_mix"][0])
    c[:, C_GCROSS:C_GCROSS + 8] = _vec8(inp["g_cross"][0])
    c[:, C_GMEM:C_GMEM + 8] = _vec8(inp["g_mem"][0])
    c[:, C_GMLP:C_GMLP + 8] = _vec8(inp["g_mlp"][0])
    c[:, C_GFIN:C_GFIN + 8] = _vec8(inp["g_final"])
    c[:, C_BGA:C_BGA + 8] = _vec8(inp["b_gate"][0][:1024])
    c[:, C_BGB:C_BGB + 8] = _vec8(inp["b_gate"][0][1024:])
    c[:, C_CONVB:C_CONVB + 6] = _vec8(inp["conv_b"][0])
    c[:, C_LNG:C_LNG + 6] = _vec8(inp["conv_ln_g"][0])
    c[:, C_LNB:C_LNB + 6] = _vec8(inp["conv_ln_b"][0])
    cw = inp["conv_w"][0]
    for j in range(6):
        c[:, C_CONVW + j * 31:C_CONVW + (j + 1) * 31] = cw[:, j * 128:(j + 1) * 128].T
    return c


def _rope_tables(pos0):
    pos = (pos0 + np.arange(4096)).astype(np.float64)
    inv = 500000.0 ** (-np.arange(0, 32, 2, dtype=np.float64) / 32.0)
    ang = pos[None, :] * inv[:, None]
    cs, sn = np.cos(ang).astype(np.float32), np.sin(ang).astype(np.float32)
    C = np.ones((64, 4096), np.float32)
    S = np.zeros((64, 4096), np.float32)
    C[0:16] = cs
    C[32:48] = cs
    S[0:16] = sn
    S[32:48] = -sn
    return C, S


_CACHE = {}


def _get_nc(stop_after=None, dbg=False):
    key = (stop_after, dbg)
    if key not in _CACHE:
        _CACHE[key] = Kern(stop_after, dbg).build()
    return _CACHE[key]


def _in_maps(inputs):
    inp = {k: np.asarray(v, dtype=np.float32) for k, v in inputs.items()}
    wp = _pack_weights(inp)
    x, mem = inp["x"], inp["mem"]
    maps = []
    for c in range(NCORES):
        b, q = c // 4, c % 4
        main = x[b, q * T:(q + 1) * T]
        halo = x[b, (q - 1) * T:q * T] if q > 0 else np.zeros((HALO, D), np.float32)
        C, S = _rope_tables(q * T - HALO)
        maps.append({
            "xh": np.ascontiguousarray(np.concatenate([halo, main], axis=0)),
            "memb": np.ascontiguousarray(mem[b]),
            "consts": _consts(inp, q > 0),
            "ropeC": C, "ropeS": S,
            "wpack": wp,
            "grow": np.ascontiguousarray(np.broadcast_to(inp["g_final"][None, :], (128, D))),
        })
    return maps


def kernel(**inputs):
    nc = _get_nc()
    maps = _in_maps(inputs)
    res = run_bass_kernel_spmd(nc, maps, core_ids=list(range(NCORES)))
    out = np.zeros((2, 4 * T, D), np.float32)
    for c in range(NCORES):
        out[c // 4, (c % 4) * T:(c % 4 + 1) * T] = res.results[c]["out"]
    return out
```

```python
import numpy as np
import ml_dtypes
import concourse.bass as bass
import concourse.mybir as mybir
from concourse.bass_utils import run_bass_kernel_spmd

F32 = mybir.dt.float32
BF16 = mybir.dt.bfloat16
AF = mybir.ActivationFunctionType
ALU = mybir.AluOpType

NCORES = 8
T = 2048
HALO = 2048
D = 1024
EPS = 1e-6
WBLK = 4096
PERM = np.array(list(range(0, 16)) + list(range(32, 48)) + list(range(16, 32)) + list(range(48, 128)))
DIL = (1, 4, 16)

C_ID = 0
C_MASK = 128
C_MASK0 = 384
C_EPS = 640
C_GMIX = 641
C_GCROSS = 649
C_GMEM = 657
C_GMLP = 665
C_GFIN = 673
C_BGA = 681
C_BGB = 689
C_CONVB = 697
C_LNG = 703
C_LNB = 709
C_CONVW = 715
CW = 715 + 186

WB_A = 0
WB_C = 12
WB_E = 15
WB_M = 23
WB_OUT = 27
WB_CQ = 29
WB_CO = 31
WB_UP = 33
WB_DN = 41
NWB = 49


class Trk:
    def __init__(self, nc):
        self.nc = nc
        self.eng = {"pe": nc.tensor, "act": nc.scalar, "dve": nc.vector, "pool": nc.gpsimd, "sp": nc.sync}
        self.sems = {}
        self.cnt = {}
        for e in ("pe", "act", "dve", "pool"):
            self.cnt[e] = 0
        self.seen = {e: {} for e in self.eng}
        self.last_w = {}
        self.readers = {}
        self.dma_cnt = {}
        self.label = ""
        self.log = {e: [] for e in self.eng}

    def _sem(self, name):
        if self.sems.get(name) is None:
            cm = self.nc.semaphore(f"s_{name}")
            self.sems[name] = cm.__enter__()
        return self.sems[name]

    def _deps(self, eng, r, w):
        deps = {}

        def add(tok):
            if tok is None:
                return
            s, v = tok
            if deps.get(s, 0) < v:
                deps[s] = v

        for k in r:
            add(self.last_w.get(k))
        for k in w:
            add(self.last_w.get(k))
            for tok in self.readers.get(k, ()):
                add(tok)
        need = []
        for s, v in deps.items():
            if s == "pe" and eng == "pe":
                continue
            if self.seen[eng].get(s, 0) < v:
                need.append((s, v))
        return need

    def _emit(self, eng, fn, need):
        e = self.eng[eng]
        for s, v in need[:-1]:
            e.wait_ge(self._sem(s), v)
        ins = fn()
        if need:
            s, v = need[-1]
            ins._wait_ge(self._sem(s), v)
        for s, v in need:
            self.seen[eng][s] = v
        return ins

    def _record(self, tok, r, w):
        for k in w:
            self.last_w[k] = tok
            self.readers[k] = []
        for k in r:
            self.readers.setdefault(k, []).append(tok)

    def op(self, eng, fn, r=(), w=()):
        need = self._deps(eng, r, w)
        ins = self._emit(eng, fn, need)
        self.cnt[eng] += 1
        self.log[eng].append(self.label)
        ins.then_inc(self._sem(eng), 1)
        self._record((eng, self.cnt[eng]), r, w)

    def dma(self, q, fn, stream, r=(), w=()):
        need = self._deps(q, r, w)
        ins = self._emit(q, fn, need)
        s = "dma_" + stream
        self.dma_cnt[s] = self.dma_cnt.get(s, 0) + 16
        ins.then_inc(self._sem(s), 16)
        self._record((s, self.dma_cnt[s]), r, w)

    def barrier(self, waiters=("pe", "act", "dve", "sp"), skip_prefix="dma_w"):
        toks = [(e, self.cnt[e]) for e in ("pe", "act", "dve", "pool") if self.cnt[e] > 0]
        toks += [(s, v) for s, v in self.dma_cnt.items() if not s.startswith(skip_prefix)]
        for wtr in waiters:
            for s, v in toks:
                if self.seen[wtr].get(s, 0) < v:
                    self.eng[wtr].wait_ge(self._sem(s), v)
                    self.seen[wtr][s] = v


def _selfsync(t, engines=("act", "dve", "pool")):
    for e in engines:
        v = t.cnt[e]
        if v > 0 and t.seen[e].get(e, 0) < v:
            t.eng[e].wait_ge(t._sem(e), v)
            t.seen[e][e] = v


class Kern:
    def __init__(self, stop_after=None, dbg=False):
        self.stop_after = stop_after
        nc = self.nc = bass.Bass("TRN2", target_bir_lowering=False)
        self.xh = nc.dram_tensor("xh", [HALO + T, D], F32, kind="ExternalInput").ap()
        self.memd = nc.dram_tensor("memb", [256, D], F32, kind="ExternalInput").ap()
        self.constd = nc.dram_tensor("consts", [128, CW], F32, kind="ExternalInput").ap()
        self.ropeCd = nc.dram_tensor("ropeC", [64, 4096], F32, kind="ExternalInput").ap()
        self.ropeSd = nc.dram_tensor("ropeS", [64, 4096], F32, kind="ExternalInput").ap()
        self.wpack = nc.dram_tensor("wpack", [NWB, 128, WBLK], F32, kind="ExternalInput").ap()
        self.growd = nc.dram_tensor("grow", [128, D], F32, kind="ExternalInput").ap()
        self.outd = nc.dram_tensor("out", [T, D], F32, kind="ExternalOutput").ap()
        self.dbg = dbg
        if dbg:
            self.dbgd = nc.dram_tensor("dbg", [128, 8 * 4096], F32, kind="ExternalOutput").ap()
        self.t = Trk(nc)
        self.consts = nc.alloc_sbuf_tensor("consts_sb", [128, CW], F32)
        self.ones_bf = nc.alloc_sbuf_tensor("ones_bf", [128, 128], BF16)
        self.ident_bf = nc.alloc_sbuf_tensor("ident_bf", [128, 128], BF16)
        self.mask_bf = nc.alloc_sbuf_tensor("mask_bf", [128, 256], BF16)
        self.mask0_bf = nc.alloc_sbuf_tensor("mask0_bf", [128, 256], BF16)
        self.rcols = nc.alloc_sbuf_tensor("rcols", [128, 8], F32)
        base = nc.SBUF_PARTITION_SIZE_BYTES - nc.sbuf_bytes_remaining
        base = (base + 63) // 64 * 64
        self.arena_total = 24576 + 65536 + 16384 + 24576 + 73728 + 2048
        nc.alloc_sbuf_tensor("arena", [128, self.arena_total + 64], mybir.dt.uint8)
        self.oW = base
        self.oU = self.oW + 24576
        self.oB = self.oU + 65536
        self.oC = self.oB + 16384
        self.oD = self.oC + 24576
        self._n = 0
        self.ps = [nc.alloc_psum_tensor(f"psb{i}", [128, 512], F32) for i in range(8)]
        self._bank = 0
        self.bank_pool = list(range(8))
        self.wslots = [self.at(self.oW + i * 8192, [128, WBLK], BF16) for i in range(3)]
        self.wq = []
        self.wq_issued = 0
        self.wq_pos = 0

    def at(self, off, shape, dt):
        self._n += 1
        assert off % 32 == 0, off
        return self.nc.alloc_sbuf_tensor_at(f"m{self._n}", shape, dt, offset=off)

    def bank(self):
        pool = self.bank_pool
        b = pool[self._bank % len(pool)]
        self._bank += 1
        return b

    def cc(self, col, n=1):
        return self.consts[:, col:col + n]

    def w_plan(self, blocks):
        self.wq.extend(blocks)

    def _w_issue(self):
        i = self.wq_issued
        blk = self.wq[i]
        slot = i % 3
        dst = self.wslots[slot]
        self.t.dma("pool", lambda: self.nc.gpsimd.dma_start(out=dst[:, :], in_=self.wpack[blk]),
                   stream=f"w{slot}", w=[("w", slot)])
        self.wq_issued += 1

    def w_take(self, n):
        first = self.wq_pos
        while self.wq_issued < min(len(self.wq), first + 3):
            self._w_issue()
        out = []
        for i in range(n):
            slot = (first + i) % 3
            out.append((self.wslots[slot], ("w", slot)))
        self.wq_pos += n
        return out

    def w_next(self):
        return self.w_take(1)[0]

    def mm_group(self, b, ncols, pairs, r_keys, col0=0):
        n = len(pairs)
        out = self.ps[b][:, col0:col0 + ncols]
        for i, (lh, rh) in enumerate(pairs):
            self.t.op("pe", lambda lh=lh, rh=rh, i=i: self.nc.tensor.matmul(
                out, lhsT=lh, rhs=rh, start=(i == 0), stop=(i == n - 1)),
                r=r_keys if i == 0 else (), w=[("ps", b)])
        self.t._record(("pe", self.t.cnt["pe"]), r_keys, ())

    def norm(self, xall, xk, xkeys, gcol, out_fn, okeys, ntok, sq, lnv, rstd, tag, split=False, pool_sq=None):
        nc, t = self.nc, self.t

        def part_sq():
            if pool_sq is None:
                t.op("act", lambda: nc.scalar.activation(out=sq[:, :, 0:ntok], in_=xall, func=AF.Square),
                     r=xkeys, w=[("sq", tag)])
            else:
                xlo, xhi = pool_sq
                t.op("act", lambda: nc.scalar.activation(out=sq[:, 0:4, 0:ntok], in_=xlo, func=AF.Square),
                     r=xkeys, w=[("sq", tag)])
                t.op("pool", lambda: nc.gpsimd.tensor_tensor(out=sq[:, 4:8, 0:ntok], in0=xhi, in1=xhi,
                                                             op=ALU.mult),
                     r=xkeys, w=[("sq", tag, 1)])

        def part_rest():
            self._norm_rest(xk, xkeys, gcol, out_fn, okeys, ntok, sq, lnv, rstd, tag)
        if split:
            return part_sq, part_rest
        part_sq()
        part_rest()

    def _norm_rest(self, xk, xkeys, gcol, out_fn, okeys, ntok, sq, lnv, rstd, tag):
        nc, t = self.nc, self.t
        b = self.bank()
        self.mm_group(b, ntok, [(self.ones_bf[:, :], sq[:, k, 0:ntok]) for k in range(8)],
                      [("sq", tag), ("sq", tag, 1)])
        t.op("act", lambda: nc.scalar.activation(out=lnv[:, 0:ntok], in_=self.ps[b][:, 0:ntok], func=AF.Ln,
                                                 bias=self.cc(C_EPS), scale=1.0 / D),
             r=[("ps", b)], w=[("lnv", tag)])
        t.op("act", lambda: nc.scalar.activation(out=rstd[:, 0:ntok], in_=lnv[:, 0:ntok], func=AF.Exp, scale=-0.5),
             r=[("lnv", tag)], w=[("rstd", tag)])
        for k in range(8):
            t.op("dve", lambda k=k: nc.vector.scalar_tensor_tensor(
                out=out_fn(k), in0=xk(k), scalar=self.cc(gcol + k), in1=rstd[:, 0:ntok],
                op0=ALU.mult, op1=ALU.mult),
                r=xkeys + [("rstd", tag)], w=okeys)

    def norm_split(self, xall, xk, xkeys, gcol, ug_fn, ugkeys, ntok, sq, lnv, rstd_out, rkey, tag):
        nc, t = self.nc, self.t
        for k in range(8):
            if k < 4:
                t.op("act", lambda k=k: nc.scalar.activation(out=ug_fn(k), in_=xk(k), func=AF.Copy,
                                                             scale=self.cc(gcol + k)),
                     r=xkeys + ["consts"], w=ugkeys)
            else:
                t.op("dve", lambda k=k: nc.vector.tensor_scalar(out=ug_fn(k), in0=xk(k), scalar1=self.cc(gcol + k),
                                                                scalar2=None, op0=ALU.mult),
                     r=xkeys + ["consts"], w=ugkeys)
        def part_sq():
            t.op("act", lambda: nc.scalar.activation(out=sq[:, :, 0:ntok], in_=xall, func=AF.Square),
                 r=xkeys, w=[("sq", tag)])

        def part_b():
            b = self.bank()
            self.mm_group(b, ntok, [(self.ones_bf[:, :], sq[:, k, 0:ntok]) for k in range(8)], [("sq", tag)])
            t.op("act", lambda: nc.scalar.activation(out=lnv[:, 0:ntok], in_=self.ps[b][:, 0:ntok], func=AF.Ln,
                                                     bias=self.cc(C_EPS), scale=1.0 / D),
                 r=[("ps", b)], w=[("lnv", tag)])
            t.op("act", lambda: nc.scalar.activation(out=rstd_out[:, 0:ntok], in_=lnv[:, 0:ntok], func=AF.Exp,
                                                     scale=-0.5),
                 r=[("lnv", tag)], w=[rkey])
        return part_sq, part_b

    def io_alloc(self, nslots, exclude=()):
        while True:
            slot = self._xs_i % nslots
            self._xs_i += 1
            if slot not in exclude:
                return slot

    def lt_issue(self, rows, xs_slots, stream, exclude=()):
        nc, t = self.nc, self.t
        slot = self.io_alloc(len(xs_slots), exclude)
        xs = xs_slots[slot]
        t.dma("sp", lambda: nc.sync.dma_start(out=xs[:, :], in_=rows), stream=f"{stream}{slot}",
              w=[("xs", slot)])
        return slot

    def load_transpose(self, src_rows, xs_slots, nsub, dst, dkey, stream, evac=("act", "dve"), pre=(), s_off=0):
        nc, t = self.nc, self.t
        ident = self.consts[:, C_ID:C_ID + 128]
        pre = list(pre)
        for s_ in range(nsub):
            s = s_ + s_off
            if s_ < len(pre):
                slot = pre[s_]
            else:
                slot = self.lt_issue(src_rows(s), xs_slots, stream, exclude=pre[s_ + 1:])
            xs = xs_slots[slot]
            for hf in range(2):
                b = self.bank()
                for kk in range(4):
                    k = hf * 4 + kk
                    t.op("pe", lambda k=k, kk=kk: nc.tensor.transpose(
                        self.ps[b][:, kk * 128:(kk + 1) * 128], xs[:, k * 128:(k + 1) * 128], ident),
                        r=[("xs", slot)], w=[("ps", b)])
                src = self.ps[b][:, 0:512].rearrange("p (a b) -> p a b", a=4)
                dd = dst[:, hf * 4:hf * 4 + 4, s * 128:(s + 1) * 128]
                if evac[hf] == "act":
                    t.op("act", lambda: nc.scalar.copy(out=dd, in_=src), r=[("ps", b)], w=dkey(s, hf))
                else:
                    t.op("dve", lambda: nc.vector.tensor_copy(out=dd, in_=src), r=[("ps", b)], w=dkey(s, hf))

    def ucols(self, k, a0, n, step=1):
        if a0 < 2048:
            tt, o = self.uTh, a0
        else:
            tt, o = self.uTm, a0 - 2048
        assert o + (n - 1) * step < 2048
        return tt[:, k, o:o + (n - 1) * step + 1:step]

    def ukeys(self, a0, n, step=1):
        return [("uT", i) for i in range(a0 // 512, (a0 + (n - 1) * step) // 512 + 1)]

    def kcols(self, a0, n, step=1):
        if a0 < 2048:
            tt, o = self.kTh, a0
        else:
            tt, o = self.kTm, a0 - 2048
        assert o + (n - 1) * step < 2048
        return tt[:, o:o + (n - 1) * step + 1:step]

    def kkeys(self, a0, n, step=1):
        return [("kT", i) for i in range(a0 // 512, (a0 + (n - 1) * step) // 512 + 1)]

    def build(self):
        nc, t = self.nc, self.t
        self._xs_i = 0
        em = [WB_E + j for j in range(5)]
        for i in range(3):
            em += [WB_M + i, WB_E + 5 + i]
        em += [WB_M + 3]
        plan = list(range(WB_A, WB_A + 12)) + list(range(WB_C, WB_C + 3)) * 2 + em + list(range(WB_OUT, NWB)) * 2
        self.w_plan(plan)
        oU, oB, oC, oD = self.oU, self.oB, self.oC, self.oD
        self.uTh = self.at(oU, [128, 8, 2048], BF16)
        self.uTm = self.at(oU + 32768, [128, 8, 2048], BF16)
        self.attnT = self.at(oB, [128, 4, 2048], BF16)
        self.cT = self.at(oC, [128, 6, 2048], BF16)
        ropeC = self.at(oC, [64, 4096], F32)
        ropeS = self.at(oD + 57344, [64, 4096], F32)

        t.dma("sp", lambda: nc.sync.dma_start(out=self.consts[:, :], in_=self.constd), stream="c0", w=["consts"])
        t.dma("sp", lambda: nc.sync.dma_start(out=ropeC[:, :], in_=self.ropeCd), stream="c1", w=["ropeC"])
        t.op("dve", lambda: nc.vector.memset(self.ones_bf[:, :], 1.0), w=["ones"])
        t.op("act", lambda: nc.scalar.copy(out=self.ident_bf[:, :], in_=self.consts[:, C_ID:C_ID + 128]),
             r=["consts"], w=["identbf"])
        t.op("act", lambda: nc.scalar.copy(out=self.mask_bf[:, :], in_=self.consts[:, C_MASK:C_MASK + 256]),
             r=["consts"], w=["mask"])
        t.op("act", lambda: nc.scalar.copy(out=self.mask0_bf[:, :], in_=self.consts[:, C_MASK0:C_MASK0 + 256]),
             r=["consts"], w=["mask"])
        t.barrier()

        while self.wq_issued < 3:
            self._w_issue()
        t.label = "p0"
        xs0 = [self.at(oD + i * 4096, [128, 1024], F32) for i in range(3)]
        xTts = [self.at(oD + 12288 + i * 16384, [128, 8, 512], F32) for i in range(3)]
        sq = self.at(oD + 61440, [128, 8, 512], BF16)
        lnv = self.at(oD + 69632, [128, 512], F32)
        rstd = self.at(oD + 71680, [128, 512], F32)
        def p0_load(tt):
            xk_ = ("xTt", tt % 3)
            self.load_transpose(lambda s, tt=tt: self.xh[tt * 512 + s * 128: tt * 512 + (s + 1) * 128, :],
                                xs0, 4, xTts[tt % 3], lambda s, hf, xk_=xk_: [xk_ + (hf,)], "x", evac=("act", "act"))

        def p0_norm(tt):
            xTt = xTts[tt % 3]
            xk_ = ("xTt", tt % 3)
            dstT = self.uTh if tt < 4 else self.uTm
            c0 = (tt % 4) * 512
            return self.norm(xTt[:, :, :], lambda k: xTt[:, k, :], [xk_ + (0,), xk_ + (1,)], C_GMIX,
                             lambda k: dstT[:, k, c0:c0 + 512], [("uT", tt)], 512, sq, lnv, rstd, "p0", split=True,
                             pool_sq=(xTt[:, 0:4, :], xTt[:, 4:8, :]))

        p0_load(0)
        parts = p0_norm(0)
        parts[0]()
        for tt in range(8):
            if tt + 1 < 8:
                p0_load(tt + 1)
            parts[1]()
            if tt + 1 < 8:
                parts = p0_norm(tt + 1)
                parts[0]()
        if self.stop_after == "p0":
            return self.finish_dbg([(self.uTm, 8 * 2048, BF16)])
        t.barrier(waiters=("act", "dve", "sp"))

        t.dma("sp", lambda: nc.sync.dma_start(out=ropeS[:, :], in_=self.ropeSd), stream="c2", w=["ropeS"])
        self.qT = self.at(oD, [128, 2048], BF16)
        self.kTh = self.at(oD + 4096, [128, 2048], BF16)
        self.kTm = self.at(oD + 8192, [128, 2048], BF16)
        Vt = self.at(oD + 12288, [128, 32, 128], BF16)
        acc = self.at(oD + 20480, [128, 2, 2048], F32)
        a32 = [self.at(oD + 36864 + i * 2048, [128, 512], F32) for i in range(2)]
        tmp = [self.at(oD + 40960 + i * 2048, [128, 512], F32) for i in range(2)]
        pts = [self.at(oD + 45056 + i * 512, [128, 256], BF16) for i in range(4)]
        pms = [self.at(oD + 47104 + i * 512, [128, 256], BF16) for i in range(4)]
        for i in range(2):
            t.op("dve", lambda i=i: nc.vector.memset(tmp[i][:, :], 0.0), w=[("tmp", i)])
        rope_i = 0
        blk_i = 0
        scale = 1.0 / np.sqrt(128.0)
        for h4 in range(4):
            for g in range(3):
                d = DIL[g]
                halo = 128 * d
                wt, wkey = self.w_next()
                wq_ = lambda k: wt[:, k * 128:(k + 1) * 128]
                wk_ = lambda k: wt[:, 1024 + k * 128:1024 + (k + 1) * 128]
                wv_ = lambda k: wt[:, 2048 + k * 128:2048 + (k + 1) * 128]
                def emit_qk():
                    nonlocal rope_i
                    t.label = "A.qk"
                    jobs = []
                    for tt in range(4):
                        jobs.append(("q", 2048 + tt * 512, 512))
                    a = 2048 - halo
                    while a < 4096:
                        n = min(512, 4096 - a, 512 - (a % 512) if a % 512 else 512)
                        jobs.append(("k", a, n))
                        a += n
                    for (kind, a0, n) in jobs:
                        wsel = wq_ if kind == "q" else wk_
                        b = self.bank()
                        self.mm_group(b, n, [(wsel(k), self.ucols(k, a0, n)) for k in range(8)],
                                      [wkey] + self.ukeys(a0, n))
                        if kind == "q":
                            dT, dc = self.qT, a0 - 2048
                            dkeys = [("qT", (a0 - 2048) // 512), "qTall"]
                        else:
                            dT, dc = (self.kTh, a0) if a0 < 2048 else (self.kTm, a0 - 2048)
                            dkeys = self.kkeys(a0, n)
                        z = self.ps[b]
                        sl = rope_i % 2
                        rope_i += 1
                        A, Tm = a32[sl], tmp[sl]
                        perm_q = (kind == "q" and d > 1)
                        if perm_q:
                            m0, ml = dc // d, n // d
                            pv = lambda ap: ap.rearrange("p (m r) -> p r m", r=d)
                            o_hi = self.qT[64:128, :].rearrange("p (r m) -> p r m", r=d)[:, :, m0:m0 + ml]
                            o_lo = self.qT[0:64, :].rearrange("p (r m) -> p r m", r=d)[:, :, m0:m0 + ml]
                            dkeys = ["qTall"]
                            t.op("act", lambda: nc.scalar.copy(out=o_hi, in_=pv(z[64:128, 0:n])),
                                 r=[("ps", b)], w=[(kk, "hi") for kk in dkeys] + dkeys)
                        else:
                            t.op("act", lambda: nc.scalar.copy(out=dT[64:128, dc:dc + n], in_=z[64:128, 0:n]),
                                 r=[("ps", b)], w=[(kk, "hi") for kk in dkeys] + dkeys)
                        t.op("dve", lambda: nc.vector.tensor_tensor(out=A[0:64, 0:n], in0=z[0:64, 0:n],
                                                                    in1=ropeC[0:64, a0:a0 + n], op=ALU.mult),
                             r=[("ps", b), "ropeC"], w=[("a32", sl)])
                        t.op("dve", lambda: nc.vector.tensor_tensor(out=Tm[0:16, 0:n], in0=z[32:48, 0:n],
                                                                    in1=ropeS[32:48, a0:a0 + n], op=ALU.mult),
                             r=[("ps", b), "ropeS"], w=[("tmp", sl)])
                        t.op("dve", lambda: nc.vector.tensor_tensor(out=Tm[32:48, 0:n], in0=z[0:16, 0:n],
                                                                    in1=ropeS[0:16, a0:a0 + n], op=ALU.mult),
                             r=[("ps", b), "ropeS"], w=[("tmp", sl)])
                        if perm_q:
                            t.op("pool", lambda: nc.gpsimd.tensor_tensor(out=o_lo, in0=pv(A[0:64, 0:n]),
                                                                         in1=pv(Tm[0:64, 0:n]), op=ALU.add),
                                 r=[("a32", sl), ("tmp", sl)], w=dkeys)
                        else:
                            t.op("pool", lambda: nc.gpsimd.tensor_tensor(out=dT[0:64, dc:dc + n], in0=A[0:64, 0:n],
                                                                         in1=Tm[0:64, 0:n], op=ALU.add),
                                 r=[("a32", sl), ("tmp", sl)], w=dkeys)
                def emit_v():
                    t.label = "A.v"
                    nb = 16 // d
                    vlist = [(r, j) for r in range(d) for j in range(-1, nb)]
                    for v0 in range(0, len(vlist), 4):
                        grp = vlist[v0:v0 + 4]
                        b = self.bank()
                        rk = [wkey]
                        for gi, (r, j) in enumerate(grp):
                            a0 = 2048 + r + 128 * d * j
                            rk = rk + self.ukeys(a0, 128, d)
                            for k in range(8):
                                t.op("pe", lambda k=k, gi=gi, a0=a0: nc.tensor.matmul(
                                    self.ps[b][:, gi * 128:(gi + 1) * 128], lhsT=self.ucols(k, a0, 128, d),
                                    rhs=wv_(k), start=(k == 0), stop=(k == 7)),
                                    r=rk if k == 0 else (), w=[("ps", b)])
                        t._record(("pe", t.cnt["pe"]), rk, ())
                        ng = len(grp)
                        t.op("act", lambda v0=v0, ng=ng, b=b: nc.scalar.copy(
                            out=Vt[:, v0:v0 + ng, :],
                            in_=self.ps[b][:, 0:ng * 128].rearrange("p (a b) -> p a b", a=ng)),
                            r=[("ps", b)], w=[("V", v0 + i) for i in range(ng)])
                nb = 16 // d
                if h4 == 0 and g == 0:
                    emit_v()
                    emit_qk()
                else:
                    emit_qk()
                    emit_v()
                t.label = "A.attn"
                blocks = [(r, j) for r in range(d) for j in range(nb)]
                LAG = 3
                st = {}
                for it in range(len(blocks) + LAG):
                    if it < len(blocks):
                        r, j = blocks[it]
                        a0 = 2048 + r + 128 * d * j
                        ap_ = a0 - 128 * d
                        b = self.bank()
                        if d > 1:
                            qc = r * (2048 // d) + 128 * j
                            qv = self.qT[:, qc:qc + 128]
                            qk = ["qTall"]
                        else:
                            qv = self.qT[:, a0 - 2048:a0 - 2048 + 127 * d + 1:d]
                            qk = [("qT", i) for i in range((a0 - 2048) // 512, (a0 - 2048 + 127 * d) // 512 + 1)] \
                                + ["qTall"]
                        mk = self.mask0_bf if j == 0 else self.mask_bf
                        t.op("pe", lambda: nc.tensor.matmul(self.ps[b][:, 0:256], lhsT=self.ident_bf[:, :],
                                                            rhs=mk[:, :], start=True, stop=False),
                             r=["identbf", "mask"], w=[("ps", b)])
                        t.op("pe", lambda: nc.tensor.matmul(self.ps[b][:, 0:128], lhsT=self.kcols(ap_, 128, d),
                                                            rhs=qv, start=False, stop=False),
                             r=qk + self.kkeys(ap_, 128, d), w=[("ps", b)])
                        t.op("pe", lambda: nc.tensor.matmul(self.ps[b][:, 128:256], lhsT=self.kcols(a0, 128, d),
                                                            rhs=qv, start=False, stop=True),
                             r=qk + self.kkeys(a0, 128, d), w=[("ps", b)])
                        ps_i = blk_i % 4
                        blk_i += 1
                        pm = pms[ps_i]
                        t.op("act", lambda: nc.scalar.activation(out=pm[:, :], in_=self.ps[b][:, 0:256],
                                                                 func=AF.Exp, scale=float(scale)),
                             r=[("ps", b)], w=[("pm", ps_i)])
                        st[it] = (b, ps_i, r, j, a0)
                    if it >= LAG:
                        b, ps_i, r, j, a0 = st.pop(it - LAG)
                        pm = pms[ps_i]
                        vprev = r * (nb + 1) + j
                        vcur = vprev + 1
                        o = self.ps[b][:, 256:384]
                        dn = self.ps[b][:, 384:512]
                        t.op("pe", lambda: nc.tensor.matmul(o, lhsT=Vt[:, vprev, :], rhs=pm[:, 0:128],
                                                            start=True, stop=False),
                             r=[("V", vprev), ("pm", ps_i)], w=[("ps", b)])
                        t.op("pe", lambda: nc.tensor.matmul(o, lhsT=Vt[:, vcur, :], rhs=pm[:, 128:256],
                                                            start=False, stop=True),
                             r=[("V", vcur)], w=[("ps", b)])
                        t.op("pe", lambda: nc.tensor.matmul(dn, lhsT=self.ones_bf[:, :], rhs=pm[:, 0:128],
                                                            start=True, stop=False), r=["ones"], w=[("ps", b)])
                        t.op("pe", lambda: nc.tensor.matmul(dn, lhsT=self.ones_bf[:, :], rhs=pm[:, 128:256],
                                                            start=False, stop=True), r=[("pm", ps_i)], w=[("ps", b)])
                        q0 = a0 - 2048
                        dst = acc[:, :, q0:q0 + 127 * d + 1:d]
                        src = self.ps[b][:, 256:512].rearrange("p (a b) -> p a b", a=2)
                        akeys = [("acc", i) for i in range(q0 // 512, (q0 + 127 * d) // 512 + 1)]
                        if g == 0:
                            t.op("act", lambda: nc.scalar.copy(out=dst, in_=src), r=[("ps", b)], w=akeys)
                        else:
                            t.op("dve", lambda: nc.vector.tensor_tensor(out=dst, in0=src, in1=dst, op=ALU.add),
                                 r=[("ps", b)] + akeys, w=akeys)
            t.label = "A.fin"
            for tt in range(4):
                sl_ = slice(tt * 512, (tt + 1) * 512)
                t.op("act", lambda: nc.scalar.activation(out=acc[:, 1, sl_], in_=acc[:, 1, sl_], func=AF.Ln),
                     r=[("acc", tt)], w=[("acc", tt)])
                t.op("act", lambda: nc.scalar.activation(out=acc[:, 1, sl_], in_=acc[:, 1, sl_], func=AF.Exp,
                                                         scale=-1.0),
                     r=[("acc", tt)], w=[("acc", tt)])
                t.op("dve", lambda: nc.vector.tensor_tensor(out=self.attnT[:, h4, sl_], in0=acc[:, 0, sl_],
                                                            in1=acc[:, 1, sl_], op=ALU.mult),
                     r=[("acc", tt)], w=[("attnT", tt)])
        if self.stop_after == "pA":
            return self.finish_dbg([(self.attnT, 4 * 2048, BF16)])
        t.barrier(waiters=("act", "dve", "sp"))

        conv = self.at(oD, [128, 6, 1024], F32)
        cglu = [self.at(oD + 24576 + i * 2176, [128, 1056], BF16) for i in range(2)]
        diags = [self.at(oD + 28928 + i * 7936, [128, 31, 128], BF16) for i in range(2)]
        sg = [self.at(oD + 44800 + i * 2048, [128, 512], F32) for i in range(2)]
        xbs = [self.at(oD + 48896 + i * 1024, [128, 512], BF16) for i in range(4)]
        xsqs = [self.at(oD + 55040 + i * 1024, [128, 512], BF16) for i in range(4)]
        st_pend = []
        self.bank_pool = [0, 1, 2, 3]
        st_i = 0
        mean = self.at(oD + 61184, [128, 512], F32)
        var = self.at(oD + 63232, [128, 512], F32)
        lnv = self.at(oD + 65280, [128, 512], F32)
        rstd = self.at(oD + 67328, [128, 512], F32)
        t1 = [self.at(oD + 69376, [128, 512], F32), self.at(oD + 52992, [128, 512], F32)]
        t2 = [self.at(oD + 71424, [128, 512], F32), self.at(oD + 59136, [128, 512], F32)]
        ln_pending = []
        dg_i = 0
        sg_i = 0
        wst = {}
        for half in range(2):
            base_a = 2048 + half * 1024

            def glu_part(jj, half=half, base_a=base_a):
                nonlocal dg_i, sg_i
                if jj % 2 == 0:
                    wst["w"] = self.w_next()
                wt, wkey = wst["w"]
                wa = lambda k, o=(jj % 2) * 2048: wt[:, o + k * 128:o + (k + 1) * 128]
                wb = lambda k, o=(jj % 2) * 2048 + 1024: wt[:, o + k * 128:o + (k + 1) * 128]
                cg = cglu[jj % 2]
                ckey = ("cglu", jj % 2)
                t.label = "C.diag"
                diag = diags[dg_i % 2]
                dgk = ("diag", dg_i % 2)
                dg_i += 1
                t.op("dve", lambda: nc.vector.tensor_tensor(
                    out=diag[:, :, :], in0=self.ident_bf[:, :].unsqueeze(1).broadcast_to([128, 31, 128]),
                    in1=self.consts[:, C_CONVW + jj * 31:C_CONVW + (jj + 1) * 31].unsqueeze(2).broadcast_to(
                        [128, 31, 128]), op=ALU.mult),
                    r=["identbf", "consts"], w=[dgk])
                t.label = "C.glu"
                for (a0, n, c0) in ((base_a - 32, 32, 0), (base_a, 512, 32), (base_a + 512, 512, 544)):
                    ba = self.bank()
                    self.mm_group(ba, n, [(wa(k), self.ucols(k, a0, n)) for k in range(8)],
                                  [wkey] + self.ukeys(a0, n))
                    bb = self.bank()
                    self.mm_group(bb, n, [(wb(k), self.ucols(k, a0, n)) for k in range(8)],
                                  [wkey] + self.ukeys(a0, n))
                    s_ = sg[sg_i % 2]
                    sk = ("sg", sg_i % 2)
                    sg_i += 1
                    t.op("act", lambda: nc.scalar.activation(out=s_[:, 0:n], in_=self.ps[bb][:, 0:n],
                                                             func=AF.Sigmoid), r=[("ps", bb)], w=[sk])
                    t.op("dve", lambda: nc.vector.tensor_tensor(out=cg[:, c0:c0 + n], in0=self.ps[ba][:, 0:n],
                                                                in1=s_[:, 0:n], op=ALU.mult),
                         r=[("ps", ba), sk], w=[ckey])
                return cg, ckey, diag, dgk

            def conv_part(jj, ctx, mid_hook=None):
                nonlocal st_i
                cg, ckey, diag, dgk = ctx
                t.label = "C.conv"
                for tt in range(2):
                    b = self.bank()
                    self.mm_group(b, 512, [(diag[:, tap, :], cg[:, 2 + tt * 512 + tap: 2 + tt * 512 + tap + 512])
                                           for tap in range(31)], [dgk, ckey])
                    cv_ = conv[:, jj, tt * 512:(tt + 1) * 512]
                    t.op("act", lambda b=b: nc.scalar.activation(
                        out=cv_, in_=self.ps[b][:, :], func=AF.Identity,
                        bias=self.cc(C_CONVB + jj)), r=[("ps", b), "consts"], w=[("conv", tt)])
                    xb_, xq_ = xbs[st_i % 4], xsqs[st_i % 4]
                    kb_, kq_ = ("xb", st_i % 4), ("xsq", st_i % 4)
                    st_i += 1
                    t.op("act", lambda: nc.scalar.copy(out=xb_[:, :], in_=cv_), r=[("conv", tt)], w=[kb_])
                    t.op("act", lambda: nc.scalar.activation(out=xq_[:, :], in_=cv_, func=AF.Square),
                         r=[("conv", tt)], w=[kq_])

                    def _stats(tt=tt, jj=jj, xb_=xb_, xq_=xq_, kb_=kb_, kq_=kq_):
                        t.op("pe", lambda: nc.tensor.matmul(self.ps[4 + tt][:, :], lhsT=self.ones_bf[:, :],
                                                            rhs=xb_[:, :], start=(jj == 0), stop=(jj == 5)),
                             r=[kb_, "ones"], w=[("ps", 4 + tt)])
                        t.op("pe", lambda: nc.tensor.matmul(self.ps[6 + tt][:, :], lhsT=self.ones_bf[:, :],
                                                            rhs=xq_[:, :], start=(jj == 0), stop=(jj == 5)),
                             r=[kq_], w=[("ps", 6 + tt)])
                    st_pend.append(_stats)
                    if len(st_pend) > 2:
                        st_pend.pop(0)()
                    if tt == 0 and mid_hook is not None:
                        mid_hook()
                        t.label = "C.conv"

            ctx_next = glu_part(0)
            for jj in range(6):
                ctx = ctx_next
                if jj + 1 < 6:
                    ctx_next = glu_part(jj + 1)
                hook = None
                if jj == 0 and ln_pending:
                    ln_pending.pop(0)()
                    hook = ln_pending.pop(0)
                conv_part(jj, ctx, hook)
            while st_pend:
                st_pend.pop(0)()
            def ln_stage(tt, half=half):
                t.label = "C.ln"
                if True:
                    b1 = 4 + tt
                    b2 = 6 + tt
                    t.op("dve", lambda: nc.vector.tensor_scalar(out=mean[:, :], in0=self.ps[b1][:, :],
                                                                scalar1=1.0 / 768, scalar2=None, op0=ALU.mult),
                         r=[("ps", b1)], w=["mean"])
                    t.op("dve", lambda: nc.vector.tensor_tensor(out=var[:, :], in0=mean[:, :], in1=mean[:, :],
                                                                op=ALU.mult), r=["mean"], w=["var"])
                    t.op("dve", lambda: nc.vector.scalar_tensor_tensor(out=var[:, :], in0=self.ps[b2][:, :],
                                                                       scalar=1.0 / 768, in1=var[:, :],
                                                                       op0=ALU.mult, op1=ALU.subtract),
                         r=[("ps", b2), "var"], w=["var"])
                    t.op("act", lambda: nc.scalar.activation(out=lnv[:, :], in_=var[:, :], func=AF.Ln,
                                                             bias=self.cc(C_EPS)), r=["var"], w=["lnvc"])
                    t.op("act", lambda: nc.scalar.activation(out=rstd[:, :], in_=lnv[:, :], func=AF.Exp, scale=-0.5),
                         r=["lnvc"], w=["rstdc"])
                    for jj in range(6):
                        a_, b_ = t1[jj % 2], t2[jj % 2]
                        t.op("dve", lambda: nc.vector.tensor_tensor(out=a_[:, :], in0=conv[:, jj, tt * 512:(tt + 1) * 512],
                                                                    in1=mean[:, :], op=ALU.subtract),
                             r=[("conv", tt), "mean"], w=[("t1", jj % 2)])
                        t.op("dve", lambda: nc.vector.scalar_tensor_tensor(out=b_[:, :], in0=a_[:, :],
                                                                           scalar=self.cc(C_LNG + jj), in1=rstd[:, :],
                                                                           op0=ALU.mult, op1=ALU.mult),
                             r=[("t1", jj % 2), "rstdc"], w=[("t2", jj % 2)])
                        c0 = half * 1024 + tt * 512
                        t.op("act", lambda: nc.scalar.activation(out=self.cT[:, jj, c0:c0 + 512], in_=b_[:, :],
                                                                 func=AF.Silu, bias=self.cc(C_LNB + jj)),
                             r=[("t2", jj % 2)], w=[("cT", c0 // 512)])
            ln_pending.append(lambda f=ln_stage: f(0))
            ln_pending.append(lambda f=ln_stage: f(1))
        while len(ln_pending) > 1:
            ln_pending.pop(0)()
        self.bank_pool = [0, 1, 2, 3, 4, 6]
        self._bank = 0
        if self.stop_after == "pC":
            return self.finish_dbg([(self.cT, 6 * 2048, BF16)])
        t.barrier(waiters=("sp",))
        _selfsync(t)
        xs1 = [self.at(oD + 32768 + i * 4096, [128, 1024], F32) for i in range(4)]
        self._xs_i = 0
        pre_mem = [self.lt_issue(self.memd[s_ * 128:(s_ + 1) * 128, :], xs1, "xb") for s_ in range(2)]
        pre_x0 = [self.lt_issue(self.xh[HALO + s_ * 128: HALO + (s_ + 1) * 128, :], xs1, "xb") for s_ in range(2)]

        t.label = "E"
        mergedT = self.at(oU, [128, 8, 2048], BF16)
        sa = [self.at(oD + 24576, [128, 512], F32)] * 2
        sb = [self.at(oD + 26624, [128, 512], F32)] * 2
        e1 = [self.at(oD + 28672, [128, 512], F32)] * 2
        e2 = [self.at(oD + 30720, [128, 512], F32)] * 2
        ei = 0
        ckT = self.at(oD + 63488, [128, 8, 256], BF16)
        cV = self.at(oD + 67584, [128, 2, 1024], BF16)

        def m_block(mi):
            t.label = "M"
            wt_, wkey_ = self.w_next()
            if mi < 2:
                for jj in range(4):
                    j_ = mi * 4 + jj
                    b = self.bank()
                    self.mm_group(b, 256, [(wt_[:, jj * 1024 + k * 128: jj * 1024 + (k + 1) * 128], mT[:, k, :])
                                           for k in range(8)], [wkey_, "mT"])
                    t.op("act", lambda j_=j_, b=b: nc.scalar.copy(out=ckT[:, j_, :], in_=self.ps[b][:, 0:256]),
                         r=[("ps", b)], w=["ckT"])
            else:
                blk = mi - 2
                for mc in range(2):
                    b = self.bank()
                    self.mm_group(b, 512, [(mT[:, k, mc * 128:(mc + 1) * 128], wt_[:, k * 512:(k + 1) * 512])
                                           for k in range(8)], [wkey_, "mT"])
                    t.op("act", lambda mc=mc, b=b, blk=blk: nc.scalar.copy(
                        out=cV[:, mc, blk * 512:(blk + 1) * 512], in_=self.ps[b][:, :]), r=[("ps", b)], w=["cV"])
            t.label = "E"
        memT = self.at(oD + 0, [128, 8, 256], F32)
        mT = self.at(oD + 8192, [128, 8, 256], BF16)
        sq_m = self.at(oD + 49152, [128, 8, 512], BF16)
        lnv_m = self.at(oD + 57344, [128, 512], F32)
        rstd_m = self.at(oD + 59392, [128, 512], F32)
        for j in range(8):
            if j == 4:
                _selfsync(t)
                t.label = "M"
                self.load_transpose(lambda s: self.memd[s * 128:(s + 1) * 128, :], xs1, 2, memT,
                                    lambda s, hf: [("memT", hf)], "xb", pre=pre_mem)
                self.norm(memT[:, :, :], lambda k: memT[:, k, :], [("memT", 0), ("memT", 1)], C_GMEM,
                          lambda k: mT[:, k, :], ["mT"], 256, sq_m, lnv_m, rstd_m, "pm")
                t.label = "E"
            wt, wkey = self.w_next()
            wga = lambda k: wt[:, k * 128:(k + 1) * 128]
            wap = lambda k: wt[:, 1024 + k * 128:1024 + (k + 1) * 128]
            wgb = lambda k: wt[:, 1536 + k * 128:1536 + (k + 1) * 128]
            wcp = lambda k: wt[:, 2560 + k * 128:2560 + (k + 1) * 128]
            for tt in range(4):
                cs = slice(tt * 512, (tt + 1) * 512)
                bga = self.bank()
                self.mm_group(bga, 512, [(wga(k), self.uTm[:, k, cs]) for k in range(8)], [wkey, ("uT", 4 + tt)])
                bya = self.bank()
                self.mm_group(bya, 512, [(wap(k), self.attnT[:, k, cs]) for k in range(4)], [wkey, ("attnT", tt)])
                bgb = self.bank()
                self.mm_group(bgb, 512, [(wgb(k), self.uTm[:, k, cs]) for k in range(8)], [wkey, ("uT", 4 + tt)])
                byc = self.bank()
                self.mm_group(byc, 512, [(wcp(k), self.cT[:, k, cs]) for k in range(6)], [wkey, ("cT", tt)])
                s = 0
                ei += 1
                t.op("act", lambda: nc.scalar.activation(out=sa[s][:, :], in_=self.ps[bga][:, :], func=AF.Sigmoid,
                                                         bias=self.cc(C_BGA + j)), r=[("ps", bga)], w=[("sa", s)])
                t.op("act", lambda: nc.scalar.activation(out=sb[s][:, :], in_=self.ps[bgb][:, :], func=AF.Sigmoid,
                                                         bias=self.cc(C_BGB + j)), r=[("ps", bgb)], w=[("sb", s)])
                t.op("dve", lambda: nc.vector.tensor_tensor(out=e1[s][:, :], in0=self.ps[bya][:, :], in1=sa[s][:, :],
                                                            op=ALU.mult), r=[("ps", bya), ("sa", s)], w=[("e1", s)])
                t.op("dve", lambda: nc.vector.tensor_tensor(out=e2[s][:, :], in0=self.ps[byc][:, :], in1=sb[s][:, :],
                                                            op=ALU.mult), r=[("ps", byc), ("sb", s)], w=[("e2", s)])
                t.op("pool", lambda: nc.gpsimd.tensor_tensor(out=mergedT[:, j, cs], in0=e1[s][:, :], in1=e2[s][:, :],
                                                             op=ALU.add), r=[("e1", s), ("e2", s)], w=[("mg", tt)])
                if j == 0 and tt == 1 and ln_pending:
                    ln_pending.pop(0)()
                    t.label = "E"
                    self.bank_pool = list(range(8))
            if j >= 4:
                m_block(j - 4)
        if self.stop_after == "pE":
            return self.finish_dbg([(mergedT, 8 * 2048, BF16)])

        t.label = "M"
        hT = self.at(oD, [128, 16, 1024], BF16)
        bufA = self.at(oU + 32768, [128, 8, 1024], BF16)
        bufB = self.at(oU + 49152, [128, 8, 1024], BF16)
        xT = self.at(oB, [128, 8, 1024], F32)

        sq = self.at(oD + 49152, [128, 8, 512], BF16)
        lnv = self.at(oD + 57344, [128, 512], F32)
        rstdn = [self.at(oD + 59392 + i * 2048, [128, 512], F32) for i in range(2)]
        rstd = rstdn[0]
        ptc = [self.at(oB + 32768 + i * 1024, [128, 512], BF16) for i in range(4)]
        rl = [self.at(oB + 36864 + i * 2048, [128, 512], F32) for i in range(2)]
        rds = rl
        grow = self.at(oD + 71680, [128, 1024], F32)
        _selfsync(t)
        ri = 0
        TS = [slice(0, 512), slice(512, 1024)]
        for half in range(2):
            h0 = half * 1024
            t.label = "F0"
            f0_rows = lambda s: self.xh[HALO + h0 + s * 128: HALO + h0 + (s + 1) * 128, :]
            f0_keys = lambda s, hf: [("xTl", s // 4, hf), ("xT", s // 4)]
            self.load_transpose(f0_rows, xs1, 4, xT, f0_keys, "xb", pre=(pre_x0 if half == 0 else pre_x1))

            def f0_second():
                t.label = "F0"
                self.load_transpose(f0_rows, xs1, 4, xT, f0_keys, "xb", s_off=4)
                t.label = "F1"

            def proj_res(src, skeys, nk=8, tile_outer=True, mid_hook=None):
                per_blk = WBLK // (nk * 128)
                if tile_outer:
                    blks = self.w_take(8 // per_blk)
                    order = [(j, tt) for tt in range(2) for j in range(8)]
                else:
                    blks = None
                    order = [(j, tt) for j in range(8) for tt in range(2)]
                cur = None
                for (j, tt) in order:
                    if mid_hook is not None and tt == 1 and j == 0:
                        mid_hook()
                    if tile_outer:
                        wt_, wk_ = blks[j // per_blk]
                    else:
                        if j % per_blk == 0 and tt == 0:
                            cur = self.w_next()
                        wt_, wk_ = cur
                    o = (j % per_blk) * nk * 128
                    cs = TS[tt]
                    b = self.bank()
                    self.mm_group(b, 512, [(wt_[:, o + k * 128:o + (k + 1) * 128], src(k, cs)) for k in range(nk)],
                                  [wk_, skeys(tt)])
                    t.op("dve", lambda j=j, cs=cs, b=b: nc.vector.tensor_tensor(
                        out=xT[:, j, cs], in0=self.ps[b][:, :], in1=xT[:, j, cs], op=ALU.add),
                        r=[("ps", b), ("xT", tt), ("xTl", tt, 0), ("xTl", tt, 1)], w=[("xT", tt)])

            t.label = "F1"
            proj_res(lambda k, cs: mergedT[:, k, h0 + cs.start:h0 + cs.stop], lambda tt: ("mg", 0), mid_hook=f0_second)
            if self.stop_after == "F1":
                return self.finish_dbg([(xT, 8 * 1024, F32)])
            t.label = "F2"
            nb_ = {}
            for tt in range(2):
                cs = TS[tt]
                nb_[tt] = self.norm_split(xT[:, :, cs], lambda k, cs=cs: xT[:, k, cs], [("xT", tt)], C_GCROSS,
                                          lambda k, cs=cs: bufA[:, k, cs], [("bufA", tt)], 512, sq, lnv,
                                          rstdn[tt], ("rstdn", tt), "f2")
            nb_[0][0]()
            t.label = "F3"
            blks = self.w_take(2)
            for tt in range(2):
                cs = TS[tt]
                for j in range(8):
                    wt, wkey = blks[j // 4]
                    o = (j % 4) * 1024
                    b = self.bank()
                    self.mm_group(b, 512, [(wt[:, o + k * 128:o + (k + 1) * 128], bufA[:, k, cs]) for k in range(8)],
                                  [wkey, ("bufA", tt)])
                    if j == 0:
                        pend_ev = []
                    pend_ev.append(lambda j=j, cs=cs, b=b, tt=tt: t.op(
                        "dve", lambda: nc.vector.tensor_tensor(
                            out=bufB[:, j, cs], in0=self.ps[b][:, :], in1=rstdn[tt][:, :], op=ALU.mult),
                        r=[("ps", b), ("rstdn", tt)], w=[("bufB", tt)]))
                    if j == 2:
                        nb_[tt][1]()
                        if tt == 0:
                            nb_[1][0]()
                    if j >= 2:
                        while pend_ev:
                            pend_ev.pop(0)()
            t.label = "F4"
            items = [(tt, hc) for tt in range(2) for hc in range(4)]
            pend = {}
            pi = 0
            for it in range(len(items) + 1):
                if it < len(items):
                    tt, hc = items[it]
                    cs = TS[tt]
                    pp = []
                    for mc in range(2):
                        b = self.bank()
                        self.mm_group(b, 512, [(ckT[:, 2 * hc + e, mc * 128:(mc + 1) * 128], bufB[:, 2 * hc + e, cs])
                                               for e in range(2)], ["ckT", ("bufB", tt)])
                        p_ = ptc[pi % 4]
                        pk = ("ptc", pi % 4)
                        pi += 1
                        t.op("act", lambda p_=p_, b=b: nc.scalar.activation(out=p_[:, :], in_=self.ps[b][:, :],
                                                                            func=AF.Exp, scale=1.0 / 16.0),
                             r=[("ps", b)], w=[pk])
                        pp.append((p_, pk))
                    pend[it] = (tt, hc, pp)
                if it >= 1:
                    tt, hc, pp = pend.pop(it - 1)
                    cs = TS[tt]
                    bd = self.bank()
                    self.mm_group(bd, 512, [(self.ones_bf[:, :], pp[mc][0][:, :]) for mc in range(2)],
                                  [pp[0][1], pp[1][1], "ones"])
                    rdk = ("rd", it % 2)
                    rd_ = rds[it % 2]
                    t.op("act", lambda bd=bd, rd_=rd_: nc.scalar.activation(out=rd_[:, :], in_=self.ps[bd][:, :],
                                                                            func=AF.Ln), r=[("ps", bd)], w=[rdk])
                    t.op("act", lambda rd_=rd_: nc.scalar.activation(out=rd_[:, :], in_=rd_[:, :], func=AF.Exp,
                                                                     scale=-1.0), r=[rdk], w=[rdk])
                    for e in range(2):
                        bo = self.bank()
                        ec = 2 * hc + e
                        self.mm_group(bo, 512, [(cV[:, mc, ec * 128:(ec + 1) * 128], pp[mc][0][:, :])
                                                for mc in range(2)], ["cV", pp[0][1], pp[1][1]])
                        t.op("dve", lambda ec=ec, cs=cs, bo=bo, rd_=rd_: nc.vector.tensor_tensor(
                            out=bufA[:, ec, cs], in0=self.ps[bo][:, :], in1=rd_[:, :], op=ALU.mult),
                            r=[("ps", bo), rdk], w=[("bufA", tt)])
            t.label = "F5"
            proj_res(lambda k, cs: bufA[:, k, cs], lambda tt: ("bufA", tt))
            if self.stop_after == "F5":
                return self.finish_dbg([(xT, 8 * 1024, F32)])
            t.label = "G1"
            nb_ = {}
            for tt in range(2):
                cs = TS[tt]
                nb_[tt] = self.norm_split(xT[:, :, cs], lambda k, cs=cs: xT[:, k, cs], [("xT", tt)], C_GMLP,
                                          lambda k, cs=cs: bufB[:, k, cs], [("bufB", tt)], 512, sq, lnv,
                                          rstdn[tt], ("rstdn", tt), "g1")
            nb_[0][0]()
            r_i = 0
            for fh in range(2):
                t.label = "G2"
                for fb in range(4):
                    wt, wkey = self.w_next()
                    for tt in range(2):
                        cs = TS[tt]
                        for fc in range(4):
                            f = fb * 4 + fc
                            b = self.bank()
                            self.mm_group(b, 512, [(wt[:, fc * 1024 + k * 128: fc * 1024 + (k + 1) * 128],
                                                    bufB[:, k, cs]) for k in range(8)], [wkey, ("bufB", tt)])
                            first_ = (fh == 0 and fb == 0)
                            if fc == 0:
                                pend_ev = []

                            def _ev(b=b, tt=tt, f=f, cs=cs):
                                nonlocal r_i
                                r_ = rl[r_i % 2]
                                rk_ = ("rl", r_i % 2)
                                r_i += 1
                                t.op("dve", lambda: nc.vector.scalar_tensor_tensor(
                                    out=r_[:, :], in0=self.ps[b][:, :], scalar=0.0, in1=rstdn[tt][:, :],
                                    op0=ALU.max, op1=ALU.mult),
                                    r=[("ps", b), ("rstdn", tt)], w=[rk_])
                                t.op("act", lambda: nc.scalar.activation(
                                    out=hT[:, f, cs], in_=r_[:, :], func=AF.Square), r=[rk_], w=[("hT", tt)])
                            pend_ev.append(_ev)
                            if first_ and fc == 2:
                                nb_[tt][1]()
                                if tt == 0:
                                    nb_[1][0]()
                            if (not first_) or fc >= 2:
                                while pend_ev:
                                    pend_ev.pop(0)()
                t.label = "G3"
                proj_res(lambda k, cs: hT[:, k, cs], lambda tt: ("hT", tt), nk=16, tile_outer=False)
            if self.stop_after == "G3":
                return self.finish_dbg([(xT, 8 * 1024, F32)])
            if half == 0:
                t.dma("sp", lambda: nc.sync.dma_start(out=grow[:, :], in_=self.growd), stream="c3", w=["grow"])
                pre_x1 = [self.lt_issue(self.xh[HALO + 1024 + s_ * 128: HALO + 1024 + (s_ + 1) * 128, :], xs1, "xb")
                          for s_ in range(4)]
            t.label = "H"
            ident = self.consts[:, C_ID:C_ID + 128]
            sqh = [self.at(oU + 32768, [128, 8, 512], BF16), self.at(oU + 49152, [128, 8, 512], BF16)]
            ogs = [self.at(oU + 32768 + 8192 + i * 4096, [128, 1024], F32) for i in range(2)] + \
                  [self.at(oU + 49152 + 8192 + i * 4096, [128, 1024], F32) for i in range(2)]
            ogkeys = [[("og", i), ("bufA" if i < 2 else "bufB", 0), ("bufA" if i < 2 else "bufB", 1)]
                      for i in range(4)]
            self._og_i = 0
            for tt in range(2):
                t.op("act", lambda tt=tt: nc.scalar.activation(out=sqh[tt][:, :, :], in_=xT[:, :, TS[tt]],
                                                               func=AF.Square),
                     r=[("xT", tt)], w=[("sqh", tt)])
            for tt in range(2):
                cs = TS[tt]
                rs_ = rstdn[tt]
                rk = ("rstdn", tt)
                sqx = sqh[tt]

                def h_stats():
                    bs = self.bank()
                    self.mm_group(bs, 512, [(self.ones_bf[:, :], sqx[:, k, :]) for k in range(8)], [("sqh", tt)])
                    t.op("act", lambda: nc.scalar.activation(out=lnv[:, :], in_=self.ps[bs][:, :], func=AF.Ln,
                                                             bias=self.cc(C_EPS), scale=1.0 / D),
                         r=[("ps", bs)], w=[("lnv", "h")])
                    t.op("act", lambda: nc.scalar.activation(out=rs_[:, :], in_=lnv[:, :], func=AF.Exp, scale=-0.5),
                         r=[("lnv", "h")], w=[rk])

                def x_tr(s, tt=tt, cs=cs):
                    bl = []
                    for hf in range(2):
                        b = self.bank()
                        for kk in range(4):
                            k = hf * 4 + kk
                            c0 = cs.start + s * 128
                            t.op("pe", lambda k=k, kk=kk, b=b, c0=c0: nc.tensor.transpose(
                                self.ps[b][:, kk * 128:(kk + 1) * 128], xT[:, k, c0:c0 + 128], ident),
                                r=[("xT", tt)], w=[("ps", b)])
                        bl.append(b)
                    return bl

                def x_ev(s, bl, tt=tt):
                    osl = self._og_i % 4
                    self._og_i += 1
                    og = ogs[osl]
                    okeys = ogkeys[osl]
                    for hf in range(2):
                        b = bl[hf]
                        t.op("dve", lambda b=b, hf=hf: nc.vector.scalar_tensor_tensor(
                            out=og[:, hf * 512:(hf + 1) * 512], in0=self.ps[b][:, :],
                            scalar=self.rcols[:, tt * 4 + s:tt * 4 + s + 1],
                            in1=grow[:, hf * 512:(hf + 1) * 512], op0=ALU.mult, op1=ALU.mult),
                            r=[("ps", b), ("rcols", tt), "grow"], w=okeys)
                    r0 = h0 + tt * 512 + s * 128
                    t.dma("sp", lambda: nc.sync.dma_start(out=self.outd[r0:r0 + 128, :], in_=og[:, :]),
                          stream=f"o{osl}", r=okeys, w=[("outd", r0)])

                pend = [(s, x_tr(s)) for s in range(2)]
                h_stats()
                br = self.bank()
                for s in range(4):
                    t.op("pe", lambda s=s: nc.tensor.transpose(self.ps[br][:, s * 128:(s + 1) * 128],
                                                               rs_[:, s * 128:(s + 1) * 128], ident),
                         r=[rk], w=[("ps", br)])
                t.op("act", lambda: nc.scalar.copy(out=self.rcols[:, tt * 4:tt * 4 + 4],
                                                   in_=self.ps[br][:, 0:512:128]),
                     r=[("ps", br)], w=[("rcols", tt)])
                for s in range(2, 4):
                    s_, bl_ = pend.pop(0)
                    x_ev(s_, bl_)
                    pend.append((s, x_tr(s)))
                while pend:
                    s_, bl_ = pend.pop(0)
                    x_ev(s_, bl_)
        t.barrier(waiters=("sp",), skip_prefix="dma_w")
        return nc

    def finish_dbg(self, items):
        nc, t = self.nc, self.t
        t.barrier()
        stage = self.at(self.oD + 73728, [128, 512], F32)
        off = 0
        for (tens, n, dt) in items:
            for c0 in range(0, n, 512):
                m = min(512, n - c0)
                src = self._flat(tens, c0, m)
                t.op("act", lambda: nc.scalar.copy(out=stage[:, 0:m], in_=src), w=["stg"])
                t.dma("sp", lambda: nc.sync.dma_start(out=self.dbgd[:, off + c0: off + c0 + m], in_=stage[:, 0:m]),
                      stream="dbg", r=["stg"], w=[("dbgo", off + c0)])
                t.barrier()
            off += n
        t.barrier(waiters=("sp",))
        return nc

    def _flat(self, tens, c0, m):
        shp = list(tens.shape)
        if len(shp) == 2:
            return tens[:, c0:c0 + m]
        inner = shp[2]
        if inner >= m:
            assert inner % m == 0
            return tens[:, c0 // inner, (c0 % inner):(c0 % inner) + m]
        assert c0 % inner == 0 and m % inner == 0
        return tens[:, c0 // inner:(c0 + m) // inner, :]


def _chunk(W, c0, ncols=128, cols=None):
    if cols is None:
        cols = np.arange(c0, c0 + ncols)
    sub = W[:, cols]
    K = sub.shape[0]
    return sub.reshape(K // 128, 128, len(cols)).transpose(1, 0, 2).reshape(128, -1)


def _pack_weights(inp):
    w_in = inp["w_in"][0]
    blocks = np.zeros((NWB, 128, WBLK), np.float32)

    def put(bi, off, arr):
        blocks[bi, :, off:off + arr.shape[1]] = arr

    bi = WB_A
    for h4 in range(4):
        for g in range(3):
            head = g * 4 + h4
            put(bi, 0, _chunk(w_in, 0, cols=head * 128 + PERM))
            put(bi, 1024, _chunk(w_in, 0, cols=1536 + head * 128 + PERM))
            put(bi, 2048, _chunk(w_in, 3072 + head * 128))
            bi += 1
    for i in range(3):
        for jj in range(2):
            j = i * 2 + jj
            put(WB_C + i, jj * 2048, _chunk(w_in, 4608 + j * 128))
            put(WB_C + i, jj * 2048 + 1024, _chunk(w_in, 4608 + 768 + j * 128))
    wap = inp["w_attn_proj"][0]
    wcp = inp["w_conv_proj"][0]
    for j in range(8):
        put(WB_E + j, 0, _chunk(w_in, 6144 + j * 128))
        put(WB_E + j, 1024, _chunk(wap, j * 128))
        put(WB_E + j, 1536, _chunk(w_in, 7168 + j * 128))
        put(WB_E + j, 2560, _chunk(wcp, j * 128))
    wckv = inp["w_ckv"][0]
    for j in range(8):
        put(WB_M + j // 4, (j % 4) * 1024, _chunk(wckv, j * 128))
    for i in range(2):
        put(WB_M + 2 + i, 0, _chunk(wckv, 1024 + i * 512, ncols=512))
    for (wb, name) in ((WB_OUT, "w_out"), (WB_CQ, "w_cq"), (WB_CO, "w_co")):
        W = inp[name][0]
        for j in range(8):
            put(wb + j // 4, (j % 4) * 1024, _chunk(W, j * 128))
    wup = inp["w_up"][0]
    wdn = inp["w_down"][0]
    bi = WB_UP
    for fh in range(2):
        for fb in range(4):
            for fc in range(4):
                f = fh * 16 + fb * 4 + fc
                put(bi, fc * 1024, _chunk(wup, f * 128))
            bi += 1
        for jb in range(4):
            for jc in range(2):
                j = jb * 2 + jc
                put(bi, jc * 2048, _chunk(wdn[fh * 2048:(fh + 1) * 2048], j * 128))
            bi += 1
    assert bi == NWB
    return blocks


def _vec8(v):
    return v.reshape(-1, 128).T


def _consts(inp, flag):
    c = np.zeros((128, CW), np.float32)
    c[:, C_ID:C_ID + 128] = np.eye(128, dtype=np.float32)
    kk = np.arange(128)[:, None]
    qq = np.arange(128)[None, :]
    NEG = np.float32(-30000.0)
    prev = np.where(kk >= qq, np.float32(0.0), NEG).astype(np.float32)
    cur = np.where(kk <= qq, np.float32(0.0), NEG).astype(np.float32)
    c[:, C_MASK:C_MASK + 128] = prev
    c[:, C_MASK + 128:C_MASK + 256] = cur
    c[:, C_MASK0:C_MASK0 + 128] = prev if flag else NEG
    c[:, C_MASK0 + 128:C_MASK0 + 256] = cur
    c[:, C_EPS] = EPS
    c[:, C_GMIX:C_GMIX + 8] = _vec8(inp["g_mix"][0])
    c[:, C_GCROSS:C_GCROSS + 8] = _vec8(inp["g_cross"][0])
    c[:, C_GMEM:C_GMEM + 8] = _vec8(inp["g_mem"][0])
    c[:, C_GMLP:C_GMLP + 8] = _vec8(inp["g_mlp"][0])
    c[:, C_GFIN:C_GFIN + 8] = _vec8(inp["g_final"])
    c[:, C_BGA:C_BGA + 8] = _vec8(inp["b_gate"][0][:1024])
    c[:, C_BGB:C_BGB + 8] = _vec8(inp["b_gate"][0][1024:])
    c[:, C_CONVB:C_CONVB + 6] = _vec8(inp["conv_b"][0])
    c[:, C_LNG:C_LNG + 6] = _vec8(inp["conv_ln_g"][0])
    c[:, C_LNB:C_LNB + 6] = _vec8(inp["conv_ln_b"][0])
    cw = inp["conv_w"][0]
    for j in range(6):
        c[:, C_CONVW + j * 31:C_CONVW + (j + 1) * 31] = cw[:, j * 128:(j + 1) * 128].T
    return c


def _rope_tables(pos0):
    pos = (pos0 + np.arange(4096)).astype(np.float32)
    inv = (np.float32(500000.0) ** (-np.arange(0, 32, 2, dtype=np.float32) / np.float32(32))).astype(np.float32)
    ang = (pos[None, :] * inv[:, None]).astype(np.float32)
    cs, sn = np.cos(ang).astype(np.float32), np.sin(ang).astype(np.float32)
    C = np.ones((64, 4096), np.float32)
    S = np.zeros((64, 4096), np.float32)
    C[0:16] = cs
    C[32:48] = cs
    S[0:16] = sn
    S[32:48] = -sn
    return C, S


_CACHE = {}


def _get_nc(stop_after=None, dbg=False):
    key = (stop_after, dbg)
    if key not in _CACHE:
        _CACHE[key] = Kern(stop_after, dbg).build()
    return _CACHE[key]


def _in_maps(inputs):
    inp = {k: np.asarray(v, dtype=np.float32) for k, v in inputs.items()}
    wp = _pack_weights(inp)
    x, mem = inp["x"], inp["mem"]
    maps = []
    for c in range(NCORES):
        b, q = c // 4, c % 4
        main = x[b, q * T:(q + 1) * T]
        halo = x[b, (q - 1) * T:q * T] if q > 0 else np.zeros((HALO, D), np.float32)
        C, S = _rope_tables(q * T - HALO)
        maps.append({
            "xh": np.ascontiguousarray(np.concatenate([halo, main], axis=0)),
            "memb": np.ascontiguousarray(mem[b]),
            "consts": _consts(inp, q > 0),
            "ropeC": C, "ropeS": S,
            "wpack": wp,
            "grow": np.ascontiguousarray(np.broadcast_to(inp["g_final"][None, :], (128, D))),
        })
    return maps


def kernel(**inputs):
    nc = _get_nc()
    maps = _in_maps(inputs)
    res = run_bass_kernel_spmd(nc, maps, core_ids=list(range(NCORES)))
    out = np.zeros((2, 4 * T, D), np.float32)
    for c in range(NCORES):
        out[c // 4, (c % 4) * T:(c % 4 + 1) * T] = res.results[c]["out"]
    return out
```
